# Optimizing a Trainium2 kernel written in Bass

```python
import math
import jax, jax.numpy as jnp
from jax import lax
import numpy as np

D_MODEL = 1024
BATCH = 2
SEQ = 16384
DEPTH = 1
DEC_BATCH = 8
DEC_SEQ = 2048
PAST_LEN = 128

N_MEM = 256
NORM_EPS = 1e-6
MASK_VALUE = -1e30
GDN_HEADS = 8
GDN_DK = 128
GDN_DV = 128
GDN_CONV = 5
GDN_CHUNK = 64
GDN_QK_W = GDN_HEADS * GDN_DK
GDN_V_W = GDN_HEADS * GDN_DV
DIL_GROUPS = ((128, 1), (512, 4), (2048, 16))
DIL_HEADS = 4
DIL_HEAD_DIM = 64
DIL_W = DIL_HEADS * DIL_HEAD_DIM
N_ATT_HEADS = len(DIL_GROUPS) * DIL_HEADS
REL_BUCKETS = 32
REL_MAX_DIST = 1024
MEM_HEADS = 4
MEM_HEAD_DIM = D_MODEL // MEM_HEADS
_FF_RAW = -(-8 * D_MODEL // 3)
D_FF = -(-_FF_RAW // 256) * 256
GDN_QKV_W = 2 * GDN_QK_W + GDN_V_W
DIL_QKV_W = 3 * len(DIL_GROUPS) * DIL_W
IN_SPLITS = [GDN_QKV_W, GDN_QKV_W + GDN_V_W, GDN_QKV_W + GDN_V_W + 2 * GDN_HEADS,
             GDN_QKV_W + GDN_V_W + 4 * GDN_HEADS]
IN_COLS = GDN_QKV_W + GDN_V_W + 4 * GDN_HEADS + DIL_QKV_W

kernel_name = "hybrid_gdn_dilated_encoder"


def rmsnorm(x, gain):
    xf = x.astype(jnp.float32)
    y = xf * lax.rsqrt(jnp.mean(xf * xf, axis=-1, keepdims=True) + NORM_EPS)
    return (y * gain.astype(jnp.float32)).astype(x.dtype)


def l2norm(x):
    xf = x.astype(jnp.float32)
    return xf * lax.rsqrt(jnp.sum(xf * xf, axis=-1, keepdims=True) + NORM_EPS)


def t5_bucket(rel):
    half = REL_BUCKETS // 2
    exact = half // 2
    n = np.abs(rel)
    large = exact + (np.log(np.maximum(n, 1) / exact) / math.log(REL_MAX_DIST / exact)
                     * (half - exact)).astype(np.int32)
    large = np.minimum(large, half - 1)
    return (rel > 0).astype(np.int32) * half + np.where(n < exact, n, large).astype(np.int32)


def centred_depthwise_conv(x, w):
    K, C = w.shape
    return lax.conv_general_dilated(x, w[:, None, :].astype(x.dtype), window_strides=(1,),
                                    padding=[((K - 1) // 2, K // 2)],
                                    dimension_numbers=('NWC', 'WIO', 'NWC'),
                                    feature_group_count=C)


def gated_delta_chunked(q, k, v, g, beta):
    B, T, H, dk = q.shape
    dv = v.shape[-1]
    C = GDN_CHUNK
    n = T // C

    def chunks(a):
        return jnp.moveaxis(a.reshape((B, n, C, H) + a.shape[3:]), 3, 2)

    q, k, v, g, beta = chunks(q), chunks(k), chunks(v), chunks(g), chunks(beta)
    gc = jnp.cumsum(g, axis=-1)
    incl = jnp.tril(jnp.ones((C, C), dtype=bool))
    strict = jnp.tril(jnp.ones((C, C), dtype=bool), -1)
    decay = jnp.exp(jnp.where(incl, gc[..., :, None] - gc[..., None, :], -jnp.inf))
    kb = k * beta[..., None]
    vb = v * beta[..., None]
    lower = jnp.where(strict, jnp.einsum('bnhid,bnhjd->bnhij', kb, k) * decay, 0.0)
    eye = jnp.eye(C, dtype=q.dtype)
    tinv = lax.linalg.triangular_solve(eye + lower, jnp.broadcast_to(eye, lower.shape),
                                       left_side=True, lower=True, unit_diagonal=True)
    u = tinv @ vb
    w = tinv @ (kb * jnp.exp(gc)[..., None])
    a_intra = jnp.where(incl, jnp.einsum('bnhid,bnhjd->bnhij', q, k) * decay, 0.0)
    q_dec = q * jnp.exp(gc)[..., None]
    k_dec = k * jnp.exp(gc[..., -1:] - gc)[..., None]
    g_last = jnp.exp(gc[..., -1])

    def step(S, xs):
        u_i, w_i, qd_i, kd_i, a_i, gl_i = xs
        v_new = u_i - jnp.einsum('bhck,bhkv->bhcv', w_i, S)
        o = jnp.einsum('bhck,bhkv->bhcv', qd_i, S) + jnp.einsum('bhij,bhjv->bhiv', a_i, v_new)
        S = S * gl_i[..., None, None] + jnp.einsum('bhck,bhcv->bhkv', kd_i, v_new)
        return S, o

    xs = tuple(jnp.moveaxis(a, 1, 0) for a in (u, w, q_dec, k_dec, a_intra, g_last))
    S0 = jnp.zeros((B, H, dk, dv), jnp.float32)
    _, o = lax.scan(step, S0, xs)
    return jnp.transpose(o, (1, 0, 3, 2, 4)).reshape(B, T, H, dv)


def gdn_mixer(qkv, z, a, b, conv_w, a_log, dt_bias, norm_w):
    B, T, _ = qkv.shape
    qkv = jax.nn.silu(centred_depthwise_conv(qkv, conv_w))
    q, k, v = jnp.split(qkv, [GDN_QK_W, 2 * GDN_QK_W], axis=-1)
    q = l2norm(q.reshape(B, T, GDN_HEADS, GDN_DK)) * (GDN_DK ** -0.5)
    k = l2norm(k.reshape(B, T, GDN_HEADS, GDN_DK))
    v = v.reshape(B, T, GDN_HEADS, GDN_DV).astype(jnp.float32)
    a = a.astype(jnp.float32).reshape(B, T, 2, GDN_HEADS)
    b = b.astype(jnp.float32).reshape(B, T, 2, GDN_HEADS)
    g = -jnp.exp(a_log.astype(jnp.float32)) * jax.nn.softplus(a + dt_bias.astype(jnp.float32))
    beta = jax.nn.sigmoid(b)
    o_f = gated_delta_chunked(q, k, v, g[:, :, 0], beta[:, :, 0])
    flip = lambda t: jnp.flip(t, axis=1)
    o_b = flip(gated_delta_chunked(flip(q), flip(k), flip(v), flip(g[:, :, 1]), flip(beta[:, :, 1])))
    o = o_f + o_b
    o = o * lax.rsqrt(jnp.mean(o * o, axis=-1, keepdims=True) + NORM_EPS) * norm_w.astype(jnp.float32)
    o = o * jax.nn.silu(z.astype(jnp.float32).reshape(B, T, GDN_HEADS, GDN_DV))
    return o.reshape(B, T, GDN_V_W).astype(qkv.dtype)


def dilated_group_attention(q, k, v, dil, side, rel_table):
    B, T, H, dh = q.shape
    W = side
    L = T // dil
    nb = -(-L // W)
    Lp = nb * W
    N = B * dil

    def strided(a):
        return a.reshape(B, L, dil, H, dh).transpose(0, 2, 1, 3, 4).reshape(N, L, H, dh)

    qs = jnp.pad(strided(q), ((0, 0), (0, Lp - L), (0, 0), (0, 0))).reshape(N, nb, W, H, dh)

    def kblocks(a):
        a = jnp.pad(strided(a), ((0, 0), (W, Lp - L + W), (0, 0), (0, 0))).reshape(N, nb + 2, W, H, dh)
        return jnp.concatenate([a[:, :-2], a[:, 1:-1], a[:, 2:]], axis=2)

    kb, vb = kblocks(k), kblocks(v)
    rel = np.arange(3 * W)[None, :] - W - np.arange(W)[:, None]
    band = np.abs(rel) <= W
    kpos = np.arange(nb)[:, None] * W + np.arange(3 * W)[None, :] - W
    mask = band[None] & ((kpos >= 0) & (kpos < L))[:, None, :]
    bias = jnp.transpose(rel_table[t5_bucket(rel * dil)], (2, 0, 1)).astype(jnp.float32)
    s = jnp.einsum('nbqhd,nbkhd->nbhqk', qs.astype(jnp.float32), kb.astype(jnp.float32)) * (dh ** -0.5) + bias
    s = jnp.where(mask[None, :, None], s, MASK_VALUE)
    m = jnp.max(s, axis=-1, keepdims=True)
    e = jnp.exp(s - m)
    den = jnp.sum(e, axis=-1)
    o = jnp.einsum('nbhqk,nbkhd->nbqhd', e, vb.astype(jnp.float32)) / jnp.moveaxis(den, 3, 2)[..., None]
    lse = jnp.moveaxis(m[..., 0] + jnp.log(den), 3, 2)
    o = o.reshape(N, Lp, H, dh)[:, :L].reshape(B, dil, L, H, dh).transpose(0, 2, 1, 3, 4).reshape(B, T, H, dh)
    lse = lse.reshape(N, Lp, H)[:, :L].reshape(B, dil, L, H).transpose(0, 2, 1, 3).reshape(B, T, H)
    return o, lse


def dilated_mixer(qkv_b, rel_table):
    B, T, _ = qkv_b.shape
    qkv_b = qkv_b.reshape(B, T, len(DIL_GROUPS), 3, DIL_HEADS, DIL_HEAD_DIM)
    outs, lses = [], []
    for gi, (window, dil) in enumerate(DIL_GROUPS):
        o, lse = dilated_group_attention(qkv_b[:, :, gi, 0], qkv_b[:, :, gi, 1], qkv_b[:, :, gi, 2],
                                         dil, window // (2 * dil),
                                         rel_table[:, gi * DIL_HEADS:(gi + 1) * DIL_HEADS])
        outs.append(o)
        lses.append(lse)
    wts = jax.nn.softmax(jnp.stack(lses), axis=0)
    o = jnp.sum(wts[..., None] * jnp.stack(outs), axis=0)
    return o.reshape(B, T, DIL_W).astype(qkv_b.dtype)


def memory_cross_attention(h, mem_n, w_cq, w_ckv, w_co):
    B, T, _ = h.shape
    M = mem_n.shape[1]
    q = (h @ w_cq).reshape(B, T, MEM_HEADS, MEM_HEAD_DIM)
    k, v = jnp.split(mem_n @ w_ckv, 2, axis=-1)
    k = k.reshape(B, M, MEM_HEADS, MEM_HEAD_DIM)
    v = v.reshape(B, M, MEM_HEADS, MEM_HEAD_DIM)
    s = jnp.einsum('bthd,bmhd->bhtm', q.astype(jnp.float32), k.astype(jnp.float32)) * (MEM_HEAD_DIM ** -0.5)
    p = jax.nn.softmax(s, axis=-1)
    o = jnp.einsum('bhtm,bmhd->bthd', p, v.astype(jnp.float32)).reshape(B, T, D_MODEL).astype(h.dtype)
    return o @ w_co


def trunk(x, mem, norm_mix, w_in, conv_w, gdn_a_log, gdn_dt_bias, gdn_norm, w_gate, w_pa, w_pb, w_o,
          rel_bias, norm_cross, norm_mem, w_cq, w_ckv, w_co, norm_ffn, w_ff1, w_ff3, w_ff2, norm_final):
    for l in range(DEPTH):
        h = rmsnorm(x, norm_mix[l])
        proj = h @ w_in[l]
        qkv_a, z, a, b, qkv_b = jnp.split(proj, IN_SPLITS, axis=-1)
        o_a = gdn_mixer(qkv_a, z, a, b, conv_w[l], gdn_a_log[l], gdn_dt_bias[l], gdn_norm[l])
        o_b = dilated_mixer(qkv_b, rel_bias)
        gate_a, gate_b = jnp.split(jax.nn.sigmoid(h @ w_gate[l]), 2, axis=-1)
        x = x + (gate_a * (o_a @ w_pa[l]) + gate_b * (o_b @ w_pb[l])) @ w_o[l]
        h = rmsnorm(x, norm_cross[l])
        x = x + memory_cross_attention(h, rmsnorm(mem, norm_mem[l]), w_cq[l], w_ckv[l], w_co[l])
        h = rmsnorm(x, norm_ffn[l])
        x = x + (jax.nn.silu(h @ w_ff1[l]) * (h @ w_ff3[l])) @ w_ff2[l]
    return rmsnorm(x, norm_final)


def setup_inputs(seed: int = 0) -> dict:
    key = jax.random.key(seed)
    ks = jax.random.split(key, 32)
    f32 = jnp.float32
    nrm = lambda k, shape, scale: jax.random.normal(k, shape, f32) * scale
    gain = lambda k, shape: 1.0 + 0.02 * jax.random.normal(k, shape, f32)
    dt = jnp.exp(jax.random.uniform(ks[8], (DEPTH, 2, GDN_HEADS), f32, math.log(1e-3), math.log(1e-1)))
    return {
        "x_prompt": nrm(ks[0], (BATCH, SEQ, D_MODEL), 1.0),
        "x_sample": nrm(ks[1], (DEC_BATCH, DEC_SEQ, D_MODEL), 1.0),
        "mem_prompt": nrm(ks[2], (BATCH, N_MEM, D_MODEL), 1.0),
        "mem_sample": nrm(ks[3], (DEC_BATCH, N_MEM, D_MODEL), 1.0),
        "norm_mix": gain(ks[4], (DEPTH, D_MODEL)),
        "w_in": nrm(ks[5], (DEPTH, D_MODEL, IN_COLS), D_MODEL ** -0.5),
        "conv_w": nrm(ks[6], (DEPTH, GDN_CONV, GDN_QKV_W), GDN_CONV ** -0.5),
        "gdn_a_log": jnp.log(jax.random.uniform(ks[7], (DEPTH, 2, GDN_HEADS), f32, 1.0, 16.0)),
        "gdn_dt_bias": dt + jnp.log(-jnp.expm1(-dt)),
        "gdn_norm": gain(ks[9], (DEPTH, GDN_DV)),
        "w_gate": nrm(ks[10], (DEPTH, D_MODEL, 2 * D_MODEL), D_MODEL ** -0.5),
        "w_pa": nrm(ks[11], (DEPTH, GDN_V_W, D_MODEL), GDN_V_W ** -0.5),
        "w_pb": nrm(ks[12], (DEPTH, DIL_W, D_MODEL), DIL_W ** -0.5),
        "w_o": nrm(ks[13], (DEPTH, D_MODEL, D_MODEL), D_MODEL ** -0.5),
        "rel_bias": nrm(ks[14], (REL_BUCKETS, N_ATT_HEADS), 0.1),
        "norm_cross": gain(ks[15], (DEPTH, D_MODEL)),
        "norm_mem": gain(ks[16], (DEPTH, D_MODEL)),
        "w_cq": nrm(ks[17], (DEPTH, D_MODEL, D_MODEL), D_MODEL ** -0.5),
        "w_ckv": nrm(ks[18], (DEPTH, D_MODEL, 2 * D_MODEL), D_MODEL ** -0.5),
        "w_co": nrm(ks[19], (DEPTH, D_MODEL, D_MODEL), D_MODEL ** -0.5),
        "norm_ffn": gain(ks[20], (DEPTH, D_MODEL)),
        "w_ff1": nrm(ks[21], (DEPTH, D_MODEL, D_FF), D_MODEL ** -0.5),
        "w_ff3": nrm(ks[22], (DEPTH, D_MODEL, D_FF), D_MODEL ** -0.5),
        "w_ff2": nrm(ks[23], (DEPTH, D_FF, D_MODEL), D_FF ** -0.5),
        "norm_final": gain(ks[24], (D_MODEL,)),
    }


def reference(x_prompt, x_sample, mem_prompt, mem_sample, norm_mix, w_in, conv_w, gdn_a_log, gdn_dt_bias,
              gdn_norm, w_gate, w_pa, w_pb, w_o, rel_bias, norm_cross, norm_mem, w_cq, w_ckv, w_co,
              norm_ffn, w_ff1, w_ff3, w_ff2, norm_final):
    y_prompt = trunk(x_prompt, mem_prompt, norm_mix, w_in, conv_w, gdn_a_log, gdn_dt_bias, gdn_norm,
                     w_gate, w_pa, w_pb, w_o, rel_bias, norm_cross, norm_mem, w_cq, w_ckv, w_co,
                     norm_ffn, w_ff1, w_ff3, w_ff2, norm_final)
    y_sample = trunk(x_sample, mem_sample, norm_mix, w_in, conv_w, gdn_a_log, gdn_dt_bias, gdn_norm,
                     w_gate, w_pa, w_pb, w_o, rel_bias, norm_cross, norm_mem, w_cq, w_ckv, w_co,
                     norm_ffn, w_ff1, w_ff3, w_ff2, norm_final)
    return (y_prompt, y_sample)
```

```python
import math
import numpy as np
import concourse.bass as bass
import concourse.mybir as mybir
from concourse.bass_utils import run_bass_kernel_spmd

F32 = mybir.dt.float32
BF16 = mybir.dt.bfloat16
U8 = mybir.dt.uint8
ALU = mybir.AluOpType
AF = mybir.ActivationFunctionType
AX = mybir.AxisListType

D = 1024
IN_COLS = 6432
D_FF = 2816
NMEM = 256
EPS = 1e-6


import types as _types


def _freeze(fn):
    if fn.__closure__ is None:
        return fn
    cells = []
    for c in fn.__closure__:
        try:
            cells.append(_types.CellType(c.cell_contents))
        except ValueError:
            cells.append(c)
    return _types.FunctionType(fn.__code__, fn.__globals__, fn.__name__, fn.__defaults__, tuple(cells))


class Counter:
    def __init__(self, K, name, step, limit=24000):
        self.K, self.name, self.step, self.limit = K, name, step, limit
        self.sems = []
        self.val = 0
        self._new()

    def _new(self):
        if self.sems:
            self.K.closed.append((self.sems[-1], self.val))
        self.sems.append(self.K.nc.alloc_semaphore(name=f"{self.name}_{len(self.sems)}"))
        self.val = 0

    def next_event(self):
        if self.val + self.step > self.limit:
            self._new()
        self.val += self.step
        return (self.sems[-1], self.val)

    def last(self):
        return (self.sems[-1], self.val)


class Buf:
    def __init__(self, K, t, name):
        self.K, self.t, self.name = K, t, name
        self.w = {}
        self.r = {}
        self.pr = {}
        self.excl = False
        self._dmac = None

    def __getitem__(self, key):
        return self.t[key]

    @property
    def dmac(self):
        if self._dmac is None:
            self._dmac = self.K.get_dmac()
        return self._dmac


class EngState:
    def __init__(self, K, name):
        self.name = name
        self.counter = Counter(K, "e_" + name, 1)
        self.seen = {}
        self.prog = []


class Kern:
    ENGS = ("tensor", "vector", "scalar", "gpsimd", "sync")

    def __init__(self, nc, arena_bytes=210000):
        self.nc = nc
        self.closed = []
        self.eng = {n: EngState(self, n) for n in self.ENGS}
        self.nbuf = 0
        self.n_inst = 0
        self.arena = nc.alloc_sbuf_tensor("arena", [128, arena_bytes], U8)
        self.arena_bytes = arena_bytes
        self.off = 0
        self.base = 0
        self.dmacs = []
        self.dmac_free = []
        self.phase_bufs = []

    def get_dmac(self):
        if self.dmac_free:
            return self.dmac_free.pop()
        c = Counter(self, f"d{len(self.dmacs)}", 16)
        self.dmacs.append(c)
        return c

    def sb(self, shape, dtype=F32, name=None):
        self.nbuf += 1
        name = name or f"sb{self.nbuf}"
        esz = {F32: 4, BF16: 2, U8: 1}[dtype]
        n = int(np.prod(shape[1:]))
        nbytes = n * esz
        off = (self.off + 31) // 32 * 32
        assert off + nbytes <= self.arena_bytes, f"SBUF arena overflow at {name}: {off + nbytes}"
        self.off = off + nbytes
        v = self.arena[0:shape[0], off:off + nbytes]
        if dtype != U8:
            v = v.bitcast(dtype)
        if len(shape) == 3:
            v = v.rearrange("p (a b) -> p a b", a=shape[1])
        elif len(shape) == 4:
            v = v.rearrange("p (a b c) -> p a b c", a=shape[1], b=shape[2])
        b = Buf(self, v, name)
        self.phase_bufs.append(b)
        return b

    def ps(self, shape, dtype=F32, name=None):
        self.nbuf += 1
        name = name or f"ps{self.nbuf}"
        b = Buf(self, self.nc.alloc_psum_tensor(name, list(shape), dtype), name)
        b.excl = True
        return b

    def dram(self, name, shape, dtype, kind="Internal"):
        return Buf(self, self.nc.dram_tensor(name, list(shape), dtype, kind=kind), name)

    def _deps(self, reads, writes, parts):
        deps = {}

        def merge(d):
            for s, v in d.items():
                if deps.get(s, 0) < v:
                    deps[s] = v

        for b in reads:
            merge(b.w)
            if b.excl:
                merge(b.r)
        for b in writes:
            merge(b.w)
            merge(b.r)
            merge(b.pr)
        for b in parts:
            merge(b.r)
            merge(b.pr)
        return deps

    def _emit_waits(self, E, deps):
        own = E.counter.sems
        for s, v in deps.items():
            if E.name == "tensor" and s in own:
                continue
            if E.seen.get(s, 0) < v:
                E.prog.append(("wait", s, v))
                E.seen[s] = v

    def _commit(self, ev, reads, writes, parts):
        s, v = ev
        for b in reads:
            if b.r.get(s, 0) < v:
                b.r[s] = v
        for b in writes:
            b.w = {s: v}
            b.r = {}
            b.pr = {}
        for b in parts:
            if b.r:
                b.pr = dict(b.w)
                for s2, v2 in b.r.items():
                    if b.pr.get(s2, 0) < v2:
                        b.pr[s2] = v2
                b.w = {}
                b.r = {}
            if b.w.get(s, 0) < v:
                b.w[s] = v

    def op(self, eng, fn, r=(), w=(), p=()):
        E = self.eng[eng]
        self._emit_waits(E, self._deps(r, w, p))
        ev = E.counter.next_event()
        E.prog.append(("inst", _freeze(fn), ev, 1))
        self._commit(ev, r, w, p)
        self.n_inst += 1

    def v(self, fn, **kw):
        self.op("vector", fn, **kw)

    def a(self, fn, **kw):
        self.op("scalar", fn, **kw)

    def g(self, fn, **kw):
        self.op("gpsimd", fn, **kw)

    def pe(self, fn, **kw):
        self.op("tensor", fn, **kw)

    def dma(self, q, out, in_, r=(), w=(), p=(), cbuf=None):
        E = self.eng[q]
        self._emit_waits(E, self._deps(r, w, p))
        ev = cbuf.dmac.next_event()
        E.prog.append(("inst", lambda e: e.dma_start(out=out, in_=in_), ev, 16))
        self._commit(ev, r, w, p)
        self.n_inst += 1

    def load(self, sbuf, sb_ap, dram, dr_ap, part=False):
        if part:
            self.dma("sync", sb_ap, dr_ap, r=[dram], p=[sbuf], cbuf=sbuf)
        else:
            self.dma("sync", sb_ap, dr_ap, r=[dram], w=[sbuf], cbuf=sbuf)

    def store(self, dram, dr_ap, sbuf, sb_ap):
        self.dma("gpsimd", dr_ap, sb_ap, r=[sbuf], p=[dram], cbuf=sbuf)

    def barrier(self, extra=()):
        deps = {}
        for E in self.eng.values():
            s, v = E.counter.last()
            if v > 0:
                deps[s] = v
        for c in self.dmacs:
            s, v = c.last()
            if v > 0:
                deps[s] = v
        for s, v in self.closed:
            deps[s] = v
        for E in self.eng.values():
            self._emit_waits(E, deps)

    def new_phase(self):
        self.barrier()
        for b in self.phase_bufs:
            if b._dmac is not None:
                self.dmac_free.append(b._dmac)
                b._dmac = None
        self.phase_bufs = []
        self.off = self.base

    def persist(self):
        self.base = self.off
        self.phase_bufs = []

    def build(self):
        nc = self.nc

        def run(E, e):
            for item in E.prog:
                if item[0] == "wait":
                    e.wait_ge(item[1], item[2])
                else:
                    _, fn, (s, v), step = item
                    fn(e).then_inc(s, step)

        with nc.Block() as block:
            @block.tensor
            def _(e):
                run(self.eng["tensor"], e)

            @block.vector
            def _(e):
                run(self.eng["vector"], e)

            @block.scalar
            def _(e):
                run(self.eng["scalar"], e)

            @block.gpsimd
            def _(e):
                run(self.eng["gpsimd"], e)

            @block.sync
            def _(e):
                run(self.eng["sync"], e)


NCONST = 15
(C_ID, C_TRIF, C_TRIB, C_BLK, C_CS0, C_CS1, C_AF, C_AB, C_MSF, C_MIF, C_MSB, C_MIB, C_ONES, C_X1, C_X2) = range(NCONST)


def make_consts():
    i = np.arange(128)
    a = i[:, None]
    b = i[None, :]
    same = (a // 64) == (b // 64)
    c = np.zeros((NCONST, 128, 128), np.float32)
    c[C_ID] = (a == b)
    c[C_TRIF] = (a <= b) & same
    c[C_TRIB] = (a >= b) & same
    c[C_BLK] = same
    c[C_CS0] = (a < 64) & (b >= 0)
    c[C_CS1] = (a >= 64) & (b >= 0)
    c[C_AF] = (a > b) & same
    c[C_AB] = (a < b) & same
    c[C_MSF] = (b > a) & same
    c[C_MIF] = (b >= a) & same
    c[C_MSB] = (b < a) & same
    c[C_MIB] = (b <= a) & same
    c[C_ONES] = 1.0
    return np.ascontiguousarray(c.transpose(1, 0, 2))


def t5_bucket(rel):
    half = 16
    exact = 8
    n = np.abs(rel)
    large = exact + (np.log(np.maximum(n, 1) / exact) / math.log(1024 / exact) * (half - exact)).astype(np.int32)
    large = np.minimum(large, half - 1)
    return (rel > 0).astype(np.int32) * half + np.where(n < exact, n, large).astype(np.int32)


DILS = (1, 4, 16)


def make_biasmask(rel_bias):
    qi = np.arange(128)[:, None]
    kj = np.arange(256)[None, :]
    rel = kj - 64 - qi
    band = np.abs(rel) <= 64
    out = np.empty((128, 3, 4, 4, 256), np.float32)
    for g, dil in enumerate(DILS):
        bk = t5_bucket(rel * dil)
        for h in range(4):
            vals = rel_bias[bk, g * 4 + h]
            for var in range(4):
                m = band.copy()
                if var & 1:
                    m = m & (kj >= 64)
                if var & 2:
                    m = m & (kj < 192)
                out[:, g, h, var, :] = np.where(m, vals, np.float32(-1e30))
    return out


WSPECS = [("w_in", D, IN_COLS), ("w_gate", D, 2 * D), ("w_pa", D, D), ("w_pb", 256, D), ("w_o", D, D),
          ("w_cq", D, D), ("w_ckv", D, 2 * D), ("w_co", D, D), ("w_ff1", D, D_FF), ("w_ff3", D, D_FF),
          ("w_ff2", D_FF, D)]


def finish(K, S, yout, seqs):
    K.barrier()
    K.build()
    return K.nc


def common_inputs(inp):
    m = {n: np.ascontiguousarray(inp[n][0]) for n, _, _ in WSPECS}
    norms = np.zeros((6, D), np.float32)
    norms[0] = inp["norm_mix"][0]
    norms[1] = inp["norm_cross"][0]
    norms[2] = inp["norm_mem"][0]
    norms[3] = inp["norm_ffn"][0]
    norms[4] = inp["norm_final"]
    norms[5, :] = np.tile(inp["gdn_norm"][0], 8)
    m["norms"] = norms
    m["conv_w"] = np.ascontiguousarray(inp["conv_w"][0].reshape(5, 24, 128).transpose(2, 1, 0))
    m["a_log"] = np.ascontiguousarray(inp["gdn_a_log"][0].reshape(16))
    m["dt_bias"] = np.ascontiguousarray(inp["gdn_dt_bias"][0].reshape(16))
    m["consts"] = make_consts()
    m["biasmask"] = make_biasmask(np.asarray(inp["rel_bias"]))
    return m


def build_program(seq_lens, debug=(), upto=99):
    nc = bass.Bass("TRN2", target_bir_lowering=False)
    K = Kern(nc)
    X = K.op

    def din(name, shape, dt=F32):
        return K.dram(name, shape, dt, kind="ExternalInput")

    def dscr(name, shape, dt):
        return K.dram(name, shape, dt, kind=("ExternalOutput" if name in debug else "Internal"))

    seqs = [(nm, L) for nm, L in seq_lens if L > 0]
    xin = {nm: din("x" + nm, [L, D]) for nm, L in seqs}
    memin = {nm: din("mem" + nm, [NMEM, D]) for nm, L in seqs}
    yout = {nm: K.dram("y" + nm, [L, D], F32, kind="ExternalOutput") for nm, L in seqs}
    wf32 = {n: din(n, [k, c]) for n, k, c in WSPECS}
    wbf = {n: dscr(n + "_bf", [k, c], BF16) for n, k, c in WSPECS}
    d_norms = din("norms", [6, D])
    d_convw = din("conv_w", [128, 24, 5])
    d_alog = din("a_log", [16])
    d_dtb = din("dt_bias", [16])
    d_consts = din("consts", [128, NCONST, 128])
    d_bm = din("biasmask", [128, 3, 4, 4, 256])

    S = {}
    for nm, L in seqs:
        S[nm] = dict(
            PT=dscr("PT" + nm, [3072, L + 4], BF16),
            SZ=dscr("SZ" + nm, [L, D], BF16),
            GG=dscr("GG" + nm, [L, 16], F32),
            BB=dscr("BB" + nm, [L, 16], F32),
            QB=dscr("QB" + nm, [L + 2048, 2304], BF16),
            QT=dscr("QT" + nm, [L // 128, 128, 8, 128], BF16),
            KT=dscr("KT" + nm, [L // 128, 128, 8, 128], BF16),
            KTOK=dscr("KTOK" + nm, [L, D], BF16),
            VTOK=dscr("VTOK" + nm, [L, D], BF16),
            OF=dscr("OF" + nm, [L, D], F32),
            OB=dscr("OB" + nm, [L, D], F32),
            U=[dscr(f"U{g}" + nm, [L, 256], F32) for g in range(3)],
            ST=[dscr(f"ST{g}" + nm, [L, 8], F32) for g in range(3)],
            X2=dscr("X2" + nm, [L, D], F32),
            KMT=dscr("KMT" + nm, [128, 8, 256], BF16),
            VM=dscr("VM" + nm, [128, 2, D], BF16),
        )

    cst = K.sb([128, NCONST, 128], F32, "cst")
    K.load(cst, cst[:, :, :], d_consts, d_consts.t[:, :, :])
    identb = K.sb([128, 128], BF16, "identb")
    onesb = K.sb([128, 128], BF16, "onesb")
    K.v(lambda e: e.tensor_copy(out=identb[:, :], in_=cst[:, C_ID, :]), r=[cst], w=[identb])
    K.v(lambda e: e.tensor_copy(out=onesb[:, :], in_=cst[:, C_ONES, :]), r=[cst], w=[onesb])
    def load_norm(idx):
        nt = K.sb([128, D], F32, f"norm{idx}")
        K.load(nt, nt[:, :], d_norms, d_norms.t[idx, :].partition_broadcast(128))
        return nt
    cstb = K.sb([128, NCONST, 128], BF16, "cstb")
    K.v(lambda e: e.tensor_copy(out=cstb[:, :, :], in_=cst[:, :, :]), r=[cst], w=[cstb])
    epsb = K.sb([128, 1], F32, "epsb")
    K.v(lambda e: e.memset(epsb[:, :], EPS), w=[epsb])
    oneb = K.sb([128, 1], F32, "oneb")
    K.v(lambda e: e.memset(oneb[:, :], 1.0), w=[oneb])
    K.persist()

    PSB = [K.ps([128, 512], F32, f"psb{i}") for i in range(6)]
    PST = [K.ps([128, 1024], BF16, f"pst{i}") for i in range(2)]
    psb_i = [0]
    pst_i = [0]

    def psb():
        psb_i[0] += 1
        return PSB[psb_i[0] % len(PSB)]

    def pst():
        pst_i[0] += 1
        return PST[pst_i[0] % len(PST)]

    cp_i = [0]

    def evac(out_ap, in_ap, rb, wb, part=False, eng=None):
        cp_i[0] += 1
        kw = dict(r=rb, p=wb) if part else dict(r=rb, w=wb)
        if eng is None:
            eng = "vector" if cp_i[0] % 2 else "scalar"
        if eng == "vector":
            K.v(lambda e: e.tensor_copy(out=out_ap, in_=in_ap), **kw)
        else:
            K.a(lambda e: e.copy(out=out_ap, in_=in_ap), **kw)

    K.new_phase()
    stg_f = [K.sb([128, 2048], F32, f"wsf{i}") for i in range(3)]
    stg_b = [K.sb([128, 2048], BF16, f"wsb{i}") for i in range(3)]
    ci = 0
    for n, kk, cc in WSPECS:
        for kc in range(kk // 128):
            for c0 in range(0, cc, 2048):
                cw = min(2048, cc - c0)
                sf, sbb = stg_f[ci % 3], stg_b[ci % 3]
                K.load(sf, sf[:, 0:cw], wf32[n], wf32[n].t[kc * 128:(kc + 1) * 128, c0:c0 + cw])
                eng = ("vector", "scalar", "gpsimd")[ci % 3]
                if eng == "scalar":
                    K.a(lambda e, sf=sf, sbb=sbb, cw=cw: e.copy(out=sbb[:, 0:cw], in_=sf[:, 0:cw]), r=[sf], w=[sbb])
                else:
                    X(eng, lambda e, sf=sf, sbb=sbb, cw=cw: e.tensor_copy(out=sbb[:, 0:cw], in_=sf[:, 0:cw]), r=[sf], w=[sbb])
                K.store(wbf[n], wbf[n].t[kc * 128:(kc + 1) * 128, c0:c0 + cw], sbb, sbb[:, 0:cw])
                ci += 1

    def load_w(name, rows, cols, c0=0, buf=None):
        kc = rows // 128
        wt = buf or K.sb([128, kc, cols], BF16, "W_" + name)
        for k in range(kc):
            K.load(wt, wt[:, k, :], wbf[name], wbf[name].t[k * 128:(k + 1) * 128, c0:c0 + cols], part=True)
        return wt

    def rmsnorm_T(xt, nt, ht, hT, sq, ss):
        K.a(lambda e: e.activation(out=sq[:, :], in_=xt[:, :], func=AF.Square, accum_out=ss[:, 0:1]), r=[xt], w=[sq, ss])
        K.v(lambda e: e.tensor_scalar(out=ss[:, 1:2], in0=ss[:, 0:1], scalar1=1.0 / D, scalar2=EPS, op0=ALU.mult, op1=ALU.add),
            r=[ss], w=[ss])
        K.a(lambda e: e.sqrt(out=ss[:, 1:2], in_=ss[:, 1:2]), r=[ss], w=[ss])
        K.v(lambda e: e.reciprocal(out=ss[:, 1:2], in_=ss[:, 1:2]), r=[ss], w=[ss])
        K.v(lambda e: e.scalar_tensor_tensor(out=ht[:, :], in0=xt[:, :], scalar=ss[:, 1:2], in1=nt[:, :],
                                             op0=ALU.mult, op1=ALU.mult), r=[xt, ss, nt], w=[ht])
        transpose_to(ht, hT, 8)

    def transpose_to(src, dstT, nchunks, src_off=0):
        for c0 in range(0, nchunks, 8):
            n = min(8, nchunks - c0)
            p = pst()
            for c in range(n):
                K.pe(lambda e, p=p, c=c, c0=c0: e.transpose(out=p[:, c * 128:(c + 1) * 128],
                                                          in_=src[:, src_off + (c0 + c) * 128: src_off + (c0 + c + 1) * 128],
                                                          identity=identb[:, :]), r=[src, identb], p=[p])
            evac(dstT[:, c0:c0 + n, :], p[:, 0:n * 128].rearrange("p (a b) -> p a b", a=n), [p], [dstT], part=True)

    K.new_phase()
    w_in = load_w("w_in", D, IN_COLS)
    n_mix = load_norm(0)
    dtb = K.sb([128, 16], F32, "dtb")
    K.load(dtb, dtb[:, :], d_dtb, d_dtb.t[:].partition_broadcast(128))
    nA = K.sb([128, 16], F32, "nA")
    K.load(nA, nA[:, :], d_alog, d_alog.t[:].partition_broadcast(128))
    K.a(lambda e: e.activation(out=nA[:, :], in_=nA[:, :], func=AF.Exp), r=[nA], w=[nA])
    K.v(lambda e: e.tensor_scalar(out=nA[:, :], in0=nA[:, :], scalar1=-1.0, scalar2=None, op0=ALU.mult), r=[nA], w=[nA])
    zt = K.sb([128, 2304], BF16, "zeros")
    K.v(lambda e: e.memset(zt[:, :], 0.0), w=[zt])
    x_t = [K.sb([128, D], F32, f"x{i}") for i in range(3)]
    h_t = [K.sb([128, D], BF16, f"h{i}") for i in range(2)]
    sq = K.sb([128, D], F32, "sq")
    sst = [K.sb([128, 2], F32, f"ss{i}") for i in range(2)]
    hTb = [K.sb([128, 8, 512], BF16, f"hT{i}") for i in range(2)]
    hT1 = [K.sb([128, 8, 128], BF16, f"hT1_{i}") for i in range(2)]
    pts = [K.sb([128, 6, 512], BF16, f"pts{i}") for i in range(2)]
    szs = [K.sb([128, D], BF16, f"szs{i}") for i in range(2)]
    qbs = [K.sb([128, 2304], BF16, f"qbs{i}") for i in range(2)]
    abt = [K.sb([128, 4, 32], F32, f"abt{i}") for i in range(2)]
    it = 0
    for nm, L in seqs:
        sc = S[nm]
        K.store(sc["PT"], sc["PT"].t[:, 0:2].rearrange("(c p) t -> p c t", p=128), zt, zt[:, 0:48].rearrange("p (c t) -> p c t", t=2))
        K.store(sc["PT"], sc["PT"].t[:, L + 2:L + 4].rearrange("(c p) t -> p c t", p=128), zt, zt[:, 0:48].rearrange("p (c t) -> p c t", t=2))
        for r0 in list(range(0, 1024, 128)) + list(range(L + 1024, L + 2048, 128)):
            K.store(sc["QB"], sc["QB"].t[r0:r0 + 128, :], zt, zt[:, :])
        for blk in range(L // 512):
            hTt = hTb[blk % 2]
            for tt in range(4):
                xt = x_t[it % 3]
                ht = h_t[it % 2]
                ss = sst[it % 2]
                h1 = hT1[it % 2]
                it += 1
                t0 = blk * 512 + tt * 128
                K.load(xt, xt[:, :], xin[nm], xin[nm].t[t0:t0 + 128, :])
                rmsnorm_T(xt, n_mix, ht, h1, sq, ss)
                K.g(lambda e, hTt=hTt, h1=h1, tt=tt: e.tensor_copy(out=hTt[:, :, tt * 128:(tt + 1) * 128], in_=h1[:, :, :]),
                    r=[h1], p=[hTt])
            for cg in range(4):
                pt_s = pts[cg % 2]
                for cc in range(6):
                    ct = cg * 6 + cc
                    p = psb()
                    for kc in range(8):
                        K.pe(lambda e, p=p, kc=kc, ct=ct, hTt=hTt: e.matmul(p[:, :], lhsT=w_in[:, kc, ct * 128:(ct + 1) * 128], rhs=hTt[:, kc, :],
                                                                       start=(kc == 0), stop=(kc == 7)), r=[w_in, hTt], p=[p])
                    evac(pt_s[:, cc, :], p[:, :], [p], [pt_s], part=True)
                K.store(sc["PT"], sc["PT"].t[cg * 768:(cg + 1) * 768, 2 + blk * 512: 2 + (blk + 1) * 512].rearrange("(c p) t -> p c t", p=128),
                        pt_s, pt_s[:, :, :])
            for tt in range(4):
                t0 = blk * 512 + tt * 128
                lhs = lambda kc, hTt=hTt, tt=tt: hTt[:, kc, tt * 128:(tt + 1) * 128]
                szt = szs[tt % 2]
                for n in range(2):
                    p = psb()
                    for kc in range(8):
                        K.pe(lambda e, p=p, kc=kc, n=n, lhs=lhs: e.matmul(p[:, :], lhsT=lhs(kc), rhs=w_in[:, kc, 3072 + n * 512:3072 + (n + 1) * 512],
                                                                     start=(kc == 0), stop=(kc == 7)), r=[w_in, hTt], p=[p])
                    K.a(lambda e, p=p, n=n, szt=szt: e.activation(out=szt[:, n * 512:(n + 1) * 512], in_=p[:, :], func=AF.Silu), r=[p], p=[szt])
                K.store(sc["SZ"], sc["SZ"].t[t0:t0 + 128, :], szt, szt[:, :])
                qbt = qbs[tt % 2]
                for n in range(5):
                    cw = 512 if n < 4 else 256
                    p = psb()
                    for kc in range(8):
                        K.pe(lambda e, p=p, kc=kc, n=n, cw=cw, lhs=lhs: e.matmul(p[:, 0:cw], lhsT=lhs(kc), rhs=w_in[:, kc, 4128 + n * 512:4128 + n * 512 + cw],
                                                                            start=(kc == 0), stop=(kc == 7)), r=[w_in, hTt], p=[p])
                    evac(qbt[:, n * 512:n * 512 + cw], p[:, 0:cw], [p], [qbt], part=True)
                K.store(sc["QB"], sc["QB"].t[1024 + t0:1024 + t0 + 128, :], qbt, qbt[:, :])
                ab = abt[tt % 2]
                p = psb()
                for kc in range(8):
                    K.pe(lambda e, p=p, kc=kc, lhs=lhs: e.matmul(p[:, 0:32], lhsT=lhs(kc), rhs=w_in[:, kc, 4096:4128],
                                                              start=(kc == 0), stop=(kc == 7)), r=[w_in, hTt], p=[p])
                K.v(lambda e, p=p, ab=ab: e.tensor_tensor(out=ab[:, 0, 0:16], in0=p[:, 0:16], in1=dtb[:, :], op=ALU.add), r=[p, dtb], w=[ab])
                K.a(lambda e, ab=ab: e.activation(out=ab[:, 0, 0:16], in_=ab[:, 0, 0:16], func=AF.Exp), r=[ab], w=[ab])
                K.a(lambda e, ab=ab: e.activation(out=ab[:, 0, 0:16], in_=ab[:, 0, 0:16], func=AF.Ln, bias=oneb[:, 0:1]), r=[ab, oneb], w=[ab])
                K.v(lambda e, ab=ab: e.tensor_tensor(out=ab[:, 1, 0:16], in0=ab[:, 0, 0:16], in1=nA[:, :], op=ALU.mult), r=[ab, nA], w=[ab])
                K.a(lambda e, p=p, ab=ab: e.activation(out=ab[:, 2, 0:16], in_=p[:, 16:32], func=AF.Exp, scale=-1.0), r=[p], w=[ab])
                K.v(lambda e, ab=ab: e.tensor_scalar(out=ab[:, 2, 0:16], in0=ab[:, 2, 0:16], scalar1=1.0, scalar2=None, op0=ALU.add), r=[ab], w=[ab])
                K.v(lambda e, ab=ab: e.reciprocal(out=ab[:, 3, 0:16], in_=ab[:, 2, 0:16]), r=[ab], w=[ab])
                K.store(sc["GG"], sc["GG"].t[t0:t0 + 128, :], ab, ab[:, 1, 0:16])
                K.store(sc["BB"], sc["BB"].t[t0:t0 + 128, :], ab, ab[:, 3, 0:16])

    if upto <= 1:
        return finish(K, S, yout, seqs)
    K.new_phase()
    cw = K.sb([128, 24, 5], F32, "cw")
    K.load(cw, cw[:, :, :], d_convw, d_convw.t[:, :, :])
    pin = [K.sb([128, 516], BF16, f"pin{i}") for i in range(3)]
    acc = [K.sb([128, 512], F32, f"acc{i}") for i in range(2)]
    sil = [K.sb([128, 512], F32, f"sil{i}") for i in range(2)]
    sqb = [K.sb([128, 512], BF16, f"sqb{i}") for i in range(2)]
    rn = [K.sb([128, 512], F32, f"rn{i}") for i in range(2)]
    qkn = [K.sb([128, 512], BF16, f"qkn{i}") for i in range(2)]
    tkb = [K.sb([128, 4, 128], BF16, f"tkb{i}") for i in range(2)]
    i2 = 0
    for nm, L in seqs:
        sc = S[nm]
        for blk in range(L // 512):
            for ct in range(24):
                pi, ac, si, sq_, rn_, qn, tk = pin[i2 % 3], acc[i2 % 2], sil[i2 % 2], sqb[i2 % 2], rn[i2 % 2], qkn[i2 % 2], tkb[i2 % 2]
                i2 += 1
                K.load(pi, pi[:, :], sc["PT"], sc["PT"].t[ct * 128:(ct + 1) * 128, blk * 512:blk * 512 + 516])
                K.v(lambda e, pi=pi, ac=ac, ct=ct: e.tensor_scalar(out=ac[:, :], in0=pi[:, 0:512], scalar1=cw[:, ct, 0:1], scalar2=None, op0=ALU.mult),
                    r=[pi, cw], w=[ac])
                for k in range(1, 5):
                    X("vector",
                      lambda e, pi=pi, ac=ac, ct=ct, k=k: e.scalar_tensor_tensor(out=ac[:, :], in0=pi[:, k:k + 512], scalar=cw[:, ct, k:k + 1], in1=ac[:, :],
                                                                                 op0=ALU.mult, op1=ALU.add), r=[pi, cw, ac], w=[ac])
                if ct < 16:
                    K.a(lambda e, ac=ac, si=si: e.activation(out=si[:, :], in_=ac[:, :], func=AF.Silu), r=[ac], w=[si])
                    K.g(lambda e, si=si, sq_=sq_: e.tensor_tensor(out=sq_[:, :], in0=si[:, :], in1=si[:, :], op=ALU.mult), r=[si], w=[sq_])
                    p = psb()
                    K.pe(lambda e, p=p, sq_=sq_: e.matmul(p[:, :], lhsT=onesb[:, :], rhs=sq_[:, :], start=True, stop=True), r=[onesb, sq_], p=[p])
                    K.v(lambda e, p=p, rn_=rn_: e.tensor_scalar(out=rn_[:, :], in0=p[:, :], scalar1=EPS, scalar2=None, op0=ALU.add), r=[p], w=[rn_])
                    K.a(lambda e, rn_=rn_: e.sqrt(out=rn_[:, :], in_=rn_[:, :]), r=[rn_], w=[rn_])
                    K.v(lambda e, rn_=rn_: e.reciprocal(out=rn_[:, :], in_=rn_[:, :]), r=[rn_], w=[rn_])
                    scl = (128 ** -0.5) if ct < 8 else 1.0
                    K.v(lambda e, si=si, rn_=rn_, qn=qn, scl=scl: e.scalar_tensor_tensor(out=qn[:, :], in0=si[:, :], scalar=scl, in1=rn_[:, :], op0=ALU.mult, op1=ALU.mult),
                        r=[si, rn_], w=[qn])
                    hd = ct % 8
                    dst = sc["QT"] if ct < 8 else sc["KT"]
                    K.store(dst, dst.t[blk * 4:(blk + 1) * 4, :, hd, :].rearrange("t d k -> d t k"), qn, qn[:, :].rearrange("p (t k) -> p t k", t=4))
                else:
                    K.a(lambda e, ac=ac, qn=qn: e.activation(out=qn[:, :], in_=ac[:, :], func=AF.Silu), r=[ac], w=[qn])
                    hd = ct - 16
                if ct >= 8:
                    p = pst()
                    for tt in range(4):
                        K.pe(lambda e, p=p, qn=qn, tt=tt: e.transpose(out=p[:, tt * 128:(tt + 1) * 128], in_=qn[:, tt * 128:(tt + 1) * 128], identity=identb[:, :]),
                             r=[qn, identb], p=[p])
                    evac(tk[:, :, :], p[:, 0:512].rearrange("p (t d) -> p t d", t=4), [p], [tk])
                    dst = sc["KTOK"] if ct < 16 else sc["VTOK"]
                    K.store(dst, dst.t[blk * 512:(blk + 1) * 512, hd * 128:(hd + 1) * 128].rearrange("(t p) d -> p t d", p=128), tk, tk[:, :, :])
    if upto <= 2:
        return finish(K, S, yout, seqs)

    K.new_phase()
    PSQ = [Buf(K, PSB[i].t[:, j * 128:(j + 1) * 128], f"psq{i}_{j}") for i in range(6) for j in range(4)]
    PSTQ = [Buf(K, PST[i].t[:, j * 128:(j + 1) * 128], f"pstq{i}_{j}") for i in range(2) for j in range(8)]
    psq_i = [0]
    pstq_i = [0]

    def psq():
        psq_i[0] += 1
        return PSQ[psq_i[0] % len(PSQ)]

    def pstq():
        pstq_i[0] += 1
        return PSTQ[pstq_i[0] % len(PSTQ)]

    def ev(out_ap, in_ap, rb, wb):
        evac(out_ap, in_ap, rb, wb)

    qTd = [[K.sb([128, 8, 128], BF16, f"qTd{d}{i}") for i in range(2)] for d in range(2)]
    kTd = [[K.sb([128, 8, 128], BF16, f"kTd{d}{i}") for i in range(2)] for d in range(2)]
    ktk = [[K.sb([128, D], BF16, f"ktk{d}{i}") for i in range(2)] for d in range(2)]
    vtk = [[K.sb([128, D], BF16, f"vtk{d}{i}") for i in range(2)] for d in range(2)]
    ggd = [[K.sb([128, 8], F32, f"gg{d}{i}") for i in range(2)] for d in range(2)]
    bbd = [[K.sb([128, 8], F32, f"bb{d}{i}") for i in range(2)] for d in range(2)]
    gpd = [[K.sb([128, 8, 8], F32, f"gp{d}{i}") for i in range(2)] for d in range(2)]
    gsd = [[K.sb([128, 3, 8], BF16, f"gs{d}{i}") for i in range(2)] for d in range(2)]
    grd = [[K.sb([128, 2, 8], F32, f"gr{d}{i}") for i in range(2)] for d in range(2)]
    o32 = [[K.sb([128, D], F32, f"o32{d}{i}") for i in range(2)] for d in range(2)]
    S32 = [K.sb([128, 128], F32, f"S32_{u}") for u in range(16)]
    Sb = [K.sb([128, 128], BF16, f"Sb_{u}") for u in range(16)]
    Ag3 = [K.sb([128, 3, 128], BF16, f"Ag3_{u}") for u in range(4)]
    kT2 = [K.sb([128, 8, 128], BF16, f"kT2_{d}") for d in range(2)]
    Eb = [K.sb([128, 128], F32, f"E{u}") for u in range(4)]
    EMS = [K.sb([128, 128], F32, f"EMS{u}") for u in range(4)]
    EMI = [K.sb([128, 128], F32, f"EMI{u}") for u in range(4)]
    Xb = [K.sb([128, 128], BF16, f"X{u}") for u in range(16)]
    XTb = [K.sb([128, 128], BF16, f"XT{u}") for u in range(16)]
    Pb = [[K.sb([128, 128], BF16, f"P{u}_{i}") for i in range(2)] for u in range(16)]
    PTb = [[K.sb([128, 128], BF16, f"PT{u}_{i}") for i in range(2)] for u in range(16)]
    Qb = [K.sb([128, 128], BF16, f"Q{u}") for u in range(16)]
    ATb = [K.sb([128, 128], BF16, f"AT{u}") for u in range(16)]
    rb_ = [K.sb([128, 128], BF16, f"r{u}") for u in range(16)]
    vnb = [K.sb([128, 128], BF16, f"vn{u}") for u in range(16)]
    vnsb = [K.sb([128, 128], BF16, f"vns{u}") for u in range(16)]
    tmpb = [K.sb([128, 128], F32, f"tmp{u}") for u in range(4)]
    for u in range(16):
        for t_ in (rb_[u], vnb[u], vnsb[u]):
            K.g(lambda e, t_=t_: e.memset(t_[:, :], 0.0), w=[t_])

    for nm, L in seqs:
        sc = S[nm]
        ntile = L // 128
        for u in range(16):
            K.g(lambda e, u=u: e.memset(S32[u][:, :], 0.0), w=[S32[u]])
            K.g(lambda e, u=u: e.memset(Sb[u][:, :], 0.0), w=[Sb[u]])
        for it_ in range(ntile):
            par = it_ % 2
            cur = {}
            for d in range(2):
                tile = it_ if d == 0 else ntile - 1 - it_
                t0 = tile * 128
                qT_, kT_, kt_, vt_, gg_, bb_, gp_, o_ = qTd[d][par], kTd[d][par], ktk[d][par], vtk[d][par], ggd[d][par], bbd[d][par], gpd[d][par], o32[d][par]
                cur[d] = (qT_, kT_, kt_, vt_, gg_, bb_, gp_, o_, t0)
                gs_, gr_ = gsd[d][par], grd[d][par]
                cur[(d, "gs")] = gs_
                K.load(qT_, qT_[:, :, :], sc["QT"], sc["QT"].t[tile])
                K.load(kT_, kT_[:, :, :], sc["KT"], sc["KT"].t[tile])
                K.load(kt_, kt_[:, :], sc["KTOK"], sc["KTOK"].t[t0:t0 + 128, :])
                K.load(vt_, vt_[:, :], sc["VTOK"], sc["VTOK"].t[t0:t0 + 128, :])
                K.load(gg_, gg_[:, :], sc["GG"], sc["GG"].t[t0:t0 + 128, d * 8:(d + 1) * 8])
                K.load(bb_, bb_[:, :], sc["BB"], sc["BB"].t[t0:t0 + 128, d * 8:(d + 1) * 8])
                pg = psb()
                tri = C_TRIF if d == 0 else C_TRIB
                K.v(lambda e, gs_=gs_, gg_=gg_: e.tensor_copy(out=gs_[:, 0, :], in_=gg_[:, :]), r=[gg_], w=[gs_])
                K.v(lambda e, gs_=gs_, gg_=gg_, gr_=gr_: e.tensor_tensor(out=gr_[:, 0, :], in0=gg_[:, :], in1=gs_[:, 0, :], op=ALU.subtract), r=[gg_, gs_], w=[gr_])
                K.v(lambda e, gs_=gs_, gr_=gr_: e.tensor_copy(out=gs_[:, 1, :], in_=gr_[:, 0, :]), r=[gr_], w=[gs_])
                K.v(lambda e, gs_=gs_, gr_=gr_: e.tensor_tensor(out=gr_[:, 1, :], in0=gr_[:, 0, :], in1=gs_[:, 1, :], op=ALU.subtract), r=[gr_, gs_], w=[gr_])
                K.v(lambda e, gs_=gs_, gr_=gr_: e.tensor_copy(out=gs_[:, 2, :], in_=gr_[:, 1, :]), r=[gr_], w=[gs_])
                for j, cm in enumerate((tri, C_BLK, C_CS0, C_CS1)):
                    for q3 in range(3):
                        K.pe(lambda e, pg=pg, j=j, cm=cm, gs_=gs_, q3=q3: e.matmul(pg[:, j * 8:(j + 1) * 8], lhsT=cstb[:, cm, :], rhs=gs_[:, q3, :],
                                                                             start=(q3 == 0), stop=(q3 == 2)), r=[cstb, gs_], p=[pg])
                K.v(lambda e, pg=pg, gp_=gp_: e.tensor_copy(out=gp_[:, 0, :], in_=pg[:, 0:8]), r=[pg], p=[gp_])
                K.a(lambda e, pg=pg, gp_=gp_: e.activation(out=gp_[:, 1, :], in_=pg[:, 0:8], func=AF.Exp), r=[pg], p=[gp_])
                K.v(lambda e, gp_=gp_: e.tensor_scalar(out=gp_[:, 2, :], in0=gp_[:, 1, :], scalar1=-1.0, scalar2=None, op0=ALU.mult), r=[gp_], w=[gp_])
                K.v(lambda e, pg=pg, gp_=gp_: e.tensor_tensor(out=gp_[:, 7, :], in0=pg[:, 8:16], in1=gp_[:, 0, :], op=ALU.subtract), r=[pg, gp_], w=[gp_])
                K.a(lambda e, gp_=gp_: e.activation(out=gp_[:, 7, :], in_=gp_[:, 7, :], func=AF.Exp), r=[gp_], w=[gp_])
                K.v(lambda e, gp_=gp_, bb_=bb_: e.tensor_tensor(out=gp_[:, 3, :], in0=gp_[:, 7, :], in1=bb_[:, :], op=ALU.mult), r=[gp_, bb_], w=[gp_])
                K.a(lambda e, pg=pg, gp_=gp_: e.activation(out=gp_[:, 4:6, :], in_=pg[:, 16:32].rearrange("p (a b) -> p a b", a=2), func=AF.Exp), r=[pg], w=[gp_])
                K.v(lambda e, gp_=gp_, bb_=bb_: e.tensor_scalar(out=gp_[:, 6, :], in0=bb_[:, :], scalar1=-1.0, scalar2=None, op0=ALU.mult), r=[bb_], w=[gp_])
            import os as _os
            PH3 = int(_os.environ.get('PH3', '9'))
            for ug in range(4 if PH3 >= 1 else 0):
                units = list(range(4 * ug, 4 * ug + 4))
                d = units[0] // 8
                qT_, kT_, kt_, vt_, gg_, bb_, gp_, o_, t0 = cur[d]
                cA = C_AF if d == 0 else C_AB
                cB = C_TRIF if d == 0 else C_TRIB
                cMS = C_MSF if d == 0 else C_MSB
                cMI = C_MIF if d == 0 else C_MIB
                sl = lambda j: slice(j * 128, (j + 1) * 128)
                gs_ = cur[(d, "gs")]
                kT2_ = kT2[d]
                if ug % 2 == 0:
                    K.g(lambda e, kT2_=kT2_: e.tensor_copy(out=kT2_[:, :, :], in_=kT_[:, :, :]), r=[kT_], w=[kT2_])
                for j, u in enumerate(units):
                    h = u % 8
                    for q3 in range(3):
                        K.g(lambda e, j=j, h=h, q3=q3: e.tensor_scalar(out=Ag3[j][:, q3, :], in0=cstb[:, cA, :], scalar1=gs_[:, q3, h:h + 1], scalar2=None, op0=ALU.mult),
                            r=[cstb, gs_], p=[Ag3[j]])
                bD = psb()
                for j, u in enumerate(units):
                    for q3 in range(3):
                        K.pe(lambda e, j=j, q3=q3: e.matmul(bD[:, sl(j)], lhsT=Ag3[j][:, q3, :], rhs=cstb[:, cB, :], start=(q3 == 0), stop=(q3 == 2)),
                             r=[Ag3[j], cstb], p=[bD])
                for j, u in enumerate(units):
                    K.a(lambda e, j=j: e.activation(out=Eb[j][:, :], in_=bD[:, sl(j)], func=AF.Exp), r=[bD], w=[Eb[j]])
                if PH3 == 1 and int(_os.environ.get('PH3B', '9')) < 0:
                    continue
                bG = psb()
                for j, u in enumerate(units):
                    h = u % 8
                    K.pe(lambda e, j=j, h=h, kT2_=kT2_: e.matmul(bG[:, sl(j)], lhsT=kT_[:, h, :], rhs=kT2_[:, h, :], start=True, stop=True), r=[kT_, kT2_], p=[bG])
                bQK = psb()
                for j, u in enumerate(units):
                    h = u % 8
                    K.pe(lambda e, j=j, h=h: e.matmul(bQK[:, sl(j)], lhsT=kT_[:, h, :], rhs=qT_[:, h, :], start=True, stop=True), r=[kT_, qT_], p=[bQK])
                for j, u in enumerate(units):
                    K.g(lambda e, j=j: e.tensor_tensor(out=EMS[j][:, :], in0=Eb[j][:, :], in1=cst[:, cMS, :], op=ALU.mult), r=[Eb[j], cst], w=[EMS[j]])
                    K.g(lambda e, j=j: e.tensor_tensor(out=EMI[j][:, :], in0=Eb[j][:, :], in1=cst[:, cMI, :], op=ALU.mult), r=[Eb[j], cst], w=[EMI[j]])
                for j, u in enumerate(units):
                    h = u % 8
                    K.v(lambda e, j=j, u=u, h=h: e.scalar_tensor_tensor(out=Xb[u][:, :], in0=bG[:, sl(j)], scalar=gp_[:, 6, h:h + 1], in1=EMS[j][:, :],
                                                                      op0=ALU.mult, op1=ALU.mult), r=[bG, gp_, EMS[j]], w=[Xb[u]])
                for j, u in enumerate(units):
                    K.v(lambda e, j=j, u=u: e.tensor_tensor(out=ATb[u][:, :], in0=bQK[:, sl(j)], in1=EMI[j][:, :], op=ALU.mult), r=[bQK, EMI[j]], w=[ATb[u]])
                if PH3 == 1 and int(_os.environ.get('PH3B', '9')) < 1:
                    continue
                tp = pst()
                for j, u in enumerate(units):
                    K.pe(lambda e, j=j, u=u: e.transpose(out=tp[:, sl(j)], in_=Xb[u][:, :], identity=identb[:, :]), r=[Xb[u], identb], p=[tp])
                for j, u in enumerate(units):
                    K.a(lambda e, j=j, u=u: e.copy(out=XTb[u][:, :], in_=tp[:, sl(j)]), r=[tp], w=[XTb[u]])
                    K.g(lambda e, u=u: e.tensor_tensor(out=Qb[u][:, :], in0=Xb[u][:, :], in1=identb[:, :], op=ALU.add), r=[Xb[u], identb], w=[Qb[u]])
                Pc = {u: (Xb[u], XTb[u]) for u in units}
                for lev in range(1, 6 if int(_os.environ.get('PH3B', '9')) >= 2 else 1):
                    if lev < 5:
                        bP = psb()
                        for j, u in enumerate(units):
                            P_, PT_ = Pc[u]
                            K.pe(lambda e, j=j, P_=P_, PT_=PT_, bP=bP: e.matmul(bP[:, sl(j)], lhsT=PT_[:, :], rhs=P_[:, :], start=True, stop=True), r=[P_, PT_], p=[bP])
                    bPT = psb()
                    for j, u in enumerate(units):
                        P_, PT_ = Pc[u]
                        K.pe(lambda e, j=j, P_=P_, PT_=PT_, bPT=bPT: e.matmul(bPT[:, sl(j)], lhsT=P_[:, :], rhs=PT_[:, :], start=True, stop=True), r=[P_, PT_], p=[bPT])
                    for j, u in enumerate(units):
                        Pn, PnT = Pb[u][lev % 2], PTb[u][lev % 2]
                        if lev < 5:
                            K.v(lambda e, j=j, Pn=Pn, bP=bP: e.tensor_copy(out=Pn[:, :], in_=bP[:, sl(j)]), r=[bP], w=[Pn])
                        K.a(lambda e, j=j, PnT=PnT, bPT=bPT: e.copy(out=PnT[:, :], in_=bPT[:, sl(j)]), r=[bPT], w=[PnT])
                    bQ = psb()
                    for j, u in enumerate(units):
                        PnT = PTb[u][lev % 2]
                        K.pe(lambda e, j=j, u=u, PnT=PnT, bQ=bQ: e.matmul(bQ[:, sl(j)], lhsT=PnT[:, :], rhs=Qb[u][:, :], start=True, stop=True), r=[PnT, Qb[u]], p=[bQ])
                    for j, u in enumerate(units):
                        K.v(lambda e, j=j, u=u, bQ=bQ: e.tensor_tensor(out=Qb[u][:, :], in0=bQ[:, sl(j)], in1=Qb[u][:, :], op=ALU.add), r=[bQ, Qb[u]], w=[Qb[u]])
                        Pc[u] = (Pb[u][lev % 2], PTb[u][lev % 2])
            for s_ in range(2 if PH3 >= 2 else 0):
                for stage in range(4):
                    for ug in range(4):
                        units = list(range(4 * ug, 4 * ug + 4))
                        d = units[0] // 8
                        qT_, kT_, kt_, vt_, gg_, bb_, gp_, o_, t0 = cur[d]
                        c = s_ if d == 0 else 1 - s_
                        rows = slice(64 * c, 64 * c + 64)
                        sl = lambda j: slice(j * 128, (j + 1) * 128)
                        hsl = lambda u: slice((u % 8) * 128, (u % 8 + 1) * 128)
                        if stage == 0:
                            b1 = psb()
                            for j, u in enumerate(units):
                                K.pe(lambda e, j=j, u=u, b1=b1: e.matmul(b1[:, sl(j)], lhsT=kT_[:, u % 8, :], rhs=Sb[u][:, :], start=True, stop=True), r=[kT_, Sb[u]], p=[b1])
                            for j, u in enumerate(units):
                                K.v(lambda e, j=j, u=u, b1=b1: e.scalar_tensor_tensor(
                                    out=rb_[u][rows, :], in0=b1[rows, sl(j)], scalar=gp_[rows, 2, u % 8:u % 8 + 1], in1=vt_[rows, hsl(u)], op0=ALU.mult, op1=ALU.add),
                                    r=[b1, gp_, vt_], w=[rb_[u]])
                        elif stage == 1:
                            b2 = psb()
                            for j, u in enumerate(units):
                                K.pe(lambda e, j=j, u=u, b2=b2: e.matmul(b2[:, sl(j)], lhsT=Qb[u][:, :], rhs=rb_[u][:, :], start=True, stop=True), r=[Qb[u], rb_[u]], p=[b2])
                            for j, u in enumerate(units):
                                K.v(lambda e, j=j, u=u, b2=b2: e.tensor_scalar(out=vnb[u][rows, :], in0=b2[rows, sl(j)], scalar1=bb_[rows, u % 8:u % 8 + 1], scalar2=None, op0=ALU.mult),
                                    r=[b2, bb_], w=[vnb[u]])
                                K.g(lambda e, u=u: e.tensor_scalar(out=vnsb[u][rows, :], in0=vnb[u][rows, :], scalar1=gp_[rows, 7, u % 8:u % 8 + 1], scalar2=None, op0=ALU.mult),
                                    r=[vnb[u], gp_], w=[vnsb[u]])
                        elif stage == 2:
                            bq = psb()
                            for j, u in enumerate(units):
                                K.pe(lambda e, j=j, u=u, bq=bq: e.matmul(bq[:, sl(j)], lhsT=qT_[:, u % 8, :], rhs=Sb[u][:, :], start=True, stop=True), r=[qT_, Sb[u]], p=[bq])
                            ba = psb()
                            for j, u in enumerate(units):
                                K.pe(lambda e, j=j, u=u, ba=ba: e.matmul(ba[:, sl(j)], lhsT=ATb[u][:, :], rhs=vnb[u][:, :], start=True, stop=True), r=[ATb[u], vnb[u]], p=[ba])
                            for j, u in enumerate(units):
                                K.a(lambda e, j=j, ba=ba: e.copy(out=tmpb[j][rows, :], in_=ba[rows, sl(j)]), r=[ba], w=[tmpb[j]])
                            for j, u in enumerate(units):
                                K.v(lambda e, j=j, u=u, bq=bq: e.scalar_tensor_tensor(
                                    out=o_[rows, hsl(u)], in0=bq[rows, sl(j)], scalar=gp_[rows, 1, u % 8:u % 8 + 1], in1=tmpb[j][rows, :], op0=ALU.mult, op1=ALU.add),
                                    r=[bq, tmpb[j], gp_], p=[o_])
                        else:
                            bs = psb()
                            for j, u in enumerate(units):
                                K.pe(lambda e, j=j, u=u, bs=bs: e.matmul(bs[:, sl(j)], lhsT=kt_[rows, hsl(u)], rhs=vnsb[u][rows, :], start=True, stop=True),
                                     r=[kt_, vnsb[u]], p=[bs])
                            for j, u in enumerate(units):
                                K.v(lambda e, j=j, u=u, bs=bs: e.scalar_tensor_tensor(out=S32[u][:, :], in0=S32[u][:, :], scalar=gp_[:, 4 + c, u % 8:u % 8 + 1], in1=bs[:, sl(j)],
                                                                                  op0=ALU.mult, op1=ALU.add), r=[bs, gp_, S32[u]], w=[S32[u]])
                                K.g(lambda e, u=u: e.tensor_copy(out=Sb[u][:, :], in_=S32[u][:, :]), r=[S32[u]], w=[Sb[u]])
            for d in range(2):
                qT_, kT_, kt_, vt_, gg_, bb_, gp_, o_, t0 = cur[d]
                dst = sc["OF"] if d == 0 else sc["OB"]
                K.store(dst, dst.t[t0:t0 + 128, :], o_, o_[:, :])
    if upto <= 3:
        return finish(K, S, yout, seqs)
    K.new_phase()
    bm = K.sb([128, 48, 256], F32, "bm")
    K.load(bm, bm[:, :, :], d_bm, d_bm.t[:, :, :, :, :].rearrange("p g h v k -> p (g h v) k"))
    dq = [K.sb([128, 256], BF16, f"dq{i}") for i in range(2)]
    dk = [K.sb([128, 2, 256], BF16, f"dk{i}") for i in range(2)]
    dv = [K.sb([128, 2, 256], BF16, f"dv{i}") for i in range(2)]
    dqT = [K.sb([128, 2, 128], BF16, f"dqT{i}") for i in range(2)]
    dkT = [K.sb([128, 2, 256], BF16, f"dkT{i}") for i in range(2)]
    ds_ = [K.sb([128, 256], F32, f"ds{i}") for i in range(2)]
    de_ = [K.sb([128, 256], BF16, f"de{i}") for i in range(2)]
    deT = [K.sb([128, 2, 128], BF16, f"deT{i}") for i in range(2)]
    dst_ = [K.sb([128, 16], F32, f"dst{i}") for i in range(2)]
    dus = [K.sb([128, 256], F32, f"dus{i}") for i in range(2)]
    i4 = 0
    i5 = 0
    for nm, L in seqs:
        sc = S[nm]
        QBd = sc["QB"]
        for g, dil in enumerate(DILS):
            Ls = L // dil
            ntile = Ls // 128
            cq = g * 768
            for r in range(dil):
                for j in range(ntile):
                    q_, k2, v2, qT2, kT2_, st, us = dq[i4 % 2], dk[i4 % 2], dv[i4 % 2], dqT[i4 % 2], dkT[i4 % 2], dst_[i4 % 2], dus[i4 % 2]
                    i4 += 1
                    m0 = j * 128
                    qrow0 = 1024 + m0 * dil + r
                    krow0 = 1024 + (m0 - 64) * dil + r
                    span = 127 * dil + 1
                    K.load(q_, q_[:, :], QBd, QBd.t[qrow0:qrow0 + span:dil, cq:cq + 256])
                    for c in range(2):
                        kr = krow0 + c * 128 * dil
                        K.load(k2, k2[:, c, :], QBd, QBd.t[kr:kr + span:dil, cq + 256:cq + 512], part=True)
                        K.load(v2, v2[:, c, :], QBd, QBd.t[kr:kr + span:dil, cq + 512:cq + 768], part=True)
                    tq = pst()
                    for hp in range(2):
                        K.pe(lambda e: e.transpose(out=tq[:, hp * 128:(hp + 1) * 128], in_=q_[:, hp * 128:(hp + 1) * 128], identity=identb[:, :]),
                             r=[q_, identb], p=[tq])
                    evac(qT2[:, :, :], tq[:, 0:256].rearrange("p (a b) -> p a b", a=2), [tq], [qT2])
                    tk = pst()
                    for hp in range(2):
                        for c in range(2):
                            K.pe(lambda e: e.transpose(out=tk[:, (hp * 2 + c) * 128:(hp * 2 + c + 1) * 128], in_=k2[:, c, hp * 128:(hp + 1) * 128], identity=identb[:, :]),
                                 r=[k2, identb], p=[tk])
                    evac(kT2_[:, :, :], tk[:, 0:512].rearrange("p (a b) -> p a b", a=2), [tk], [kT2_])
                    var = (1 if j == 0 else 0) | (2 if j == ntile - 1 else 0)
                    up = psb()
                    for h in range(4):
                        s_, e_, eT_ = ds_[i5 % 2], de_[i5 % 2], deT[i5 % 2]
                        i5 += 1
                        hp = h // 2
                        po = (h % 2) * 64
                        sp = psb()
                        K.pe(lambda e: e.matmul(sp[:, 0:256], lhsT=qT2[po:po + 64, hp, :], rhs=kT2_[po:po + 64, hp, :], start=True, stop=True),
                             r=[qT2, kT2_], p=[sp])
                        bi = g * 16 + h * 4 + var
                        K.v(lambda e: e.scalar_tensor_tensor(out=s_[:, :], in0=sp[:, 0:256], scalar=0.125, in1=bm[:, bi, :], op0=ALU.mult, op1=ALU.add),
                            r=[sp, bm], w=[s_])
                        K.v(lambda e: e.reduce_max(out=st[:, h:h + 1], in_=s_[:, :], axis=AX.X), r=[s_], p=[st])
                        K.v(lambda e: e.tensor_scalar(out=st[:, 8 + h:9 + h], in0=st[:, h:h + 1], scalar1=-1.0, scalar2=None, op0=ALU.mult), r=[st], p=[st])
                        K.a(lambda e: e.activation(out=e_[:, :], in_=s_[:, :], func=AF.Exp, bias=st[:, 8 + h:9 + h], accum_out=st[:, 4 + h:5 + h]),
                            r=[s_, st], w=[e_], p=[st])
                        te = pst()
                        for c in range(2):
                            K.pe(lambda e: e.transpose(out=te[:, c * 128:(c + 1) * 128], in_=e_[:, c * 128:(c + 1) * 128], identity=identb[:, :]),
                                 r=[e_, identb], p=[te])
                        evac(eT_[:, :, :], te[:, 0:256].rearrange("p (a b) -> p a b", a=2), [te], [eT_])
                        for c in range(2):
                            K.pe(lambda e: e.matmul(up[:, h * 64:(h + 1) * 64], lhsT=eT_[:, c, :], rhs=v2[:, c, h * 64:(h + 1) * 64], start=(c == 0), stop=(c == 1)),
                                 r=[eT_, v2], p=[up])
                    evac(us[:, :], up[:, 0:256], [up], [us])
                    trow = m0 * dil + r
                    K.store(sc["U"][g], sc["U"][g].t[trow:trow + span:dil, :], us, us[:, :])
                    K.store(sc["ST"][g], sc["ST"][g].t[trow:trow + span:dil, :], st, st[:, 0:8])
    if upto <= 4:
        return finish(K, S, yout, seqs)

    K.new_phase()
    w_ckv = load_w("w_ckv", D, 2 * D)
    n_mem = load_norm(2)
    mx_ = K.sb([128, D], F32, "mx")
    mh_ = K.sb([128, D], BF16, "mh")
    msq = K.sb([128, D], F32, "msq")
    mss = K.sb([128, 2], F32, "mss")
    mT1 = K.sb([128, 8, 128], BF16, "mT1")
    memT = K.sb([128, 8, 256], BF16, "memT")
    kmT_s = K.sb([128, 8, 256], BF16, "kmT_s")
    vm_s = K.sb([128, 2, D], BF16, "vm_s")
    for nm, L in seqs:
        sc = S[nm]
        for mt in range(2):
            K.load(mx_, mx_[:, :], memin[nm], memin[nm].t[mt * 128:(mt + 1) * 128, :])
            rmsnorm_T(mx_, n_mem, mh_, mT1, msq, mss)
            K.g(lambda e: e.tensor_copy(out=memT[:, :, mt * 128:(mt + 1) * 128], in_=mT1[:, :, :]), r=[mT1], p=[memT])
        for ft in range(8):
            p = psb()
            for kc in range(8):
                K.pe(lambda e: e.matmul(p[:, 0:256], lhsT=w_ckv[:, kc, ft * 128:(ft + 1) * 128], rhs=memT[:, kc, :], start=(kc == 0), stop=(kc == 7)),
                     r=[w_ckv, memT], p=[p])
            evac(kmT_s[:, ft, :], p[:, 0:256], [p], [kmT_s], part=True)
        for mt in range(2):
            for n in range(2):
                p = psb()
                for kc in range(8):
                    K.pe(lambda e: e.matmul(p[:, :], lhsT=memT[:, kc, mt * 128:(mt + 1) * 128], rhs=w_ckv[:, kc, D + n * 512:D + (n + 1) * 512],
                                            start=(kc == 0), stop=(kc == 7)), r=[w_ckv, memT], p=[p])
                evac(vm_s[:, mt, n * 512:(n + 1) * 512], p[:, :], [p], [vm_s], part=True)
        K.store(sc["KMT"], sc["KMT"].t[:, :, :], kmT_s, kmT_s[:, :, :])
        K.store(sc["VM"], sc["VM"].t[:, :, :], vm_s, vm_s[:, :, :])

    K.new_phase()
    w_gate = load_w("w_gate", D, 2 * D)
    w_pa = load_w("w_pa", D, D)
    w_pb = load_w("w_pb", 256, D)
    w_o = load_w("w_o", D, D)
    w_cq = load_w("w_cq", D, D)
    w_co = load_w("w_co", D, D)
    n_mix = load_norm(0)
    n_cross = load_norm(1)
    n_gdn = load_norm(5)
    kmT = K.sb([128, 8, 256], BF16, "kmT")
    vm = K.sb([128, 2, D], BF16, "vm")
    ex = [K.sb([128, D], F32, f"ex{i}") for i in range(2)]
    eh = K.sb([128, D], BF16, "eh")
    ehT = K.sb([128, 8, 128], BF16, "ehT")
    esq = K.sb([128, D], F32, "esq")
    ess = K.sb([128, 2], F32, "ess")
    gates = K.sb([128, 2 * D], BF16, "gates")
    eof = K.sb([128, D], F32, "eof")
    eob = K.sb([128, D], F32, "eob")
    esz = K.sb([128, D], BF16, "esz")
    egs = K.sb([128, 24], F32, "egs")
    eoa = K.sb([128, D], BF16, "eoa")
    eoaT = K.sb([128, 8, 128], BF16, "eoaT")
    eU = [K.sb([128, 256], F32, f"eU{g}") for g in range(3)]
    eST = [K.sb([128, 8], F32, f"eST{g}") for g in range(3)]
    emg = K.sb([128, 40], F32, "emg")
    eacc = K.sb([128, 256], F32, "eacc")
    eobm = K.sb([128, 256], BF16, "eobm")
    eobT = K.sb([128, 2, 128], BF16, "eobT")
    emix = K.sb([128, D], F32, "emix")
    emixb = K.sb([128, D], BF16, "emixb")
    emixT = K.sb([128, 8, 128], BF16, "emixT")
    ex1 = K.sb([128, D], F32, "ex1")
    eh2 = K.sb([128, D], BF16, "eh2")
    eh2T = K.sb([128, 8, 128], BF16, "eh2T")
    eqc = K.sb([128, 8, 128], BF16, "eqc")
    ecs = K.sb([128, 16], F32, "ecs")
    ece = [K.sb([128, 256], BF16, f"ece{i}") for i in range(2)]
    eceT = [K.sb([128, 2, 128], BF16, f"eceT{i}") for i in range(2)]
    eoc = K.sb([128, D], BF16, "eoc")
    eocT = K.sb([128, 8, 128], BF16, "eocT")
    ex2 = [K.sb([128, D], F32, f"ex2_{i}") for i in range(2)]
    i6 = 0
    for nm, L in seqs:
        sc = S[nm]
        K.load(kmT, kmT[:, :, :], sc["KMT"], sc["KMT"].t[:, :, :])
        K.load(vm, vm[:, :, :], sc["VM"], sc["VM"].t[:, :, :])
        for tile in range(L // 128):
            t0 = tile * 128
            xt = ex[i6 % 2]
            x2t = ex2[i6 % 2]
            i6 += 1
            K.load(xt, xt[:, :], xin[nm], xin[nm].t[t0:t0 + 128, :])
            K.load(eof, eof[:, :], sc["OF"], sc["OF"].t[t0:t0 + 128, :])
            K.load(eob, eob[:, :], sc["OB"], sc["OB"].t[t0:t0 + 128, :])
            K.load(esz, esz[:, :], sc["SZ"], sc["SZ"].t[t0:t0 + 128, :])
            for g in range(3):
                K.load(eU[g], eU[g][:, :], sc["U"][g], sc["U"][g].t[t0:t0 + 128, :])
                K.load(eST[g], eST[g][:, :], sc["ST"][g], sc["ST"][g].t[t0:t0 + 128, :])
            rmsnorm_T(xt, n_mix, eh, ehT, esq, ess)
            for n in range(4):
                p = psb()
                for kc in range(8):
                    K.pe(lambda e: e.matmul(p[:, :], lhsT=ehT[:, kc, :], rhs=w_gate[:, kc, n * 512:(n + 1) * 512], start=(kc == 0), stop=(kc == 7)),
                         r=[ehT, w_gate], p=[p])
                K.a(lambda e: e.activation(out=gates[:, n * 512:(n + 1) * 512], in_=p[:, :], func=AF.Sigmoid), r=[p], p=[gates])
            K.v(lambda e: e.tensor_tensor(out=eof[:, :], in0=eof[:, :], in1=eob[:, :], op=ALU.add), r=[eof, eob], w=[eof])
            K.g(lambda e: e.tensor_tensor(out=eob[:, :], in0=eof[:, :], in1=eof[:, :], op=ALU.mult), r=[eof], w=[eob])
            K.v(lambda e: e.reduce_sum(out=egs[:, 0:8], in_=eob[:, :].rearrange("p (h d) -> p h d", h=8), axis=AX.X), r=[eob], w=[egs])
            K.v(lambda e: e.tensor_scalar(out=egs[:, 8:16], in0=egs[:, 0:8], scalar1=1.0 / 128, scalar2=EPS, op0=ALU.mult, op1=ALU.add), r=[egs], w=[egs])
            K.a(lambda e: e.sqrt(out=egs[:, 8:16], in_=egs[:, 8:16]), r=[egs], w=[egs])
            K.v(lambda e: e.reciprocal(out=egs[:, 16:24], in_=egs[:, 8:16]), r=[egs], w=[egs])
            for h in range(8):
                hs = slice(h * 128, (h + 1) * 128)
                K.v(lambda e: e.scalar_tensor_tensor(out=eof[:, hs], in0=eof[:, hs], scalar=egs[:, 16 + h:17 + h], in1=n_gdn[:, hs], op0=ALU.mult, op1=ALU.mult),
                    r=[eof, egs, n_gdn], w=[eof])
            K.v(lambda e: e.tensor_tensor(out=eoa[:, :], in0=eof[:, :], in1=esz[:, :], op=ALU.mult), r=[eof, esz], w=[eoa])
            transpose_to(eoa, eoaT, 8)
            K.v(lambda e: e.tensor_tensor(out=emg[:, 0:4], in0=eST[0][:, 0:4], in1=eST[1][:, 0:4], op=ALU.max), r=[eST[0], eST[1]], w=[emg])
            K.v(lambda e: e.tensor_tensor(out=emg[:, 0:4], in0=emg[:, 0:4], in1=eST[2][:, 0:4], op=ALU.max), r=[emg, eST[2]], w=[emg])
            for g in range(3):
                K.v(lambda e: e.tensor_tensor(out=emg[:, 4 + 4 * g:8 + 4 * g], in0=eST[g][:, 0:4], in1=emg[:, 0:4], op=ALU.subtract), r=[eST[g], emg], w=[emg])
            K.a(lambda e: e.activation(out=emg[:, 4:16], in_=emg[:, 4:16], func=AF.Exp), r=[emg], w=[emg])
            K.v(lambda e: e.tensor_tensor(out=emg[:, 16:20], in0=emg[:, 4:8], in1=eST[0][:, 4:8], op=ALU.mult), r=[emg, eST[0]], w=[emg])
            for g in (1, 2):
                K.v(lambda e: e.tensor_tensor(out=emg[:, 36:40], in0=emg[:, 4 + 4 * g:8 + 4 * g], in1=eST[g][:, 4:8], op=ALU.mult), r=[emg, eST[g]], w=[emg])
                K.v(lambda e: e.tensor_tensor(out=emg[:, 16:20], in0=emg[:, 16:20], in1=emg[:, 36:40], op=ALU.add), r=[emg], w=[emg])
            K.v(lambda e: e.reciprocal(out=emg[:, 20:24], in_=emg[:, 16:20]), r=[emg], w=[emg])
            for g in range(3):
                K.v(lambda e: e.tensor_tensor(out=emg[:, 24 + 4 * g:28 + 4 * g], in0=emg[:, 4 + 4 * g:8 + 4 * g], in1=emg[:, 20:24], op=ALU.mult), r=[emg], w=[emg])
            for h in range(4):
                hs = slice(h * 64, (h + 1) * 64)
                K.v(lambda e: e.tensor_scalar(out=eacc[:, hs], in0=eU[0][:, hs], scalar1=emg[:, 24 + h:25 + h], scalar2=None, op0=ALU.mult), r=[eU[0], emg], w=[eacc])
                for g in (1, 2):
                    K.v(lambda e: e.scalar_tensor_tensor(out=eacc[:, hs], in0=eU[g][:, hs], scalar=emg[:, 24 + 4 * g + h:25 + 4 * g + h], in1=eacc[:, hs],
                                                         op0=ALU.mult, op1=ALU.add), r=[eU[g], emg, eacc], w=[eacc])
            K.v(lambda e: e.tensor_copy(out=eobm[:, :], in_=eacc[:, :]), r=[eacc], w=[eobm])
            transpose_to(eobm, eobT, 2)
            for n in range(2):
                ns = slice(n * 512, (n + 1) * 512)
                p = psb()
                for kc in range(8):
                    K.pe(lambda e: e.matmul(p[:, :], lhsT=eoaT[:, kc, :], rhs=w_pa[:, kc, ns], start=(kc == 0), stop=(kc == 7)), r=[eoaT, w_pa], p=[p])
                K.v(lambda e: e.tensor_tensor(out=emix[:, ns], in0=p[:, :], in1=gates[:, ns], op=ALU.mult), r=[p, gates], p=[emix])
                p2 = psb()
                for kc in range(2):
                    K.pe(lambda e: e.matmul(p2[:, :], lhsT=eobT[:, kc, :], rhs=w_pb[:, kc, ns], start=(kc == 0), stop=(kc == 1)), r=[eobT, w_pb], p=[p2])
                K.v(lambda e: e.tensor_tensor(out=esq[:, ns], in0=p2[:, :], in1=gates[:, D + n * 512:D + (n + 1) * 512], op=ALU.mult), r=[p2, gates], p=[esq])
            K.v(lambda e: e.tensor_tensor(out=emixb[:, :], in0=emix[:, :], in1=esq[:, :], op=ALU.add), r=[emix, esq], w=[emixb])
            transpose_to(emixb, emixT, 8)
            for n in range(2):
                ns = slice(n * 512, (n + 1) * 512)
                p = psb()
                for kc in range(8):
                    K.pe(lambda e: e.matmul(p[:, :], lhsT=emixT[:, kc, :], rhs=w_o[:, kc, ns], start=(kc == 0), stop=(kc == 7)), r=[emixT, w_o], p=[p])
                K.v(lambda e: e.tensor_tensor(out=ex1[:, ns], in0=p[:, :], in1=xt[:, ns], op=ALU.add), r=[p, xt], p=[ex1])
            rmsnorm_T(ex1, n_cross, eh2, eh2T, esq, ess)
            for half in range(2):
                p = psb()
                for f4 in range(4):
                    ft = half * 4 + f4
                    for kc in range(8):
                        K.pe(lambda e: e.matmul(p[:, f4 * 128:(f4 + 1) * 128], lhsT=w_cq[:, kc, ft * 128:(ft + 1) * 128], rhs=eh2T[:, kc, :], start=(kc == 0), stop=(kc == 7)),
                             r=[w_cq, eh2T], p=[p])
                evac(eqc[:, half * 4:(half + 1) * 4, :], p[:, :].rearrange("p (a b) -> p a b", a=4), [p], [eqc], part=True)
            po_ = psb()
            po2_ = psb()
            for hh in range(4):
                ce, ceT = ece[hh % 2], eceT[hh % 2]
                sp = psb()
                for c in range(2):
                    K.pe(lambda e: e.matmul(sp[:, 0:256], lhsT=eqc[:, 2 * hh + c, :], rhs=kmT[:, 2 * hh + c, :], start=(c == 0), stop=(c == 1)), r=[eqc, kmT], p=[sp])
                K.v(lambda e: e.reduce_max(out=ecs[:, hh:hh + 1], in_=sp[:, 0:256], axis=AX.X), r=[sp], p=[ecs])
                K.v(lambda e: e.tensor_scalar(out=ecs[:, 4 + hh:5 + hh], in0=ecs[:, hh:hh + 1], scalar1=-1.0 / 16, scalar2=None, op0=ALU.mult), r=[ecs], p=[ecs])
                K.a(lambda e: e.activation(out=ce[:, :], in_=sp[:, 0:256], func=AF.Exp, scale=1.0 / 16, bias=ecs[:, 4 + hh:5 + hh], accum_out=ecs[:, 8 + hh:9 + hh]),
                    r=[sp, ecs], w=[ce], p=[ecs])
                te = pst()
                for c in range(2):
                    K.pe(lambda e: e.transpose(out=te[:, c * 128:(c + 1) * 128], in_=ce[:, c * 128:(c + 1) * 128], identity=identb[:, :]), r=[ce, identb], p=[te])
                evac(ceT[:, :, :], te[:, 0:256].rearrange("p (a b) -> p a b", a=2), [te], [ceT])
                pv = po_ if hh < 2 else po2_
                for c in range(2):
                    K.pe(lambda e: e.matmul(pv[:, (hh % 2) * 256:(hh % 2 + 1) * 256], lhsT=ceT[:, c, :], rhs=vm[:, c, hh * 256:(hh + 1) * 256], start=(c == 0), stop=(c == 1)),
                         r=[ceT, vm], p=[pv])
            K.v(lambda e: e.reciprocal(out=ecs[:, 12:16], in_=ecs[:, 8:12]), r=[ecs], w=[ecs])
            for hh in range(4):
                pv = po_ if hh < 2 else po2_
                K.v(lambda e: e.tensor_scalar(out=eoc[:, hh * 256:(hh + 1) * 256], in0=pv[:, (hh % 2) * 256:(hh % 2 + 1) * 256], scalar1=ecs[:, 12 + hh:13 + hh], scalar2=None, op0=ALU.mult),
                    r=[pv, ecs], p=[eoc])
            transpose_to(eoc, eocT, 8)
            for n in range(2):
                ns = slice(n * 512, (n + 1) * 512)
                p = psb()
                for kc in range(8):
                    K.pe(lambda e: e.matmul(p[:, :], lhsT=eocT[:, kc, :], rhs=w_co[:, kc, ns], start=(kc == 0), stop=(kc == 7)), r=[eocT, w_co], p=[p])
                K.v(lambda e: e.tensor_tensor(out=x2t[:, ns], in0=p[:, :], in1=ex1[:, ns], op=ALU.add), r=[p, ex1], p=[x2t])
            K.store(sc["X2"], sc["X2"].t[t0:t0 + 128, :], x2t, x2t[:, :])
    if upto <= 5:
        return finish(K, S, yout, seqs)

    K.new_phase()
    w_ff1 = load_w("w_ff1", D, D_FF)
    w_ff3 = load_w("w_ff3", D, D_FF)
    w_ff2 = load_w("w_ff2", D_FF, D)
    n_ffn = load_norm(3)
    n_fin = load_norm(4)
    fx = [K.sb([128, D], F32, f"fx{i}") for i in range(2)]
    fh = K.sb([128, D], BF16, "fh")
    fhT = K.sb([128, 8, 128], BF16, "fhT")
    fsq = K.sb([128, D], F32, "fsq")
    fss = K.sb([128, 2], F32, "fss")
    fs1 = [K.sb([128, 512], F32, f"fs1_{i}") for i in range(2)]
    fhid = K.sb([128, D_FF], BF16, "fhid")
    fhidT = K.sb([128, 22, 128], BF16, "fhidT")
    fx3 = K.sb([128, D], F32, "fx3")
    fy = [K.sb([128, D], F32, f"fy{i}") for i in range(2)]
    i7 = 0
    for nm, L in seqs:
        sc = S[nm]
        for tile in range(L // 128):
            t0 = tile * 128
            xt = fx[i7 % 2]
            yt = fy[i7 % 2]
            i7 += 1
            K.load(xt, xt[:, :], sc["X2"], sc["X2"].t[t0:t0 + 128, :])
            rmsnorm_T(xt, n_ffn, fh, fhT, fsq, fss)
            for n in range(6):
                cw_ = 512 if n < 5 else 256
                ns = slice(n * 512, n * 512 + cw_)
                s1 = fs1[n % 2]
                p1 = psb()
                for kc in range(8):
                    K.pe(lambda e: e.matmul(p1[:, 0:cw_], lhsT=fhT[:, kc, :], rhs=w_ff1[:, kc, ns], start=(kc == 0), stop=(kc == 7)), r=[fhT, w_ff1], p=[p1])
                p3 = psb()
                for kc in range(8):
                    K.pe(lambda e: e.matmul(p3[:, 0:cw_], lhsT=fhT[:, kc, :], rhs=w_ff3[:, kc, ns], start=(kc == 0), stop=(kc == 7)), r=[fhT, w_ff3], p=[p3])
                K.a(lambda e: e.activation(out=s1[:, 0:cw_], in_=p1[:, 0:cw_], func=AF.Silu), r=[p1], w=[s1])
                K.v(lambda e: e.tensor_tensor(out=fhid[:, ns], in0=p3[:, 0:cw_], in1=s1[:, 0:cw_], op=ALU.mult), r=[p3, s1], p=[fhid])
            transpose_to(fhid, fhidT, 22)
            for n in range(2):
                ns = slice(n * 512, (n + 1) * 512)
                p = psb()
                for kc in range(22):
                    K.pe(lambda e: e.matmul(p[:, :], lhsT=fhidT[:, kc, :], rhs=w_ff2[:, kc, ns], start=(kc == 0), stop=(kc == 21)), r=[fhidT, w_ff2], p=[p])
                K.v(lambda e: e.tensor_tensor(out=fx3[:, ns], in0=p[:, :], in1=xt[:, ns], op=ALU.add), r=[p, xt], p=[fx3])
            K.a(lambda e: e.activation(out=fsq[:, :], in_=fx3[:, :], func=AF.Square, accum_out=fss[:, 0:1]), r=[fx3], w=[fsq, fss])
            K.v(lambda e: e.tensor_scalar(out=fss[:, 1:2], in0=fss[:, 0:1], scalar1=1.0 / D, scalar2=EPS, op0=ALU.mult, op1=ALU.add), r=[fss], w=[fss])
            K.a(lambda e: e.sqrt(out=fss[:, 1:2], in_=fss[:, 1:2]), r=[fss], w=[fss])
            K.v(lambda e: e.reciprocal(out=fss[:, 1:2], in_=fss[:, 1:2]), r=[fss], w=[fss])
            K.v(lambda e: e.scalar_tensor_tensor(out=yt[:, :], in0=fx3[:, :], scalar=fss[:, 1:2], in1=n_fin[:, :], op0=ALU.mult, op1=ALU.mult),
                r=[fx3, fss, n_fin], w=[yt])
            K.store(yout[nm], yout[nm].t[t0:t0 + 128, :], yt, yt[:, :])
    return finish(K, S, yout, seqs)


LA, LB = 16384, 2048


def kernel(**inputs):
    inp = {k: np.asarray(v) for k, v in inputs.items()}
    nc = build_program([("A", LA), ("B", LB)])
    cm = common_inputs(inp)
    zx = np.zeros((LA, D), np.float32)
    zm = np.zeros((NMEM, D), np.float32)
    maps = []
    for c in range(8):
        m = dict(cm)
        m["xA"] = np.ascontiguousarray(inp["x_prompt"][c]) if c < 2 else zx
        m["memA"] = np.ascontiguousarray(inp["mem_prompt"][c]) if c < 2 else zm
        m["xB"] = np.ascontiguousarray(inp["x_sample"][c])
        m["memB"] = np.ascontiguousarray(inp["mem_sample"][c])
        maps.append(m)
    res = run_bass_kernel_spmd(nc, maps, core_ids=list(range(8)))
    y_prompt = np.stack([np.asarray(res.results[c]["yA"], dtype=np.float32) for c in range(2)])
    y_sample = np.stack([np.asarray(res.results[c]["yB"], dtype=np.float32) for c in range(8)])
    return y_prompt, y_sample
```

```python
import math
import numpy as np
import concourse.bass as bass
import concourse.mybir as mybir
from concourse.bass_utils import run_bass_kernel_spmd

F32 = mybir.dt.float32
BF16 = mybir.dt.bfloat16
U8 = mybir.dt.uint8
ALU = mybir.AluOpType
AF = mybir.ActivationFunctionType
AX = mybir.AxisListType

D = 1024
IN_COLS = 6432
D_FF = 2816
NMEM = 256
EPS = 1e-6


import types as _types
import os as _os_env


def _freeze(fn):
    if fn.__closure__ is None:
        return fn
    cells = []
    for c in fn.__closure__:
        try:
            cells.append(_types.CellType(c.cell_contents))
        except ValueError:
            cells.append(c)
    return _types.FunctionType(fn.__code__, fn.__globals__, fn.__name__, fn.__defaults__, tuple(cells))


class Counter:
    def __init__(self, K, name, step, limit=24000):
        self.K, self.name, self.step, self.limit = K, name, step, limit
        self.sems = []
        self.val = 0
        self._new()

    def _new(self):
        if self.sems:
            self.K.closed.append((self.sems[-1], self.val))
        self.sems.append(self.K.nc.alloc_semaphore(name=f"{self.name}_{len(self.sems)}"))
        self.val = 0

    def next_event(self):
        if self.val + self.step > self.limit:
            self._new()
        self.val += self.step
        return (self.sems[-1], self.val)

    def last(self):
        return (self.sems[-1], self.val)


class Buf:
    def __init__(self, K, t, name):
        self.K, self.t, self.name = K, t, name
        self.w = {}
        self.r = {}
        self.pr = {}
        self.excl = False
        self._dmac = None

    def __getitem__(self, key):
        return self.t[key]

    @property
    def dmac(self):
        if self._dmac is None:
            self._dmac = self.K.get_dmac()
        return self._dmac


class EngState:
    def __init__(self, K, name):
        self.name = name
        self.counter = Counter(K, "e_" + name, 1)
        self.seen = {}
        self.prog = []


class Kern:
    ENGS = ("tensor", "vector", "scalar", "gpsimd", "sync")

    def __init__(self, nc, arena_bytes=210000):
        self.nc = nc
        self.closed = []
        self.eng = {n: EngState(self, n) for n in self.ENGS}
        self.nbuf = 0
        self.n_inst = 0
        self.arena = nc.alloc_sbuf_tensor("arena", [128, arena_bytes], U8)
        self.arena_bytes = arena_bytes
        self.off = 0
        self.base = 0
        self.dmacs = []
        self.dmac_free = []
        self.phase_bufs = []

    def get_dmac(self):
        if self.dmac_free:
            return self.dmac_free.pop()
        c = Counter(self, f"d{len(self.dmacs)}", 16)
        self.dmacs.append(c)
        return c

    def sb(self, shape, dtype=F32, name=None):
        self.nbuf += 1
        name = name or f"sb{self.nbuf}"
        esz = {F32: 4, BF16: 2, U8: 1}[dtype]
        n = int(np.prod(shape[1:]))
        nbytes = n * esz
        off = (self.off + 31) // 32 * 32
        assert off + nbytes <= self.arena_bytes, f"SBUF arena overflow at {name}: {off + nbytes}"
        self.off = off + nbytes
        v = self.arena[0:shape[0], off:off + nbytes]
        if dtype != U8:
            v = v.bitcast(dtype)
        if len(shape) == 3:
            v = v.rearrange("p (a b) -> p a b", a=shape[1])
        elif len(shape) == 4:
            v = v.rearrange("p (a b c) -> p a b c", a=shape[1], b=shape[2])
        b = Buf(self, v, name)
        self.phase_bufs.append(b)
        return b

    def ps(self, shape, dtype=F32, name=None):
        self.nbuf += 1
        name = name or f"ps{self.nbuf}"
        b = Buf(self, self.nc.alloc_psum_tensor(name, list(shape), dtype), name)
        b.excl = True
        return b

    def dram(self, name, shape, dtype, kind="Internal"):
        return Buf(self, self.nc.dram_tensor(name, list(shape), dtype, kind=kind), name)

    def _deps(self, reads, writes, parts):
        deps = {}

        def merge(d):
            for s, v in d.items():
                if deps.get(s, 0) < v:
                    deps[s] = v

        for b in reads:
            merge(b.w)
            if b.excl:
                merge(b.r)
        for b in writes:
            merge(b.w)
            merge(b.r)
            merge(b.pr)
        for b in parts:
            merge(b.r)
            merge(b.pr)
        return deps

    def _emit_waits(self, E, deps):
        own = E.counter.sems
        for s, v in deps.items():
            if s in own and (E.name == "tensor" or _os_env.environ.get("NOSAME")):
                continue
            if E.seen.get(s, 0) < v:
                E.prog.append(("wait", s, v))
                E.seen[s] = v

    def _commit(self, ev, reads, writes, parts):
        s, v = ev
        for b in reads:
            if b.r.get(s, 0) < v:
                b.r[s] = v
        for b in writes:
            b.w = {s: v}
            b.r = {}
            b.pr = {}
        for b in parts:
            if b.r:
                b.pr = dict(b.w)
                for s2, v2 in b.r.items():
                    if b.pr.get(s2, 0) < v2:
                        b.pr[s2] = v2
                b.w = {}
                b.r = {}
            if b.w.get(s, 0) < v:
                b.w[s] = v

    def op(self, eng, fn, r=(), w=(), p=()):
        E = self.eng[eng]
        self._emit_waits(E, self._deps(r, w, p))
        ev = E.counter.next_event()
        E.prog.append(("inst", _freeze(fn), ev, 1))
        self._commit(ev, r, w, p)
        self.n_inst += 1

    def v(self, fn, **kw):
        self.op("vector", fn, **kw)

    def a(self, fn, **kw):
        self.op("scalar", fn, **kw)

    def g(self, fn, **kw):
        self.op("gpsimd", fn, **kw)

    def pe(self, fn, **kw):
        self.op("tensor", fn, **kw)

    def dma(self, q, out, in_, r=(), w=(), p=(), cbuf=None):
        E = self.eng[q]
        self._emit_waits(E, self._deps(r, w, p))
        ev = cbuf.dmac.next_event()
        E.prog.append(("inst", lambda e: e.dma_start(out=out, in_=in_), ev, 16))
        self._commit(ev, r, w, p)
        self.n_inst += 1

    def load(self, sbuf, sb_ap, dram, dr_ap, part=False):
        if part:
            self.dma("sync", sb_ap, dr_ap, r=[dram], p=[sbuf], cbuf=sbuf)
        else:
            self.dma("sync", sb_ap, dr_ap, r=[dram], w=[sbuf], cbuf=sbuf)

    def store(self, dram, dr_ap, sbuf, sb_ap):
        self.dma("gpsimd", dr_ap, sb_ap, r=[sbuf], p=[dram], cbuf=sbuf)

    def barrier(self, extra=()):
        deps = {}
        for E in self.eng.values():
            s, v = E.counter.last()
            if v > 0:
                deps[s] = v
        for c in self.dmacs:
            s, v = c.last()
            if v > 0:
                deps[s] = v
        for s, v in self.closed:
            deps[s] = v
        for E in self.eng.values():
            self._emit_waits(E, deps)

    def new_phase(self):
        self.barrier()
        for b in self.phase_bufs:
            if b._dmac is not None:
                self.dmac_free.append(b._dmac)
                b._dmac = None
        self.phase_bufs = []
        self.off = self.base

    def persist(self):
        self.base = self.off
        self.phase_bufs = []

    def build(self):
        nc = self.nc

        def run(E, e):
            for item in E.prog:
                if item[0] == "wait":
                    e.wait_ge(item[1], item[2])
                else:
                    _, fn, (s, v), step = item
                    fn(e).then_inc(s, step)

        with nc.Block() as block:
            @block.tensor
            def _(e):
                run(self.eng["tensor"], e)

            @block.vector
            def _(e):
                run(self.eng["vector"], e)

            @block.scalar
            def _(e):
                run(self.eng["scalar"], e)

            @block.gpsimd
            def _(e):
                run(self.eng["gpsimd"], e)

            @block.sync
            def _(e):
                run(self.eng["sync"], e)


NCONST = 15
(C_ID, C_TRIF, C_TRIB, C_BLK, C_CS0, C_CS1, C_AF, C_AB, C_MSF, C_MIF, C_MSB, C_MIB, C_ONES, C_X1, C_X2) = range(NCONST)


def make_consts():
    i = np.arange(128)
    a = i[:, None]
    b = i[None, :]
    same = (a // 64) == (b // 64)
    c = np.zeros((NCONST, 128, 128), np.float32)
    c[C_ID] = (a == b)
    c[C_TRIF] = (a <= b) & same
    c[C_TRIB] = (a >= b) & same
    c[C_BLK] = same
    c[C_CS0] = (a < 64) & (b >= 0)
    c[C_CS1] = (a >= 64) & (b >= 0)
    c[C_AF] = (a > b) & same
    c[C_AB] = (a < b) & same
    c[C_MSF] = (b > a) & same
    c[C_MIF] = (b >= a) & same
    c[C_MSB] = (b < a) & same
    c[C_MIB] = (b <= a) & same
    c[C_ONES] = 1.0
    return np.ascontiguousarray(c.transpose(1, 0, 2))


def t5_bucket(rel):
    half = 16
    exact = 8
    n = np.abs(rel)
    large = exact + (np.log(np.maximum(n, 1) / exact) / math.log(1024 / exact) * (half - exact)).astype(np.int32)
    large = np.minimum(large, half - 1)
    return (rel > 0).astype(np.int32) * half + np.where(n < exact, n, large).astype(np.int32)


DILS = (1, 4, 16)


def make_biasmask(rel_bias):
    qi = np.arange(128)[:, None]
    kj = np.arange(256)[None, :]
    rel = kj - 64 - qi
    band = np.abs(rel) <= 64
    out = np.empty((128, 3, 4, 4, 256), np.float32)
    for g, dil in enumerate(DILS):
        bk = t5_bucket(rel * dil)
        for h in range(4):
            vals = rel_bias[bk, g * 4 + h]
            for var in range(4):
                m = band.copy()
                if var & 1:
                    m = m & (kj >= 64)
                if var & 2:
                    m = m & (kj < 192)
                out[:, g, h, var, :] = np.where(m, vals, np.float32(-1e30))
    return out


WSPECS = [("w_in", D, IN_COLS), ("w_gate", D, 2 * D), ("w_pa", D, D), ("w_pb", 256, D), ("w_o", D, D),
          ("w_cq", D, D), ("w_ckv", D, 2 * D), ("w_co", D, D), ("w_ff1", D, D_FF), ("w_ff3", D, D_FF),
          ("w_ff2", D_FF, D)]


def finish(K, S, yout, seqs):
    K.barrier()
    K.build()
    return K.nc


def common_inputs(inp):
    m = {n: np.ascontiguousarray(inp[n][0]) for n, _, _ in WSPECS}
    norms = np.zeros((6, D), np.float32)
    norms[0] = inp["norm_mix"][0]
    norms[1] = inp["norm_cross"][0]
    norms[2] = inp["norm_mem"][0]
    norms[3] = inp["norm_ffn"][0]
    norms[4] = inp["norm_final"]
    norms[5, :] = np.tile(inp["gdn_norm"][0], 8)
    m["norms"] = norms
    m["conv_w"] = np.ascontiguousarray(inp["conv_w"][0].reshape(5, 24, 128).transpose(2, 1, 0))
    m["a_log"] = np.ascontiguousarray(inp["gdn_a_log"][0].reshape(16))
    m["dt_bias"] = np.ascontiguousarray(inp["gdn_dt_bias"][0].reshape(16))
    m["consts"] = make_consts()
    m["biasmask"] = make_biasmask(np.asarray(inp["rel_bias"]))
    return m


def build_program(seq_lens, debug=(), upto=99):
    nc = bass.Bass("TRN2", target_bir_lowering=False)
    K = Kern(nc)
    X = K.op

    def din(name, shape, dt=F32):
        return K.dram(name, shape, dt, kind="ExternalInput")

    def dscr(name, shape, dt):
        return K.dram(name, shape, dt, kind=("ExternalOutput" if name in debug else "Internal"))

    seqs = [(nm, L) for nm, L in seq_lens if L > 0]
    xin = {nm: din("x" + nm, [L, D]) for nm, L in seqs}
    memin = {nm: din("mem" + nm, [NMEM, D]) for nm, L in seqs}
    yout = {nm: K.dram("y" + nm, [L, D], F32, kind="ExternalOutput") for nm, L in seqs}
    wf32 = {n: din(n, [k, c]) for n, k, c in WSPECS}
    wbf = {n: dscr(n + "_bf", [k, c], BF16) for n, k, c in WSPECS}
    d_norms = din("norms", [6, D])
    d_convw = din("conv_w", [128, 24, 5])
    d_alog = din("a_log", [16])
    d_dtb = din("dt_bias", [16])
    d_consts = din("consts", [128, NCONST, 128])
    d_bm = din("biasmask", [128, 3, 4, 4, 256])

    S = {}
    for nm, L in seqs:
        S[nm] = dict(
            PT=dscr("PT" + nm, [3072, L + 4], BF16),
            SZ=dscr("SZ" + nm, [L, D], BF16),
            GG=dscr("GG" + nm, [L, 16], F32),
            BB=dscr("BB" + nm, [L, 16], F32),
            QB=dscr("QB" + nm, [L + 2048, 2304], BF16),
            QT=dscr("QT" + nm, [L // 128, 128, 8, 128], BF16),
            KT=dscr("KT" + nm, [L // 128, 128, 8, 128], BF16),
            KTOK=dscr("KTOK" + nm, [L, D], BF16),
            VTOK=dscr("VTOK" + nm, [L, D], BF16),
            OF=dscr("OF" + nm, [L, D], F32),
            OB=dscr("OB" + nm, [L, D], F32),
            U=[dscr(f"U{g}" + nm, [L, 256], F32) for g in range(3)],
            ST=[dscr(f"ST{g}" + nm, [L, 8], F32) for g in range(3)],
            X2=dscr("X2" + nm, [L, D], F32),
            KMT=dscr("KMT" + nm, [128, 8, 256], BF16),
            VM=dscr("VM" + nm, [128, 2, D], BF16),
        )

    cst = K.sb([128, NCONST, 128], F32, "cst")
    K.load(cst, cst[:, :, :], d_consts, d_consts.t[:, :, :])
    identb = K.sb([128, 128], BF16, "identb")
    onesb = K.sb([128, 128], BF16, "onesb")
    K.v(lambda e: e.tensor_copy(out=identb[:, :], in_=cst[:, C_ID, :]), r=[cst], w=[identb])
    K.v(lambda e: e.tensor_copy(out=onesb[:, :], in_=cst[:, C_ONES, :]), r=[cst], w=[onesb])
    def load_norm(idx):
        nt = K.sb([128, D], F32, f"norm{idx}")
        K.load(nt, nt[:, :], d_norms, d_norms.t[idx, :].partition_broadcast(128))
        return nt
    cstb = K.sb([128, NCONST, 128], BF16, "cstb")
    K.v(lambda e: e.tensor_copy(out=cstb[:, :, :], in_=cst[:, :, :]), r=[cst], w=[cstb])
    epsb = K.sb([128, 1], F32, "epsb")
    K.v(lambda e: e.memset(epsb[:, :], EPS), w=[epsb])
    oneb = K.sb([128, 1], F32, "oneb")
    K.v(lambda e: e.memset(oneb[:, :], 1.0), w=[oneb])
    K.persist()

    PSB = [K.ps([128, 512], F32, f"psb{i}") for i in range(6)]
    PST = [K.ps([128, 1024], BF16, f"pst{i}") for i in range(2)]
    psb_i = [0]
    pst_i = [0]

    def psb():
        psb_i[0] += 1
        return PSB[psb_i[0] % len(PSB)]

    def pst():
        pst_i[0] += 1
        return PST[pst_i[0] % len(PST)]

    cp_i = [0]

    def evac(out_ap, in_ap, rb, wb, part=False, eng=None):
        cp_i[0] += 1
        kw = dict(r=rb, p=wb) if part else dict(r=rb, w=wb)
        if eng is None:
            eng = "vector" if cp_i[0] % 2 else "scalar"
        if eng == "vector":
            K.v(lambda e: e.tensor_copy(out=out_ap, in_=in_ap), **kw)
        else:
            K.a(lambda e: e.copy(out=out_ap, in_=in_ap), **kw)

    K.new_phase()
    stg_f = [K.sb([128, 2048], F32, f"wsf{i}") for i in range(3)]
    stg_b = [K.sb([128, 2048], BF16, f"wsb{i}") for i in range(3)]
    ci = 0
    for n, kk, cc in WSPECS:
        for kc in range(kk // 128):
            for c0 in range(0, cc, 2048):
                cw = min(2048, cc - c0)
                sf, sbb = stg_f[ci % 3], stg_b[ci % 3]
                K.load(sf, sf[:, 0:cw], wf32[n], wf32[n].t[kc * 128:(kc + 1) * 128, c0:c0 + cw])
                eng = ("vector", "scalar", "gpsimd")[ci % 3]
                if eng == "scalar":
                    K.a(lambda e, sf=sf, sbb=sbb, cw=cw: e.copy(out=sbb[:, 0:cw], in_=sf[:, 0:cw]), r=[sf], w=[sbb])
                else:
                    X(eng, lambda e, sf=sf, sbb=sbb, cw=cw: e.tensor_copy(out=sbb[:, 0:cw], in_=sf[:, 0:cw]), r=[sf], w=[sbb])
                K.store(wbf[n], wbf[n].t[kc * 128:(kc + 1) * 128, c0:c0 + cw], sbb, sbb[:, 0:cw])
                ci += 1

    def load_w(name, rows, cols, c0=0, buf=None):
        kc = rows // 128
        wt = buf or K.sb([128, kc, cols], BF16, "W_" + name)
        for k in range(kc):
            K.load(wt, wt[:, k, :], wbf[name], wbf[name].t[k * 128:(k + 1) * 128, c0:c0 + cols], part=True)
        return wt

    def rmsnorm_T(xt, nt, ht, hT, sq, ss, col0=0):
        K.a(lambda e: e.activation(out=sq[:, :], in_=xt[:, :], func=AF.Square, accum_out=ss[:, 0:1]), r=[xt], w=[sq, ss])
        K.v(lambda e: e.tensor_scalar(out=ss[:, 1:2], in0=ss[:, 0:1], scalar1=1.0 / D, scalar2=EPS, op0=ALU.mult, op1=ALU.add),
            r=[ss], w=[ss])
        K.a(lambda e: e.sqrt(out=ss[:, 1:2], in_=ss[:, 1:2]), r=[ss], w=[ss])
        K.v(lambda e: e.reciprocal(out=ss[:, 1:2], in_=ss[:, 1:2]), r=[ss], w=[ss])
        K.v(lambda e: e.scalar_tensor_tensor(out=ht[:, :], in0=xt[:, :], scalar=ss[:, 1:2], in1=nt[:, :],
                                             op0=ALU.mult, op1=ALU.mult), r=[xt, ss, nt], w=[ht])
        transpose_to(ht, hT, 8, col0=col0)

    def transpose_to(src, dstT, nchunks, src_off=0, col0=0):
        for c0 in range(0, nchunks, 8):
            n = min(8, nchunks - c0)
            p = pst()
            for c in range(n):
                K.pe(lambda e, p=p, c=c, c0=c0: e.transpose(out=p[:, c * 128:(c + 1) * 128],
                                                          in_=src[:, src_off + (c0 + c) * 128: src_off + (c0 + c + 1) * 128],
                                                          identity=identb[:, :]), r=[src, identb], p=[p])
            evac(dstT[:, c0:c0 + n, col0:col0 + 128], p[:, 0:n * 128].rearrange("p (a b) -> p a b", a=n), [p], [dstT], part=True)

    K.new_phase()
    w_in = load_w("w_in", D, IN_COLS)
    n_mix = load_norm(0)
    dtb = K.sb([128, 16], F32, "dtb")
    K.load(dtb, dtb[:, :], d_dtb, d_dtb.t[:].partition_broadcast(128))
    nA = K.sb([128, 16], F32, "nA")
    K.load(nA, nA[:, :], d_alog, d_alog.t[:].partition_broadcast(128))
    K.a(lambda e: e.activation(out=nA[:, :], in_=nA[:, :], func=AF.Exp), r=[nA], w=[nA])
    K.v(lambda e: e.tensor_scalar(out=nA[:, :], in0=nA[:, :], scalar1=-1.0, scalar2=None, op0=ALU.mult), r=[nA], w=[nA])
    zt = K.sb([128, 2304], BF16, "zeros")
    K.v(lambda e: e.memset(zt[:, :], 0.0), w=[zt])
    x_t = [K.sb([128, D], F32, f"x{i}") for i in range(3)]
    h_t = [K.sb([128, D], BF16, f"h{i}") for i in range(2)]
    sq = K.sb([128, D], F32, "sq")
    sst = [K.sb([128, 2], F32, f"ss{i}") for i in range(2)]
    hTb = [K.sb([128, 8, 512], BF16, f"hT{i}") for i in range(2)]
    hT1 = [K.sb([128, 8, 128], BF16, f"hT1_{i}") for i in range(2)]
    pts = [K.sb([128, 6, 512], BF16, f"pts{i}") for i in range(2)]
    szs = [K.sb([128, D], BF16, f"szs{i}") for i in range(2)]
    qbs = [K.sb([128, 2304], BF16, f"qbs{i}") for i in range(2)]
    abt = [K.sb([128, 4, 32], F32, f"abt{i}") for i in range(2)]
    it = 0
    for nm, L in seqs:
        sc = S[nm]
        K.store(sc["PT"], sc["PT"].t[:, 0:2].rearrange("(c p) t -> p c t", p=128), zt, zt[:, 0:48].rearrange("p (c t) -> p c t", t=2))
        K.store(sc["PT"], sc["PT"].t[:, L + 2:L + 4].rearrange("(c p) t -> p c t", p=128), zt, zt[:, 0:48].rearrange("p (c t) -> p c t", t=2))
        for r0 in list(range(0, 1024, 128)) + list(range(L + 1024, L + 2048, 128)):
            K.store(sc["QB"], sc["QB"].t[r0:r0 + 128, :], zt, zt[:, :])
        for blk in range(L // 512):
            hTt = hTb[blk % 2]
            for tt in range(4):
                xt = x_t[it % 3]
                ht = h_t[it % 2]
                ss = sst[it % 2]
                h1 = hT1[it % 2]
                it += 1
                t0 = blk * 512 + tt * 128
                K.load(xt, xt[:, :], xin[nm], xin[nm].t[t0:t0 + 128, :])
                rmsnorm_T(xt, n_mix, ht, hTt, sq, ss, col0=tt * 128)
            for cg in range(4):
                pt_s = pts[cg % 2]
                for cc in range(6):
                    ct = cg * 6 + cc
                    p = psb()
                    for kc in range(8):
                        K.pe(lambda e, p=p, kc=kc, ct=ct, hTt=hTt: e.matmul(p[:, :], lhsT=w_in[:, kc, ct * 128:(ct + 1) * 128], rhs=hTt[:, kc, :],
                                                                       start=(kc == 0), stop=(kc == 7)), r=[w_in, hTt], p=[p])
                    evac(pt_s[:, cc, :], p[:, :], [p], [pt_s], part=True)
                K.store(sc["PT"], sc["PT"].t[cg * 768:(cg + 1) * 768, 2 + blk * 512: 2 + (blk + 1) * 512].rearrange("(c p) t -> p c t", p=128),
                        pt_s, pt_s[:, :, :])
            for tt in range(4):
                t0 = blk * 512 + tt * 128
                lhs = lambda kc, hTt=hTt, tt=tt: hTt[:, kc, tt * 128:(tt + 1) * 128]
                szt = szs[tt % 2]
                for n in range(2):
                    p = psb()
                    for kc in range(8):
                        K.pe(lambda e, p=p, kc=kc, n=n, lhs=lhs: e.matmul(p[:, :], lhsT=lhs(kc), rhs=w_in[:, kc, 3072 + n * 512:3072 + (n + 1) * 512],
                                                                     start=(kc == 0), stop=(kc == 7)), r=[w_in, hTt], p=[p])
                    K.a(lambda e, p=p, n=n, szt=szt: e.activation(out=szt[:, n * 512:(n + 1) * 512], in_=p[:, :], func=AF.Silu), r=[p], p=[szt])
                K.store(sc["SZ"], sc["SZ"].t[t0:t0 + 128, :], szt, szt[:, :])
                qbt = qbs[tt % 2]
                for n in range(5):
                    cw = 512 if n < 4 else 256
                    p = psb()
                    for kc in range(8):
                        K.pe(lambda e, p=p, kc=kc, n=n, cw=cw, lhs=lhs: e.matmul(p[:, 0:cw], lhsT=lhs(kc), rhs=w_in[:, kc, 4128 + n * 512:4128 + n * 512 + cw],
                                                                            start=(kc == 0), stop=(kc == 7)), r=[w_in, hTt], p=[p])
                    evac(qbt[:, n * 512:n * 512 + cw], p[:, 0:cw], [p], [qbt], part=True)
                K.store(sc["QB"], sc["QB"].t[1024 + t0:1024 + t0 + 128, :], qbt, qbt[:, :])
                ab = abt[tt % 2]
                p = psb()
                for kc in range(8):
                    K.pe(lambda e, p=p, kc=kc, lhs=lhs: e.matmul(p[:, 0:32], lhsT=lhs(kc), rhs=w_in[:, kc, 4096:4128],
                                                              start=(kc == 0), stop=(kc == 7)), r=[w_in, hTt], p=[p])
                K.v(lambda e, p=p, ab=ab: e.tensor_tensor(out=ab[:, 0, 0:16], in0=p[:, 0:16], in1=dtb[:, :], op=ALU.add), r=[p, dtb], w=[ab])
                K.a(lambda e, ab=ab: e.activation(out=ab[:, 0, 0:16], in_=ab[:, 0, 0:16], func=AF.Exp), r=[ab], w=[ab])
                K.a(lambda e, ab=ab: e.activation(out=ab[:, 0, 0:16], in_=ab[:, 0, 0:16], func=AF.Ln, bias=oneb[:, 0:1]), r=[ab, oneb], w=[ab])
                K.v(lambda e, ab=ab: e.tensor_tensor(out=ab[:, 1, 0:16], in0=ab[:, 0, 0:16], in1=nA[:, :], op=ALU.mult), r=[ab, nA], w=[ab])
                K.a(lambda e, p=p, ab=ab: e.activation(out=ab[:, 2, 0:16], in_=p[:, 16:32], func=AF.Exp, scale=-1.0), r=[p], w=[ab])
                K.v(lambda e, ab=ab: e.tensor_scalar(out=ab[:, 2, 0:16], in0=ab[:, 2, 0:16], scalar1=1.0, scalar2=None, op0=ALU.add), r=[ab], w=[ab])
                K.v(lambda e, ab=ab: e.reciprocal(out=ab[:, 3, 0:16], in_=ab[:, 2, 0:16]), r=[ab], w=[ab])
                K.store(sc["GG"], sc["GG"].t[t0:t0 + 128, :], ab, ab[:, 1, 0:16])
                K.store(sc["BB"], sc["BB"].t[t0:t0 + 128, :], ab, ab[:, 3, 0:16])

    if upto <= 1:
        return finish(K, S, yout, seqs)
    K.new_phase()
    cw = K.sb([128, 24, 5], F32, "cw")
    K.load(cw, cw[:, :, :], d_convw, d_convw.t[:, :, :])
    pin = [K.sb([128, 516], BF16, f"pin{i}") for i in range(4)]
    acc = [K.sb([128, 512], F32, f"acc{i}") for i in range(4)]
    sil = [K.sb([128, 512], F32, f"sil{i}") for i in range(4)]
    sqb = [K.sb([128, 512], BF16, f"sqb{i}") for i in range(4)]
    rn = [K.sb([128, 512], F32, f"rn{i}") for i in range(4)]
    qkn = [K.sb([128, 512], BF16, f"qkn{i}") for i in range(4)]
    tkb = [K.sb([128, 4, 128], BF16, f"tkb{i}") for i in range(4)]
    i2 = 0
    for nm, L in seqs:
        sc = S[nm]
        for blk in range(L // 512):
            for ct in range(24):
                pi, ac, si, sq_, rn_, qn, tk = pin[i2 % 4], acc[i2 % 4], sil[i2 % 4], sqb[i2 % 4], rn[i2 % 4], qkn[i2 % 4], tkb[i2 % 4]
                i2 += 1
                K.load(pi, pi[:, :], sc["PT"], sc["PT"].t[ct * 128:(ct + 1) * 128, blk * 512:blk * 512 + 516])
                K.v(lambda e, pi=pi, ac=ac, ct=ct: e.tensor_scalar(out=ac[:, :], in0=pi[:, 0:512], scalar1=cw[:, ct, 0:1], scalar2=None, op0=ALU.mult),
                    r=[pi, cw], w=[ac])
                for k in range(1, 5):
                    X("vector",
                      lambda e, pi=pi, ac=ac, ct=ct, k=k: e.scalar_tensor_tensor(out=ac[:, :], in0=pi[:, k:k + 512], scalar=cw[:, ct, k:k + 1], in1=ac[:, :],
                                                                                 op0=ALU.mult, op1=ALU.add), r=[pi, cw, ac], w=[ac])
                if ct < 16:
                    K.a(lambda e, ac=ac, si=si: e.activation(out=si[:, :], in_=ac[:, :], func=AF.Silu), r=[ac], w=[si])
                    K.a(lambda e, si=si, sq_=sq_: e.activation(out=sq_[:, :], in_=si[:, :], func=AF.Square), r=[si], w=[sq_])
                    p = psb()
                    K.pe(lambda e, p=p, sq_=sq_: e.matmul(p[:, :], lhsT=onesb[:, :], rhs=sq_[:, :], start=True, stop=True), r=[onesb, sq_], p=[p])
                    K.v(lambda e, p=p, rn_=rn_: e.tensor_scalar(out=rn_[:, :], in0=p[:, :], scalar1=EPS, scalar2=None, op0=ALU.add), r=[p], w=[rn_])
                    K.a(lambda e, rn_=rn_: e.sqrt(out=rn_[:, :], in_=rn_[:, :]), r=[rn_], w=[rn_])
                    K.v(lambda e, rn_=rn_: e.reciprocal(out=rn_[:, :], in_=rn_[:, :]), r=[rn_], w=[rn_])
                    scl = (128 ** -0.5) if ct < 8 else 1.0
                    K.v(lambda e, si=si, rn_=rn_, qn=qn, scl=scl: e.scalar_tensor_tensor(out=qn[:, :], in0=si[:, :], scalar=scl, in1=rn_[:, :], op0=ALU.mult, op1=ALU.mult),
                        r=[si, rn_], w=[qn])
                    hd = ct % 8
                    dst = sc["QT"] if ct < 8 else sc["KT"]
                    K.store(dst, dst.t[blk * 4:(blk + 1) * 4, :, hd, :].rearrange("t d k -> d t k"), qn, qn[:, :].rearrange("p (t k) -> p t k", t=4))
                else:
                    K.a(lambda e, ac=ac, qn=qn: e.activation(out=qn[:, :], in_=ac[:, :], func=AF.Silu), r=[ac], w=[qn])
                    hd = ct - 16
                if ct >= 8:
                    p = pst()
                    for tt in range(4):
                        K.pe(lambda e, p=p, qn=qn, tt=tt: e.transpose(out=p[:, tt * 128:(tt + 1) * 128], in_=qn[:, tt * 128:(tt + 1) * 128], identity=identb[:, :]),
                             r=[qn, identb], p=[p])
                    evac(tk[:, :, :], p[:, 0:512].rearrange("p (t d) -> p t d", t=4), [p], [tk])
                    dst = sc["KTOK"] if ct < 16 else sc["VTOK"]
                    K.store(dst, dst.t[blk * 512:(blk + 1) * 512, hd * 128:(hd + 1) * 128].rearrange("(t p) d -> p t d", p=128), tk, tk[:, :, :])
    if upto <= 2:
        return finish(K, S, yout, seqs)

    K.new_phase()
    PSQ = [Buf(K, PSB[i].t[:, j * 128:(j + 1) * 128], f"psq{i}_{j}") for i in range(6) for j in range(4)]
    PSTQ = [Buf(K, PST[i].t[:, j * 128:(j + 1) * 128], f"pstq{i}_{j}") for i in range(2) for j in range(8)]
    psq_i = [0]
    pstq_i = [0]

    def psq():
        psq_i[0] += 1
        return PSQ[psq_i[0] % len(PSQ)]

    def pstq():
        pstq_i[0] += 1
        return PSTQ[pstq_i[0] % len(PSTQ)]

    def ev(out_ap, in_ap, rb, wb):
        evac(out_ap, in_ap, rb, wb)

    qTd = [[K.sb([128, 8, 128], BF16, f"qTd{d}{i}") for i in range(2)] for d in range(2)]
    kTd = [[K.sb([128, 8, 128], BF16, f"kTd{d}{i}") for i in range(2)] for d in range(2)]
    ktk = [[K.sb([128, D], BF16, f"ktk{d}{i}") for i in range(2)] for d in range(2)]
    vtk = [[K.sb([128, D], BF16, f"vtk{d}{i}") for i in range(2)] for d in range(2)]
    ggd = [[K.sb([128, 8], F32, f"gg{d}{i}") for i in range(2)] for d in range(2)]
    bbd = [[K.sb([128, 8], F32, f"bb{d}{i}") for i in range(2)] for d in range(2)]
    gpd = [[K.sb([128, 8, 8], F32, f"gp{d}{i}") for i in range(2)] for d in range(2)]
    gsd = [[K.sb([128, 3, 8], BF16, f"gs{d}{i}") for i in range(2)] for d in range(2)]
    grd = [[K.sb([128, 2, 8], F32, f"gr{d}{i}") for i in range(2)] for d in range(2)]
    gfd = [[K.sb([128, 3, 8], F32, f"gf{d}{i}") for i in range(2)] for d in range(2)]
    o32 = [[K.sb([128, D], F32, f"o32{d}{i}") for i in range(2)] for d in range(2)]
    S32 = [K.sb([128, 128], F32, f"S32_{u}") for u in range(16)]
    Sb = [K.sb([128, 128], BF16, f"Sb_{u}") for u in range(16)]
    Ag3 = [K.sb([128, 3, 128], BF16, f"Ag3_{u}") for u in range(4)]
    kT2 = [K.sb([128, 8, 128], BF16, f"kT2_{d}") for d in range(2)]
    Eb = [K.sb([128, 128], F32, f"E{u}") for u in range(4)]
    EMS = [K.sb([128, 128], F32, f"EMS{u}") for u in range(4)]
    EMI = [K.sb([128, 128], F32, f"EMI{u}") for u in range(4)]
    Xb = [K.sb([128, 128], BF16, f"X{u}") for u in range(16)]
    XTb = [K.sb([128, 128], BF16, f"XT{u}") for u in range(16)]
    Pb = [[K.sb([128, 128], BF16, f"P{u}_{i}") for i in range(2)] for u in range(16)]
    PTb = [[K.sb([128, 128], BF16, f"PT{u}_{i}") for i in range(2)] for u in range(16)]
    Qb = [K.sb([128, 128], BF16, f"Q{u}") for u in range(16)]
    ATb = [K.sb([128, 128], BF16, f"AT{u}") for u in range(16)]
    rb_ = [K.sb([128, 128], BF16, f"r{u}") for u in range(16)]
    vnb = [K.sb([128, 128], BF16, f"vn{u}") for u in range(16)]
    vnsb = [K.sb([128, 128], BF16, f"vns{u}") for u in range(16)]
    tmpb = [K.sb([128, 128], F32, f"tmp{u}") for u in range(4)]
    for u in range(16):
        for t_ in (rb_[u], vnb[u], vnsb[u]):
            K.g(lambda e, t_=t_: e.memset(t_[:, :], 0.0), w=[t_])

    for nm, L in seqs:
        sc = S[nm]
        ntile = L // 128
        for u in range(16):
            K.g(lambda e, u=u: e.memset(S32[u][:, :], 0.0), w=[S32[u]])
            K.g(lambda e, u=u: e.memset(Sb[u][:, :], 0.0), w=[Sb[u]])
        for it_ in range(ntile):
            par = it_ % 2
            cur = {}
            for d in range(2):
                tile = it_ if d == 0 else ntile - 1 - it_
                t0 = tile * 128
                qT_, kT_, kt_, vt_, gg_, bb_, gp_, o_ = qTd[d][par], kTd[d][par], ktk[d][par], vtk[d][par], ggd[d][par], bbd[d][par], gpd[d][par], o32[d][par]
                cur[d] = (qT_, kT_, kt_, vt_, gg_, bb_, gp_, o_, t0)
                gs_, gr_, gf_ = gsd[d][par], grd[d][par], gfd[d][par]
                cur[(d, "gs")] = gf_
                K.load(qT_, qT_[:, :, :], sc["QT"], sc["QT"].t[tile])
                K.load(kT_, kT_[:, :, :], sc["KT"], sc["KT"].t[tile])
                K.load(kt_, kt_[:, :], sc["KTOK"], sc["KTOK"].t[t0:t0 + 128, :])
                K.load(vt_, vt_[:, :], sc["VTOK"], sc["VTOK"].t[t0:t0 + 128, :])
                K.load(gg_, gg_[:, :], sc["GG"], sc["GG"].t[t0:t0 + 128, d * 8:(d + 1) * 8])
                K.load(bb_, bb_[:, :], sc["BB"], sc["BB"].t[t0:t0 + 128, d * 8:(d + 1) * 8])
                pg = psb()
                tri = C_TRIF if d == 0 else C_TRIB
                K.v(lambda e, gs_=gs_, gg_=gg_: e.tensor_copy(out=gs_[:, 0, :], in_=gg_[:, :]), r=[gg_], w=[gs_])
                K.v(lambda e, gs_=gs_, gg_=gg_, gr_=gr_: e.tensor_tensor(out=gr_[:, 0, :], in0=gg_[:, :], in1=gs_[:, 0, :], op=ALU.subtract), r=[gg_, gs_], w=[gr_])
                K.v(lambda e, gs_=gs_, gr_=gr_: e.tensor_copy(out=gs_[:, 1, :], in_=gr_[:, 0, :]), r=[gr_], w=[gs_])
                K.v(lambda e, gs_=gs_, gr_=gr_: e.tensor_tensor(out=gr_[:, 1, :], in0=gr_[:, 0, :], in1=gs_[:, 1, :], op=ALU.subtract), r=[gr_, gs_], w=[gr_])
                K.v(lambda e, gs_=gs_, gr_=gr_: e.tensor_copy(out=gs_[:, 2, :], in_=gr_[:, 1, :]), r=[gr_], w=[gs_])
                K.v(lambda e, gs_=gs_, gf_=gf_: e.tensor_copy(out=gf_[:, :, :], in_=gs_[:, :, :]), r=[gs_], w=[gf_])
                for j, cm in enumerate((tri, C_BLK, C_CS0, C_CS1)):
                    for q3 in range(3):
                        K.pe(lambda e, pg=pg, j=j, cm=cm, gs_=gs_, q3=q3: e.matmul(pg[:, j * 8:(j + 1) * 8], lhsT=cstb[:, cm, :], rhs=gs_[:, q3, :],
                                                                             start=(q3 == 0), stop=(q3 == 2)), r=[cstb, gs_], p=[pg])
                K.v(lambda e, pg=pg, gp_=gp_: e.tensor_copy(out=gp_[:, 0, :], in_=pg[:, 0:8]), r=[pg], p=[gp_])
                K.a(lambda e, pg=pg, gp_=gp_: e.activation(out=gp_[:, 1, :], in_=pg[:, 0:8], func=AF.Exp), r=[pg], p=[gp_])
                K.v(lambda e, gp_=gp_: e.tensor_scalar(out=gp_[:, 2, :], in0=gp_[:, 1, :], scalar1=-1.0, scalar2=None, op0=ALU.mult), r=[gp_], w=[gp_])
                K.v(lambda e, pg=pg, gp_=gp_: e.tensor_tensor(out=gp_[:, 7, :], in0=pg[:, 8:16], in1=gp_[:, 0, :], op=ALU.subtract), r=[pg, gp_], w=[gp_])
                K.a(lambda e, gp_=gp_: e.activation(out=gp_[:, 7, :], in_=gp_[:, 7, :], func=AF.Exp), r=[gp_], w=[gp_])
                K.v(lambda e, gp_=gp_, bb_=bb_: e.tensor_tensor(out=gp_[:, 3, :], in0=gp_[:, 7, :], in1=bb_[:, :], op=ALU.mult), r=[gp_, bb_], w=[gp_])
                K.a(lambda e, pg=pg, gp_=gp_: e.activation(out=gp_[:, 4:6, :], in_=pg[:, 16:32].rearrange("p (a b) -> p a b", a=2), func=AF.Exp), r=[pg], w=[gp_])
                K.v(lambda e, gp_=gp_, bb_=bb_: e.tensor_scalar(out=gp_[:, 6, :], in0=bb_[:, :], scalar1=-1.0, scalar2=None, op0=ALU.mult), r=[bb_], w=[gp_])
            import os as _os
            PH3 = int(_os.environ.get('PH3', '9'))
            for ug in range(4 if PH3 >= 1 else 0):
                units = list(range(4 * ug, 4 * ug + 4))
                d = units[0] // 8
                qT_, kT_, kt_, vt_, gg_, bb_, gp_, o_, t0 = cur[d]
                cA = C_AF if d == 0 else C_AB
                cB = C_TRIF if d == 0 else C_TRIB
                cMS = C_MSF if d == 0 else C_MSB
                cMI = C_MIF if d == 0 else C_MIB
                sl = lambda j: slice(j * 128, (j + 1) * 128)
                gs_ = cur[(d, "gs")]
                kT2_ = kT_
                for j, u in enumerate(units):
                    h = u % 8
                    for q3 in range(3):
                        K.a(lambda e, j=j, h=h, q3=q3: e.activation(out=Ag3[j][:, q3, :], in_=cstb[:, cA, :], func=AF.Copy, scale=gs_[:, q3, h:h + 1]),
                            r=[cstb, gs_], p=[Ag3[j]])
                bD = psb()
                for j, u in enumerate(units):
                    for q3 in range(3):
                        K.pe(lambda e, j=j, q3=q3: e.matmul(bD[:, sl(j)], lhsT=Ag3[j][:, q3, :], rhs=cstb[:, cB, :], start=(q3 == 0), stop=(q3 == 2)),
                             r=[Ag3[j], cstb], p=[bD])
                for j, u in enumerate(units):
                    K.a(lambda e, j=j: e.activation(out=Eb[j][:, :], in_=bD[:, sl(j)], func=AF.Exp), r=[bD], w=[Eb[j]])
                if PH3 == 1 and int(_os.environ.get('PH3B', '9')) < 0:
                    continue
                bG = psb()
                for j, u in enumerate(units):
                    h = u % 8
                    K.pe(lambda e, j=j, h=h, kT2_=kT2_: e.matmul(bG[:, sl(j)], lhsT=kT_[:, h, :], rhs=kT2_[:, h, :], start=True, stop=True), r=[kT_], p=[bG])
                bQK = psb()
                for j, u in enumerate(units):
                    h = u % 8
                    K.pe(lambda e, j=j, h=h: e.matmul(bQK[:, sl(j)], lhsT=kT_[:, h, :], rhs=qT_[:, h, :], start=True, stop=True), r=[kT_, qT_], p=[bQK])
                for j, u in enumerate(units):
                    K.v(lambda e, j=j: e.tensor_tensor(out=EMS[j][:, :], in0=Eb[j][:, :], in1=cst[:, cMS, :], op=ALU.mult), r=[Eb[j], cst], w=[EMS[j]])
                    K.v(lambda e, j=j: e.tensor_tensor(out=EMI[j][:, :], in0=Eb[j][:, :], in1=cst[:, cMI, :], op=ALU.mult), r=[Eb[j], cst], w=[EMI[j]])
                for j, u in enumerate(units):
                    h = u % 8
                    K.v(lambda e, j=j, u=u, h=h: e.scalar_tensor_tensor(out=Xb[u][:, :], in0=bG[:, sl(j)], scalar=gp_[:, 6, h:h + 1], in1=EMS[j][:, :],
                                                                      op0=ALU.mult, op1=ALU.mult), r=[bG, gp_, EMS[j]], w=[Xb[u]])
                for j, u in enumerate(units):
                    K.v(lambda e, j=j, u=u: e.tensor_tensor(out=ATb[u][:, :], in0=bQK[:, sl(j)], in1=EMI[j][:, :], op=ALU.mult), r=[bQK, EMI[j]], w=[ATb[u]])
                if PH3 == 1 and int(_os.environ.get('PH3B', '9')) < 1:
                    continue
                tp = pst()
                for j, u in enumerate(units):
                    K.pe(lambda e, j=j, u=u: e.transpose(out=tp[:, sl(j)], in_=Xb[u][:, :], identity=identb[:, :]), r=[Xb[u], identb], p=[tp])
                for j, u in enumerate(units):
                    K.a(lambda e, j=j, u=u: e.copy(out=XTb[u][:, :], in_=tp[:, sl(j)]), r=[tp], w=[XTb[u]])
                    K.v(lambda e, u=u: e.tensor_tensor(out=Qb[u][:, :], in0=Xb[u][:, :], in1=identb[:, :], op=ALU.add), r=[Xb[u], identb], w=[Qb[u]])
                Pc = {u: (Xb[u], XTb[u]) for u in units}
                for lev in range(1, 6 if int(_os.environ.get('PH3B', '9')) >= 2 else 1):
                    if lev < 5:
                        bP = psb()
                        for j, u in enumerate(units):
                            P_, PT_ = Pc[u]
                            K.pe(lambda e, j=j, P_=P_, PT_=PT_, bP=bP: e.matmul(bP[:, sl(j)], lhsT=PT_[:, :], rhs=P_[:, :], start=True, stop=True), r=[P_, PT_], p=[bP])
                    bPT = psb()
                    for j, u in enumerate(units):
                        P_, PT_ = Pc[u]
                        K.pe(lambda e, j=j, P_=P_, PT_=PT_, bPT=bPT: e.matmul(bPT[:, sl(j)], lhsT=P_[:, :], rhs=PT_[:, :], start=True, stop=True), r=[P_, PT_], p=[bPT])
                    for j, u in enumerate(units):
                        Pn, PnT = Pb[u][lev % 2], PTb[u][lev % 2]
                        if lev < 5:
                            K.a(lambda e, j=j, Pn=Pn, bP=bP: e.copy(out=Pn[:, :], in_=bP[:, sl(j)]), r=[bP], w=[Pn])
                        K.a(lambda e, j=j, PnT=PnT, bPT=bPT: e.copy(out=PnT[:, :], in_=bPT[:, sl(j)]), r=[bPT], w=[PnT])
                    bQ = psb()
                    for j, u in enumerate(units):
                        PnT = PTb[u][lev % 2]
                        K.pe(lambda e, j=j, u=u, PnT=PnT, bQ=bQ: e.matmul(bQ[:, sl(j)], lhsT=PnT[:, :], rhs=Qb[u][:, :], start=True, stop=True), r=[PnT, Qb[u]], p=[bQ])
                    for j, u in enumerate(units):
                        K.v(lambda e, j=j, u=u, bQ=bQ: e.tensor_tensor(out=Qb[u][:, :], in0=bQ[:, sl(j)], in1=Qb[u][:, :], op=ALU.add), r=[bQ, Qb[u]], w=[Qb[u]])
                        Pc[u] = (Pb[u][lev % 2], PTb[u][lev % 2])
            for s_ in range(2 if PH3 >= 2 else 0):
                for stage in range(4):
                    for ug in range(4):
                        units = list(range(4 * ug, 4 * ug + 4))
                        d = units[0] // 8
                        qT_, kT_, kt_, vt_, gg_, bb_, gp_, o_, t0 = cur[d]
                        c = s_ if d == 0 else 1 - s_
                        rows = slice(64 * c, 64 * c + 64)
                        sl = lambda j: slice(j * 128, (j + 1) * 128)
                        hsl = lambda u: slice((u % 8) * 128, (u % 8 + 1) * 128)
                        if stage == 0:
                            b1 = psb()
                            for j, u in enumerate(units):
                                K.pe(lambda e, j=j, u=u, b1=b1: e.matmul(b1[:, sl(j)], lhsT=kT_[:, u % 8, :], rhs=Sb[u][:, :], start=True, stop=True), r=[kT_, Sb[u]], p=[b1])
                            for j, u in enumerate(units):
                                K.v(lambda e, j=j, u=u, b1=b1: e.scalar_tensor_tensor(
                                    out=rb_[u][rows, :], in0=b1[rows, sl(j)], scalar=gp_[rows, 2, u % 8:u % 8 + 1], in1=vt_[rows, hsl(u)], op0=ALU.mult, op1=ALU.add),
                                    r=[b1, gp_, vt_], w=[rb_[u]])
                        elif stage == 1:
                            b2 = psb()
                            for j, u in enumerate(units):
                                K.pe(lambda e, j=j, u=u, b2=b2: e.matmul(b2[:, sl(j)], lhsT=Qb[u][:, :], rhs=rb_[u][:, :], start=True, stop=True), r=[Qb[u], rb_[u]], p=[b2])
                            for j, u in enumerate(units):
                                K.v(lambda e, j=j, u=u, b2=b2: e.tensor_scalar(out=vnb[u][rows, :], in0=b2[rows, sl(j)], scalar1=bb_[rows, u % 8:u % 8 + 1], scalar2=None, op0=ALU.mult),
                                    r=[b2, bb_], w=[vnb[u]])
                                K.a(lambda e, u=u: e.activation(out=vnsb[u][rows, :], in_=vnb[u][rows, :], func=AF.Copy, scale=gp_[rows, 7, u % 8:u % 8 + 1]),
                                    r=[vnb[u], gp_], w=[vnsb[u]])
                        elif stage == 2:
                            bq = psb()
                            for j, u in enumerate(units):
                                K.pe(lambda e, j=j, u=u, bq=bq: e.matmul(bq[:, sl(j)], lhsT=qT_[:, u % 8, :], rhs=Sb[u][:, :], start=True, stop=True), r=[qT_, Sb[u]], p=[bq])
                            ba = psb()
                            for j, u in enumerate(units):
                                K.pe(lambda e, j=j, u=u, ba=ba: e.matmul(ba[:, sl(j)], lhsT=ATb[u][:, :], rhs=vnb[u][:, :], start=True, stop=True), r=[ATb[u], vnb[u]], p=[ba])
                            for j, u in enumerate(units):
                                K.a(lambda e, j=j, ba=ba: e.copy(out=tmpb[j][rows, :], in_=ba[rows, sl(j)]), r=[ba], w=[tmpb[j]])
                            for j, u in enumerate(units):
                                K.v(lambda e, j=j, u=u, bq=bq: e.scalar_tensor_tensor(
                                    out=o_[rows, hsl(u)], in0=bq[rows, sl(j)], scalar=gp_[rows, 1, u % 8:u % 8 + 1], in1=tmpb[j][rows, :], op0=ALU.mult, op1=ALU.add),
                                    r=[bq, tmpb[j], gp_], p=[o_])
                        else:
                            bs = psb()
                            for j, u in enumerate(units):
                                K.pe(lambda e, j=j, u=u, bs=bs: e.matmul(bs[:, sl(j)], lhsT=kt_[rows, hsl(u)], rhs=vnsb[u][rows, :], start=True, stop=True),
                                     r=[kt_, vnsb[u]], p=[bs])
                            for j, u in enumerate(units):
                                K.v(lambda e, j=j, u=u, bs=bs: e.scalar_tensor_tensor(out=S32[u][:, :], in0=S32[u][:, :], scalar=gp_[:, 4 + c, u % 8:u % 8 + 1], in1=bs[:, sl(j)],
                                                                                  op0=ALU.mult, op1=ALU.add), r=[bs, gp_, S32[u]], w=[S32[u]])
                                K.a(lambda e, u=u: e.copy(out=Sb[u][:, :], in_=S32[u][:, :]), r=[S32[u]], w=[Sb[u]])
            for d in range(2):
                qT_, kT_, kt_, vt_, gg_, bb_, gp_, o_, t0 = cur[d]
                dst = sc["OF"] if d == 0 else sc["OB"]
                K.store(dst, dst.t[t0:t0 + 128, :], o_, o_[:, :])
    if upto <= 3:
        return finish(K, S, yout, seqs)
    K.new_phase()
    bm = K.sb([128, 48, 256], F32, "bm")
    K.load(bm, bm[:, :, :], d_bm, d_bm.t[:, :, :, :, :].rearrange("p g h v k -> p (g h v) k"))
    dq = [K.sb([128, 256], BF16, f"dq{i}") for i in range(4)]
    dk = [K.sb([128, 2, 256], BF16, f"dk{i}") for i in range(4)]
    dv = [K.sb([128, 2, 256], BF16, f"dv{i}") for i in range(4)]
    dqT = [K.sb([128, 2, 128], BF16, f"dqT{i}") for i in range(4)]
    dkT = [K.sb([128, 2, 256], BF16, f"dkT{i}") for i in range(4)]
    ds_ = [K.sb([128, 256], F32, f"ds{i}") for i in range(4)]
    de_ = [K.sb([128, 256], BF16, f"de{i}") for i in range(4)]
    deT = [K.sb([128, 2, 128], BF16, f"deT{i}") for i in range(4)]
    dst_ = [K.sb([128, 16], F32, f"dst{i}") for i in range(4)]
    dus = [K.sb([128, 256], F32, f"dus{i}") for i in range(4)]
    i4 = 0
    i5 = 0
    for nm, L in seqs:
        sc = S[nm]
        QBd = sc["QB"]
        for g, dil in enumerate(DILS):
            Ls = L // dil
            ntile = Ls // 128
            cq = g * 768
            for r in range(dil):
                for j in range(ntile):
                    q_, k2, v2, qT2, kT2_, st, us = dq[i4 % 4], dk[i4 % 4], dv[i4 % 4], dqT[i4 % 4], dkT[i4 % 4], dst_[i4 % 4], dus[i4 % 4]
                    i4 += 1
                    m0 = j * 128
                    qrow0 = 1024 + m0 * dil + r
                    krow0 = 1024 + (m0 - 64) * dil + r
                    span = 127 * dil + 1
                    K.load(q_, q_[:, :], QBd, QBd.t[qrow0:qrow0 + span:dil, cq:cq + 256])
                    for c in range(2):
                        kr = krow0 + c * 128 * dil
                        K.load(k2, k2[:, c, :], QBd, QBd.t[kr:kr + span:dil, cq + 256:cq + 512], part=True)
                        K.load(v2, v2[:, c, :], QBd, QBd.t[kr:kr + span:dil, cq + 512:cq + 768], part=True)
                    tq = pst()
                    for hp in range(2):
                        K.pe(lambda e: e.transpose(out=tq[:, hp * 128:(hp + 1) * 128], in_=q_[:, hp * 128:(hp + 1) * 128], identity=identb[:, :]),
                             r=[q_, identb], p=[tq])
                    evac(qT2[:, :, :], tq[:, 0:256].rearrange("p (a b) -> p a b", a=2), [tq], [qT2])
                    tk = pst()
                    for hp in range(2):
                        for c in range(2):
                            K.pe(lambda e: e.transpose(out=tk[:, (hp * 2 + c) * 128:(hp * 2 + c + 1) * 128], in_=k2[:, c, hp * 128:(hp + 1) * 128], identity=identb[:, :]),
                                 r=[k2, identb], p=[tk])
                    evac(kT2_[:, :, :], tk[:, 0:512].rearrange("p (a b) -> p a b", a=2), [tk], [kT2_])
                    var = (1 if j == 0 else 0) | (2 if j == ntile - 1 else 0)
                    up = psb()
                    for h in range(4):
                        s_, e_, eT_ = ds_[i5 % 4], de_[i5 % 4], deT[i5 % 4]
                        i5 += 1
                        hp = h // 2
                        po = (h % 2) * 64
                        sp = psb()
                        K.pe(lambda e: e.matmul(sp[:, 0:256], lhsT=qT2[po:po + 64, hp, :], rhs=kT2_[po:po + 64, hp, :], start=True, stop=True),
                             r=[qT2, kT2_], p=[sp])
                        bi = g * 16 + h * 4 + var
                        K.v(lambda e: e.scalar_tensor_tensor(out=s_[:, :], in0=sp[:, 0:256], scalar=0.125, in1=bm[:, bi, :], op0=ALU.mult, op1=ALU.add),
                            r=[sp, bm], w=[s_])
                        K.v(lambda e: e.reduce_max(out=st[:, h:h + 1], in_=s_[:, :], axis=AX.X), r=[s_], p=[st])
                        K.v(lambda e: e.tensor_scalar(out=st[:, 8 + h:9 + h], in0=st[:, h:h + 1], scalar1=-1.0, scalar2=None, op0=ALU.mult), r=[st], p=[st])
                        K.a(lambda e: e.activation(out=e_[:, :], in_=s_[:, :], func=AF.Exp, bias=st[:, 8 + h:9 + h], accum_out=st[:, 4 + h:5 + h]),
                            r=[s_, st], w=[e_], p=[st])
                        te = pst()
                        for c in range(2):
                            K.pe(lambda e: e.transpose(out=te[:, c * 128:(c + 1) * 128], in_=e_[:, c * 128:(c + 1) * 128], identity=identb[:, :]),
                                 r=[e_, identb], p=[te])
                        evac(eT_[:, :, :], te[:, 0:256].rearrange("p (a b) -> p a b", a=2), [te], [eT_])
                        for c in range(2):
                            K.pe(lambda e: e.matmul(up[:, h * 64:(h + 1) * 64], lhsT=eT_[:, c, :], rhs=v2[:, c, h * 64:(h + 1) * 64], start=(c == 0), stop=(c == 1)),
                                 r=[eT_, v2], p=[up])
                    evac(us[:, :], up[:, 0:256], [up], [us])
                    trow = m0 * dil + r
                    K.store(sc["U"][g], sc["U"][g].t[trow:trow + span:dil, :], us, us[:, :])
                    K.store(sc["ST"][g], sc["ST"][g].t[trow:trow + span:dil, :], st, st[:, 0:8])
    if upto <= 4:
        return finish(K, S, yout, seqs)

    K.new_phase()
    w_ckv = load_w("w_ckv", D, 2 * D)
    n_mem = load_norm(2)
    mx_ = K.sb([128, D], F32, "mx")
    mh_ = K.sb([128, D], BF16, "mh")
    msq = K.sb([128, D], F32, "msq")
    mss = K.sb([128, 2], F32, "mss")
    mT1 = K.sb([128, 8, 128], BF16, "mT1")
    memT = K.sb([128, 8, 256], BF16, "memT")
    kmT_s = K.sb([128, 8, 256], BF16, "kmT_s")
    vm_s = K.sb([128, 2, D], BF16, "vm_s")
    for nm, L in seqs:
        sc = S[nm]
        for mt in range(2):
            K.load(mx_, mx_[:, :], memin[nm], memin[nm].t[mt * 128:(mt + 1) * 128, :])
            rmsnorm_T(mx_, n_mem, mh_, mT1, msq, mss)
            K.g(lambda e: e.tensor_copy(out=memT[:, :, mt * 128:(mt + 1) * 128], in_=mT1[:, :, :]), r=[mT1], p=[memT])
        for ft in range(8):
            p = psb()
            for kc in range(8):
                K.pe(lambda e: e.matmul(p[:, 0:256], lhsT=w_ckv[:, kc, ft * 128:(ft + 1) * 128], rhs=memT[:, kc, :], start=(kc == 0), stop=(kc == 7)),
                     r=[w_ckv, memT], p=[p])
            evac(kmT_s[:, ft, :], p[:, 0:256], [p], [kmT_s], part=True)
        for mt in range(2):
            for n in range(2):
                p = psb()
                for kc in range(8):
                    K.pe(lambda e: e.matmul(p[:, :], lhsT=memT[:, kc, mt * 128:(mt + 1) * 128], rhs=w_ckv[:, kc, D + n * 512:D + (n + 1) * 512],
                                            start=(kc == 0), stop=(kc == 7)), r=[w_ckv, memT], p=[p])
                evac(vm_s[:, mt, n * 512:(n + 1) * 512], p[:, :], [p], [vm_s], part=True)
        K.store(sc["KMT"], sc["KMT"].t[:, :, :], kmT_s, kmT_s[:, :, :])
        K.store(sc["VM"], sc["VM"].t[:, :, :], vm_s, vm_s[:, :, :])

    K.new_phase()
    w_gate = load_w("w_gate", D, 2 * D)
    w_pa = load_w("w_pa", D, D)
    w_pb = load_w("w_pb", 256, D)
    w_o = load_w("w_o", D, D)
    w_cq = load_w("w_cq", D, D)
    w_co = load_w("w_co", D, D)
    n_mix = load_norm(0)
    n_cross = load_norm(1)
    n_gdn = load_norm(5)
    kmT = K.sb([128, 8, 256], BF16, "kmT")
    vm = K.sb([128, 2, D], BF16, "vm")
    ex = [K.sb([128, D], F32, f"ex{i}") for i in range(2)]
    eh = K.sb([128, D], BF16, "eh")
    ehT = K.sb([128, 8, 128], BF16, "ehT")
    esq = K.sb([128, D], F32, "esq")
    ess = K.sb([128, 2], F32, "ess")
    gates = K.sb([128, 2 * D], BF16, "gates")
    eof = K.sb([128, D], F32, "eof")
    eob = K.sb([128, D], F32, "eob")
    esz = K.sb([128, D], BF16, "esz")
    egs = K.sb([128, 24], F32, "egs")
    eoa = K.sb([128, D], BF16, "eoa")
    eoaT = K.sb([128, 8, 128], BF16, "eoaT")
    eU = [K.sb([128, 256], F32, f"eU{g}") for g in range(3)]
    eST = [K.sb([128, 8], F32, f"eST{g}") for g in range(3)]
    emg = K.sb([128, 40], F32, "emg")
    eacc = K.sb([128, 256], F32, "eacc")
    eobm = K.sb([128, 256], BF16, "eobm")
    eobT = K.sb([128, 2, 128], BF16, "eobT")
    emix = K.sb([128, D], F32, "emix")
    emixb = K.sb([128, D], BF16, "emixb")
    emixT = K.sb([128, 8, 128], BF16, "emixT")
    ex1 = K.sb([128, D], F32, "ex1")
    eh2 = K.sb([128, D], BF16, "eh2")
    eh2T = K.sb([128, 8, 128], BF16, "eh2T")
    eqc = K.sb([128, 8, 128], BF16, "eqc")
    ecs = K.sb([128, 16], F32, "ecs")
    ece = [K.sb([128, 256], BF16, f"ece{i}") for i in range(2)]
    eceT = [K.sb([128, 2, 128], BF16, f"eceT{i}") for i in range(2)]
    eoc = K.sb([128, D], BF16, "eoc")
    eocT = K.sb([128, 8, 128], BF16, "eocT")
    ex2 = [K.sb([128, D], F32, f"ex2_{i}") for i in range(2)]
    i6 = 0
    for nm, L in seqs:
        sc = S[nm]
        K.load(kmT, kmT[:, :, :], sc["KMT"], sc["KMT"].t[:, :, :])
        K.load(vm, vm[:, :, :], sc["VM"], sc["VM"].t[:, :, :])
        for tile in range(L // 128):
            t0 = tile * 128
            xt = ex[i6 % 2]
            x2t = ex2[i6 % 2]
            i6 += 1
            K.load(xt, xt[:, :], xin[nm], xin[nm].t[t0:t0 + 128, :])
            K.load(eof, eof[:, :], sc["OF"], sc["OF"].t[t0:t0 + 128, :])
            K.load(eob, eob[:, :], sc["OB"], sc["OB"].t[t0:t0 + 128, :])
            K.load(esz, esz[:, :], sc["SZ"], sc["SZ"].t[t0:t0 + 128, :])
            for g in range(3):
                K.load(eU[g], eU[g][:, :], sc["U"][g], sc["U"][g].t[t0:t0 + 128, :])
                K.load(eST[g], eST[g][:, :], sc["ST"][g], sc["ST"][g].t[t0:t0 + 128, :])
            rmsnorm_T(xt, n_mix, eh, ehT, esq, ess)
            for n in range(4):
                p = psb()
                for kc in range(8):
                    K.pe(lambda e: e.matmul(p[:, :], lhsT=ehT[:, kc, :], rhs=w_gate[:, kc, n * 512:(n + 1) * 512], start=(kc == 0), stop=(kc == 7)),
                         r=[ehT, w_gate], p=[p])
                K.a(lambda e: e.activation(out=gates[:, n * 512:(n + 1) * 512], in_=p[:, :], func=AF.Sigmoid), r=[p], p=[gates])
            K.v(lambda e: e.tensor_tensor(out=eof[:, :], in0=eof[:, :], in1=eob[:, :], op=ALU.add), r=[eof, eob], w=[eof])
            K.g(lambda e: e.tensor_tensor(out=eob[:, :], in0=eof[:, :], in1=eof[:, :], op=ALU.mult), r=[eof], w=[eob])
            K.v(lambda e: e.reduce_sum(out=egs[:, 0:8], in_=eob[:, :].rearrange("p (h d) -> p h d", h=8), axis=AX.X), r=[eob], w=[egs])
            K.v(lambda e: e.tensor_scalar(out=egs[:, 8:16], in0=egs[:, 0:8], scalar1=1.0 / 128, scalar2=EPS, op0=ALU.mult, op1=ALU.add), r=[egs], w=[egs])
            K.a(lambda e: e.sqrt(out=egs[:, 8:16], in_=egs[:, 8:16]), r=[egs], w=[egs])
            K.v(lambda e: e.reciprocal(out=egs[:, 16:24], in_=egs[:, 8:16]), r=[egs], w=[egs])
            for h in range(8):
                hs = slice(h * 128, (h + 1) * 128)
                K.v(lambda e: e.scalar_tensor_tensor(out=eof[:, hs], in0=eof[:, hs], scalar=egs[:, 16 + h:17 + h], in1=n_gdn[:, hs], op0=ALU.mult, op1=ALU.mult),
                    r=[eof, egs, n_gdn], w=[eof])
            K.v(lambda e: e.tensor_tensor(out=eoa[:, :], in0=eof[:, :], in1=esz[:, :], op=ALU.mult), r=[eof, esz], w=[eoa])
            transpose_to(eoa, eoaT, 8)
            K.v(lambda e: e.tensor_tensor(out=emg[:, 0:4], in0=eST[0][:, 0:4], in1=eST[1][:, 0:4], op=ALU.max), r=[eST[0], eST[1]], w=[emg])
            K.v(lambda e: e.tensor_tensor(out=emg[:, 0:4], in0=emg[:, 0:4], in1=eST[2][:, 0:4], op=ALU.max), r=[emg, eST[2]], w=[emg])
            for g in range(3):
                K.v(lambda e: e.tensor_tensor(out=emg[:, 4 + 4 * g:8 + 4 * g], in0=eST[g][:, 0:4], in1=emg[:, 0:4], op=ALU.subtract), r=[eST[g], emg], w=[emg])
            K.a(lambda e: e.activation(out=emg[:, 4:16], in_=emg[:, 4:16], func=AF.Exp), r=[emg], w=[emg])
            K.v(lambda e: e.tensor_tensor(out=emg[:, 16:20], in0=emg[:, 4:8], in1=eST[0][:, 4:8], op=ALU.mult), r=[emg, eST[0]], w=[emg])
            for g in (1, 2):
                K.v(lambda e: e.tensor_tensor(out=emg[:, 36:40], in0=emg[:, 4 + 4 * g:8 + 4 * g], in1=eST[g][:, 4:8], op=ALU.mult), r=[emg, eST[g]], w=[emg])
                K.v(lambda e: e.tensor_tensor(out=emg[:, 16:20], in0=emg[:, 16:20], in1=emg[:, 36:40], op=ALU.add), r=[emg], w=[emg])
            K.v(lambda e: e.reciprocal(out=emg[:, 20:24], in_=emg[:, 16:20]), r=[emg], w=[emg])
            for g in range(3):
                K.v(lambda e: e.tensor_tensor(out=emg[:, 24 + 4 * g:28 + 4 * g], in0=emg[:, 4 + 4 * g:8 + 4 * g], in1=emg[:, 20:24], op=ALU.mult), r=[emg], w=[emg])
            for h in range(4):
                hs = slice(h * 64, (h + 1) * 64)
                K.v(lambda e: e.tensor_scalar(out=eacc[:, hs], in0=eU[0][:, hs], scalar1=emg[:, 24 + h:25 + h], scalar2=None, op0=ALU.mult), r=[eU[0], emg], w=[eacc])
                for g in (1, 2):
                    K.v(lambda e: e.scalar_tensor_tensor(out=eacc[:, hs], in0=eU[g][:, hs], scalar=emg[:, 24 + 4 * g + h:25 + 4 * g + h], in1=eacc[:, hs],
                                                         op0=ALU.mult, op1=ALU.add), r=[eU[g], emg, eacc], w=[eacc])
            K.v(lambda e: e.tensor_copy(out=eobm[:, :], in_=eacc[:, :]), r=[eacc], w=[eobm])
            transpose_to(eobm, eobT, 2)
            for n in range(2):
                ns = slice(n * 512, (n + 1) * 512)
                p = psb()
                for kc in range(8):
                    K.pe(lambda e: e.matmul(p[:, :], lhsT=eoaT[:, kc, :], rhs=w_pa[:, kc, ns], start=(kc == 0), stop=(kc == 7)), r=[eoaT, w_pa], p=[p])
                K.v(lambda e: e.tensor_tensor(out=emix[:, ns], in0=p[:, :], in1=gates[:, ns], op=ALU.mult), r=[p, gates], p=[emix])
                p2 = psb()
                for kc in range(2):
                    K.pe(lambda e: e.matmul(p2[:, :], lhsT=eobT[:, kc, :], rhs=w_pb[:, kc, ns], start=(kc == 0), stop=(kc == 1)), r=[eobT, w_pb], p=[p2])
                K.v(lambda e: e.tensor_tensor(out=esq[:, ns], in0=p2[:, :], in1=gates[:, D + n * 512:D + (n + 1) * 512], op=ALU.mult), r=[p2, gates], p=[esq])
            K.v(lambda e: e.tensor_tensor(out=emixb[:, :], in0=emix[:, :], in1=esq[:, :], op=ALU.add), r=[emix, esq], w=[emixb])
            transpose_to(emixb, emixT, 8)
            for n in range(2):
                ns = slice(n * 512, (n + 1) * 512)
                p = psb()
                for kc in range(8):
                    K.pe(lambda e: e.matmul(p[:, :], lhsT=emixT[:, kc, :], rhs=w_o[:, kc, ns], start=(kc == 0), stop=(kc == 7)), r=[emixT, w_o], p=[p])
                K.v(lambda e: e.tensor_tensor(out=ex1[:, ns], in0=p[:, :], in1=xt[:, ns], op=ALU.add), r=[p, xt], p=[ex1])
            rmsnorm_T(ex1, n_cross, eh2, eh2T, esq, ess)
            for half in range(2):
                p = psb()
                for f4 in range(4):
                    ft = half * 4 + f4
                    for kc in range(8):
                        K.pe(lambda e: e.matmul(p[:, f4 * 128:(f4 + 1) * 128], lhsT=w_cq[:, kc, ft * 128:(ft + 1) * 128], rhs=eh2T[:, kc, :], start=(kc == 0), stop=(kc == 7)),
                             r=[w_cq, eh2T], p=[p])
                evac(eqc[:, half * 4:(half + 1) * 4, :], p[:, :].rearrange("p (a b) -> p a b", a=4), [p], [eqc], part=True)
            po_ = psb()
            po2_ = psb()
            for hh in range(4):
                ce, ceT = ece[hh % 2], eceT[hh % 2]
                sp = psb()
                for c in range(2):
                    K.pe(lambda e: e.matmul(sp[:, 0:256], lhsT=eqc[:, 2 * hh + c, :], rhs=kmT[:, 2 * hh + c, :], start=(c == 0), stop=(c == 1)), r=[eqc, kmT], p=[sp])
                K.v(lambda e: e.reduce_max(out=ecs[:, hh:hh + 1], in_=sp[:, 0:256], axis=AX.X), r=[sp], p=[ecs])
                K.v(lambda e: e.tensor_scalar(out=ecs[:, 4 + hh:5 + hh], in0=ecs[:, hh:hh + 1], scalar1=-1.0 / 16, scalar2=None, op0=ALU.mult), r=[ecs], p=[ecs])
                K.a(lambda e: e.activation(out=ce[:, :], in_=sp[:, 0:256], func=AF.Exp, scale=1.0 / 16, bias=ecs[:, 4 + hh:5 + hh], accum_out=ecs[:, 8 + hh:9 + hh]),
                    r=[sp, ecs], w=[ce], p=[ecs])
                te = pst()
                for c in range(2):
                    K.pe(lambda e: e.transpose(out=te[:, c * 128:(c + 1) * 128], in_=ce[:, c * 128:(c + 1) * 128], identity=identb[:, :]), r=[ce, identb], p=[te])
                evac(ceT[:, :, :], te[:, 0:256].rearrange("p (a b) -> p a b", a=2), [te], [ceT])
                pv = po_ if hh < 2 else po2_
                for c in range(2):
                    K.pe(lambda e: e.matmul(pv[:, (hh % 2) * 256:(hh % 2 + 1) * 256], lhsT=ceT[:, c, :], rhs=vm[:, c, hh * 256:(hh + 1) * 256], start=(c == 0), stop=(c == 1)),
                         r=[ceT, vm], p=[pv])
            K.v(lambda e: e.reciprocal(out=ecs[:, 12:16], in_=ecs[:, 8:12]), r=[ecs], w=[ecs])
            for hh in range(4):
                pv = po_ if hh < 2 else po2_
                K.v(lambda e: e.tensor_scalar(out=eoc[:, hh * 256:(hh + 1) * 256], in0=pv[:, (hh % 2) * 256:(hh % 2 + 1) * 256], scalar1=ecs[:, 12 + hh:13 + hh], scalar2=None, op0=ALU.mult),
                    r=[pv, ecs], p=[eoc])
            transpose_to(eoc, eocT, 8)
            for n in range(2):
                ns = slice(n * 512, (n + 1) * 512)
                p = psb()
                for kc in range(8):
                    K.pe(lambda e: e.matmul(p[:, :], lhsT=eocT[:, kc, :], rhs=w_co[:, kc, ns], start=(kc == 0), stop=(kc == 7)), r=[eocT, w_co], p=[p])
                K.v(lambda e: e.tensor_tensor(out=x2t[:, ns], in0=p[:, :], in1=ex1[:, ns], op=ALU.add), r=[p, ex1], p=[x2t])
            K.store(sc["X2"], sc["X2"].t[t0:t0 + 128, :], x2t, x2t[:, :])
    if upto <= 5:
        return finish(K, S, yout, seqs)

    K.new_phase()
    w_ff1 = load_w("w_ff1", D, D_FF)
    w_ff3 = load_w("w_ff3", D, D_FF)
    w_ff2 = load_w("w_ff2", D_FF, D)
    n_ffn = load_norm(3)
    n_fin = load_norm(4)
    fx = [K.sb([128, D], F32, f"fx{i}") for i in range(2)]
    fh = K.sb([128, D], BF16, "fh")
    fhT = K.sb([128, 8, 128], BF16, "fhT")
    fsq = K.sb([128, D], F32, "fsq")
    fss = K.sb([128, 2], F32, "fss")
    fs1 = [K.sb([128, 512], F32, f"fs1_{i}") for i in range(2)]
    fhid = K.sb([128, D_FF], BF16, "fhid")
    fhidT = K.sb([128, 22, 128], BF16, "fhidT")
    fx3 = K.sb([128, D], F32, "fx3")
    fy = [K.sb([128, D], F32, f"fy{i}") for i in range(2)]
    i7 = 0
    for nm, L in seqs:
        sc = S[nm]
        for tile in range(L // 128):
            t0 = tile * 128
            xt = fx[i7 % 2]
            yt = fy[i7 % 2]
            i7 += 1
            K.load(xt, xt[:, :], sc["X2"], sc["X2"].t[t0:t0 + 128, :])
            rmsnorm_T(xt, n_ffn, fh, fhT, fsq, fss)
            for n in range(6):
                cw_ = 512 if n < 5 else 256
                ns = slice(n * 512, n * 512 + cw_)
                s1 = fs1[n % 2]
                p1 = psb()
                for kc in range(8):
                    K.pe(lambda e: e.matmul(p1[:, 0:cw_], lhsT=fhT[:, kc, :], rhs=w_ff1[:, kc, ns], start=(kc == 0), stop=(kc == 7)), r=[fhT, w_ff1], p=[p1])
                p3 = psb()
                for kc in range(8):
                    K.pe(lambda e: e.matmul(p3[:, 0:cw_], lhsT=fhT[:, kc, :], rhs=w_ff3[:, kc, ns], start=(kc == 0), stop=(kc == 7)), r=[fhT, w_ff3], p=[p3])
                K.a(lambda e: e.activation(out=s1[:, 0:cw_], in_=p1[:, 0:cw_], func=AF.Silu), r=[p1], w=[s1])
                K.v(lambda e: e.tensor_tensor(out=fhid[:, ns], in0=p3[:, 0:cw_], in1=s1[:, 0:cw_], op=ALU.mult), r=[p3, s1], p=[fhid])
            transpose_to(fhid, fhidT, 22)
            for n in range(2):
                ns = slice(n * 512, (n + 1) * 512)
                p = psb()
                for kc in range(22):
                    K.pe(lambda e: e.matmul(p[:, :], lhsT=fhidT[:, kc, :], rhs=w_ff2[:, kc, ns], start=(kc == 0), stop=(kc == 21)), r=[fhidT, w_ff2], p=[p])
                K.v(lambda e: e.tensor_tensor(out=fx3[:, ns], in0=p[:, :], in1=xt[:, ns], op=ALU.add), r=[p, xt], p=[fx3])
            K.a(lambda e: e.activation(out=fsq[:, :], in_=fx3[:, :], func=AF.Square, accum_out=fss[:, 0:1]), r=[fx3], w=[fsq, fss])
            K.v(lambda e: e.tensor_scalar(out=fss[:, 1:2], in0=fss[:, 0:1], scalar1=1.0 / D, scalar2=EPS, op0=ALU.mult, op1=ALU.add), r=[fss], w=[fss])
            K.a(lambda e: e.sqrt(out=fss[:, 1:2], in_=fss[:, 1:2]), r=[fss], w=[fss])
            K.v(lambda e: e.reciprocal(out=fss[:, 1:2], in_=fss[:, 1:2]), r=[fss], w=[fss])
            K.v(lambda e: e.scalar_tensor_tensor(out=yt[:, :], in0=fx3[:, :], scalar=fss[:, 1:2], in1=n_fin[:, :], op0=ALU.mult, op1=ALU.mult),
                r=[fx3, fss, n_fin], w=[yt])
            K.store(yout[nm], yout[nm].t[t0:t0 + 128, :], yt, yt[:, :])
    return finish(K, S, yout, seqs)


LA, LB = 16384, 2048


def kernel(**inputs):
    inp = {k: np.asarray(v) for k, v in inputs.items()}
    nc = build_program([("A", LA), ("B", LB)])
    cm = common_inputs(inp)
    zx = np.zeros((LA, D), np.float32)
    zm = np.zeros((NMEM, D), np.float32)
    maps = []
    for c in range(8):
        m = dict(cm)
        m["xA"] = np.ascontiguousarray(inp["x_prompt"][c]) if c < 2 else zx
        m["memA"] = np.ascontiguousarray(inp["mem_prompt"][c]) if c < 2 else zm
        m["xB"] = np.ascontiguousarray(inp["x_sample"][c])
        m["memB"] = np.ascontiguousarray(inp["mem_sample"][c])
        maps.append(m)
    res = run_bass_kernel_spmd(nc, maps, core_ids=list(range(8)))
    y_prompt = np.stack([np.asarray(res.results[c]["yA"], dtype=np.float32) for c in range(2)])
    y_sample = np.stack([np.asarray(res.results[c]["yB"], dtype=np.float32) for c in range(8)])
    return y_prompt, y_sample
```

```python
import math
import numpy as np
import concourse.bass as bass
import concourse.mybir as mybir
from concourse.bass_utils import run_bass_kernel_spmd

F32 = mybir.dt.float32
BF16 = mybir.dt.bfloat16
U8 = mybir.dt.uint8
ALU = mybir.AluOpType
AF = mybir.ActivationFunctionType
AX = mybir.AxisListType

D = 1024
IN_COLS = 6432
D_FF = 2816
NMEM = 256
EPS = 1e-6


import types as _types
import os as _os_env


def _freeze(fn):
    if fn.__closure__ is None:
        return fn
    cells = []
    for c in fn.__closure__:
        try:
            cells.append(_types.CellType(c.cell_contents))
        except ValueError:
            cells.append(c)
    return _types.FunctionType(fn.__code__, fn.__globals__, fn.__name__, fn.__defaults__, tuple(cells))


class Counter:
    def __init__(self, K, name, step, limit=24000):
        self.K, self.name, self.step, self.limit = K, name, step, limit
        self.sems = []
        self.val = 0
        self._new()

    def _new(self):
        if self.sems:
            self.K.closed.append((self.sems[-1], self.val))
        self.sems.append(self.K.nc.alloc_semaphore(name=f"{self.name}_{len(self.sems)}"))
        self.val = 0

    def next_event(self):
        if self.val + self.step > self.limit:
            self._new()
        self.val += self.step
        return (self.sems[-1], self.val)

    def last(self):
        return (self.sems[-1], self.val)


class Buf:
    def __init__(self, K, t, name):
        self.K, self.t, self.name = K, t, name
        self.w = {}
        self.r = {}
        self.pr = {}
        self.excl = False
        self._dmac = None

    def __getitem__(self, key):
        return self.t[key]

    @property
    def dmac(self):
        if self._dmac is None:
            self._dmac = self.K.get_dmac()
        return self._dmac


class EngState:
    def __init__(self, K, name):
        self.name = name
        self.counter = Counter(K, "e_" + name, 1)
        self.seen = {}
        self.prog = []


class Kern:
    ENGS = ("tensor", "vector", "scalar", "gpsimd", "sync")

    def __init__(self, nc, arena_bytes=210000):
        self.nc = nc
        self.closed = []
        self.eng = {n: EngState(self, n) for n in self.ENGS}
        self.nbuf = 0
        self.n_inst = 0
        self.arena = nc.alloc_sbuf_tensor("arena", [128, arena_bytes], U8)
        self.arena_bytes = arena_bytes
        self.off = 0
        self.base = 0
        self.dmacs = []
        self.dmac_free = []
        self.phase_bufs = []

    def get_dmac(self):
        if self.dmac_free:
            return self.dmac_free.pop()
        c = Counter(self, f"d{len(self.dmacs)}", 16)
        self.dmacs.append(c)
        return c

    def sb(self, shape, dtype=F32, name=None):
        self.nbuf += 1
        name = name or f"sb{self.nbuf}"
        esz = {F32: 4, BF16: 2, U8: 1}[dtype]
        n = int(np.prod(shape[1:]))
        nbytes = n * esz
        off = (self.off + 31) // 32 * 32
        assert off + nbytes <= self.arena_bytes, f"SBUF arena overflow at {name}: {off + nbytes}"
        self.off = off + nbytes
        v = self.arena[0:shape[0], off:off + nbytes]
        if dtype != U8:
            v = v.bitcast(dtype)
        if len(shape) == 3:
            v = v.rearrange("p (a b) -> p a b", a=shape[1])
        elif len(shape) == 4:
            v = v.rearrange("p (a b c) -> p a b c", a=shape[1], b=shape[2])
        b = Buf(self, v, name)
        self.phase_bufs.append(b)
        return b

    def ps(self, shape, dtype=F32, name=None):
        self.nbuf += 1
        name = name or f"ps{self.nbuf}"
        b = Buf(self, self.nc.alloc_psum_tensor(name, list(shape), dtype), name)
        b.excl = True
        return b

    def dram(self, name, shape, dtype, kind="Internal"):
        return Buf(self, self.nc.dram_tensor(name, list(shape), dtype, kind=kind), name)

    def _deps(self, reads, writes, parts):
        deps = {}

        def merge(d):
            for s, v in d.items():
                if deps.get(s, 0) < v:
                    deps[s] = v

        for b in reads:
            merge(b.w)
            if b.excl:
                merge(b.r)
        for b in writes:
            merge(b.w)
            merge(b.r)
            merge(b.pr)
        for b in parts:
            merge(b.r)
            merge(b.pr)
        return deps

    def _emit_waits(self, E, deps):
        own = E.counter.sems
        for s, v in deps.items():
            if s in own and (E.name == "tensor" or _os_env.environ.get("NOSAME")):
                continue
            if E.seen.get(s, 0) < v:
                E.prog.append(("wait", s, v))
                E.seen[s] = v

    def _commit(self, ev, reads, writes, parts):
        s, v = ev
        for b in reads:
            if b.r.get(s, 0) < v:
                b.r[s] = v
        for b in writes:
            b.w = {s: v}
            b.r = {}
            b.pr = {}
        for b in parts:
            if b.r:
                b.pr = dict(b.w)
                for s2, v2 in b.r.items():
                    if b.pr.get(s2, 0) < v2:
                        b.pr[s2] = v2
                b.w = {}
                b.r = {}
            if b.w.get(s, 0) < v:
                b.w[s] = v

    def op(self, eng, fn, r=(), w=(), p=()):
        E = self.eng[eng]
        self._emit_waits(E, self._deps(r, w, p))
        ev = E.counter.next_event()
        E.prog.append(("inst", _freeze(fn), ev, 1))
        self._commit(ev, r, w, p)
        self.n_inst += 1

    def v(self, fn, **kw):
        self.op("vector", fn, **kw)

    def a(self, fn, **kw):
        self.op("scalar", fn, **kw)

    def g(self, fn, **kw):
        self.op("gpsimd", fn, **kw)

    def pe(self, fn, **kw):
        self.op("tensor", fn, **kw)

    def dma(self, q, out, in_, r=(), w=(), p=(), cbuf=None):
        E = self.eng[q]
        self._emit_waits(E, self._deps(r, w, p))
        ev = cbuf.dmac.next_event()
        E.prog.append(("inst", lambda e: e.dma_start(out=out, in_=in_), ev, 16))
        self._commit(ev, r, w, p)
        self.n_inst += 1

    def load(self, sbuf, sb_ap, dram, dr_ap, part=False):
        if part:
            self.dma("sync", sb_ap, dr_ap, r=[dram], p=[sbuf], cbuf=sbuf)
        else:
            self.dma("sync", sb_ap, dr_ap, r=[dram], w=[sbuf], cbuf=sbuf)

    def store(self, dram, dr_ap, sbuf, sb_ap):
        self.dma("gpsimd", dr_ap, sb_ap, r=[sbuf], p=[dram], cbuf=sbuf)

    def barrier(self, extra=()):
        deps = {}
        for E in self.eng.values():
            s, v = E.counter.last()
            if v > 0:
                deps[s] = v
        for c in self.dmacs:
            s, v = c.last()
            if v > 0:
                deps[s] = v
        for s, v in self.closed:
            deps[s] = v
        for E in self.eng.values():
            self._emit_waits(E, deps)

    def new_phase(self):
        self.barrier()
        for b in self.phase_bufs:
            if b._dmac is not None:
                self.dmac_free.append(b._dmac)
                b._dmac = None
        self.phase_bufs = []
        self.off = self.base

    def persist(self):
        self.base = self.off
        self.phase_bufs = []

    def build(self):
        nc = self.nc

        def run(E, e):
            for item in E.prog:
                if item[0] == "wait":
                    e.wait_ge(item[1], item[2])
                else:
                    _, fn, (s, v), step = item
                    fn(e).then_inc(s, step)

        with nc.Block() as block:
            @block.tensor
            def _(e):
                run(self.eng["tensor"], e)

            @block.vector
            def _(e):
                run(self.eng["vector"], e)

            @block.scalar
            def _(e):
                run(self.eng["scalar"], e)

            @block.gpsimd
            def _(e):
                run(self.eng["gpsimd"], e)

            @block.sync
            def _(e):
                run(self.eng["sync"], e)


NCONST = 15
(C_ID, C_TRIF, C_TRIB, C_BLK, C_CS0, C_CS1, C_AF, C_AB, C_MSF, C_MIF, C_MSB, C_MIB, C_ONES, C_X1, C_X2) = range(NCONST)


def make_consts():
    i = np.arange(128)
    a = i[:, None]
    b = i[None, :]
    same = (a // 64) == (b // 64)
    c = np.zeros((NCONST, 128, 128), np.float32)
    c[C_ID] = (a == b)
    c[C_TRIF] = (a <= b) & same
    c[C_TRIB] = (a >= b) & same
    c[C_BLK] = same
    c[C_CS0] = (a < 64) & (b >= 0)
    c[C_CS1] = (a >= 64) & (b >= 0)
    c[C_AF] = (a > b) & same
    c[C_AB] = (a < b) & same
    c[C_MSF] = (b > a) & same
    c[C_MIF] = (b >= a) & same
    c[C_MSB] = (b < a) & same
    c[C_MIB] = (b <= a) & same
    c[C_ONES] = 1.0
    return np.ascontiguousarray(c.transpose(1, 0, 2))


def t5_bucket(rel):
    half = 16
    exact = 8
    n = np.abs(rel)
    large = exact + (np.log(np.maximum(n, 1) / exact) / math.log(1024 / exact) * (half - exact)).astype(np.int32)
    large = np.minimum(large, half - 1)
    return (rel > 0).astype(np.int32) * half + np.where(n < exact, n, large).astype(np.int32)


DILS = (1, 4, 16)


def make_biasmask(rel_bias):
    qi = np.arange(128)[:, None]
    kj = np.arange(256)[None, :]
    rel = kj - 64 - qi
    band = np.abs(rel) <= 64
    out = np.empty((128, 3, 4, 4, 256), np.float32)
    for g, dil in enumerate(DILS):
        bk = t5_bucket(rel * dil)
        for h in range(4):
            vals = rel_bias[bk, g * 4 + h]
            for var in range(4):
                m = band.copy()
                if var & 1:
                    m = m & (kj >= 64)
                if var & 2:
                    m = m & (kj < 192)
                out[:, g, h, var, :] = np.where(m, vals, np.float32(-1e30))
    return out


WSPECS = [("w_in", D, IN_COLS), ("w_gate", D, 2 * D), ("w_pa", D, D), ("w_pb", 256, D), ("w_o", D, D),
          ("w_cq", D, D), ("w_ckv", D, 2 * D), ("w_co", D, D), ("w_ff1", D, D_FF), ("w_ff3", D, D_FF),
          ("w_ff2", D_FF, D)]


def finish(K, S, yout, seqs):
    K.barrier()
    K.build()
    return K.nc


def common_inputs(inp):
    m = {n: np.ascontiguousarray(inp[n][0]) for n, _, _ in WSPECS}
    norms = np.zeros((6, D), np.float32)
    norms[0] = inp["norm_mix"][0]
    norms[1] = inp["norm_cross"][0]
    norms[2] = inp["norm_mem"][0]
    norms[3] = inp["norm_ffn"][0]
    norms[4] = inp["norm_final"]
    norms[5, :] = np.tile(inp["gdn_norm"][0], 8)
    m["norms"] = norms
    m["conv_w"] = np.ascontiguousarray(inp["conv_w"][0].reshape(5, 24, 128).transpose(2, 1, 0))
    m["a_log"] = np.ascontiguousarray(inp["gdn_a_log"][0].reshape(16))
    m["dt_bias"] = np.ascontiguousarray(inp["gdn_dt_bias"][0].reshape(16))
    m["consts"] = make_consts()
    m["biasmask"] = make_biasmask(np.asarray(inp["rel_bias"]))
    return m


def build_program(seq_lens, debug=(), upto=99):
    nc = bass.Bass("TRN2", target_bir_lowering=False)
    K = Kern(nc)
    X = K.op

    def din(name, shape, dt=F32):
        return K.dram(name, shape, dt, kind="ExternalInput")

    def dscr(name, shape, dt):
        return K.dram(name, shape, dt, kind=("ExternalOutput" if name in debug else "Internal"))

    seqs = [(nm, L) for nm, L in seq_lens if L > 0]
    xin = {nm: din("x" + nm, [L, D]) for nm, L in seqs}
    memin = {nm: din("mem" + nm, [NMEM, D]) for nm, L in seqs}
    yout = {nm: K.dram("y" + nm, [L, D], F32, kind="ExternalOutput") for nm, L in seqs}
    wf32 = {n: din(n, [k, c]) for n, k, c in WSPECS}
    wbf = {n: dscr(n + "_bf", [k, c], BF16) for n, k, c in WSPECS}
    d_norms = din("norms", [6, D])
    d_convw = din("conv_w", [128, 24, 5])
    d_alog = din("a_log", [16])
    d_dtb = din("dt_bias", [16])
    d_consts = din("consts", [128, NCONST, 128])
    d_bm = din("biasmask", [128, 3, 4, 4, 256])

    S = {}
    for nm, L in seqs:
        S[nm] = dict(
            PT=dscr("PT" + nm, [3072, L + 4], BF16),
            SZ=dscr("SZ" + nm, [L, D], BF16),
            GG=dscr("GG" + nm, [L, 16], F32),
            BB=dscr("BB" + nm, [L, 16], F32),
            QB=dscr("QB" + nm, [L + 2048, 2304], BF16),
            QT=dscr("QT" + nm, [L // 128, 128, 8, 128], BF16),
            KT=dscr("KT" + nm, [L // 128, 128, 8, 128], BF16),
            KTOK=dscr("KTOK" + nm, [L, D], BF16),
            VTOK=dscr("VTOK" + nm, [L, D], BF16),
            OF=dscr("OF" + nm, [L, D], F32),
            OB=dscr("OB" + nm, [L, D], F32),
            U=[dscr(f"U{g}" + nm, [L, 256], F32) for g in range(3)],
            ST=[dscr(f"ST{g}" + nm, [L, 8], F32) for g in range(3)],
            X2=dscr("X2" + nm, [L, D], F32),
            KMT=dscr("KMT" + nm, [128, 8, 256], BF16),
            VM=dscr("VM" + nm, [128, 2, D], BF16),
        )

    cst = K.sb([128, NCONST, 128], F32, "cst")
    K.load(cst, cst[:, :, :], d_consts, d_consts.t[:, :, :])
    identb = K.sb([128, 128], BF16, "identb")
    onesb = K.sb([128, 128], BF16, "onesb")
    K.v(lambda e: e.tensor_copy(out=identb[:, :], in_=cst[:, C_ID, :]), r=[cst], w=[identb])
    K.v(lambda e: e.tensor_copy(out=onesb[:, :], in_=cst[:, C_ONES, :]), r=[cst], w=[onesb])
    def load_norm(idx):
        nt = K.sb([128, D], F32, f"norm{idx}")
        K.load(nt, nt[:, :], d_norms, d_norms.t[idx, :].partition_broadcast(128))
        return nt
    cstb = K.sb([128, NCONST, 128], BF16, "cstb")
    K.v(lambda e: e.tensor_copy(out=cstb[:, :, :], in_=cst[:, :, :]), r=[cst], w=[cstb])
    epsb = K.sb([128, 1], F32, "epsb")
    K.v(lambda e: e.memset(epsb[:, :], EPS), w=[epsb])
    oneb = K.sb([128, 1], F32, "oneb")
    K.v(lambda e: e.memset(oneb[:, :], 1.0), w=[oneb])
    K.persist()

    PSB = [K.ps([128, 512], F32, f"psb{i}") for i in range(6)]
    PST = [K.ps([128, 1024], BF16, f"pst{i}") for i in range(2)]
    psb_i = [0]
    pst_i = [0]

    def psb():
        psb_i[0] += 1
        return PSB[psb_i[0] % len(PSB)]

    def pst():
        pst_i[0] += 1
        return PST[pst_i[0] % len(PST)]

    cp_i = [0]

    def evac(out_ap, in_ap, rb, wb, part=False, eng=None):
        cp_i[0] += 1
        kw = dict(r=rb, p=wb) if part else dict(r=rb, w=wb)
        if eng is None:
            eng = "vector" if cp_i[0] % 2 else "scalar"
        if eng == "vector":
            K.v(lambda e: e.tensor_copy(out=out_ap, in_=in_ap), **kw)
        else:
            K.a(lambda e: e.copy(out=out_ap, in_=in_ap), **kw)

    K.new_phase()
    stg_f = [K.sb([128, 2048], F32, f"wsf{i}") for i in range(3)]
    stg_b = [K.sb([128, 2048], BF16, f"wsb{i}") for i in range(3)]
    ci = 0
    for n, kk, cc in WSPECS:
        for kc in range(kk // 128):
            for c0 in range(0, cc, 2048):
                cw = min(2048, cc - c0)
                sf, sbb = stg_f[ci % 3], stg_b[ci % 3]
                K.load(sf, sf[:, 0:cw], wf32[n], wf32[n].t[kc * 128:(kc + 1) * 128, c0:c0 + cw])
                eng = ("vector", "scalar", "gpsimd")[ci % 3]
                if eng == "scalar":
                    K.a(lambda e, sf=sf, sbb=sbb, cw=cw: e.copy(out=sbb[:, 0:cw], in_=sf[:, 0:cw]), r=[sf], w=[sbb])
                else:
                    X(eng, lambda e, sf=sf, sbb=sbb, cw=cw: e.tensor_copy(out=sbb[:, 0:cw], in_=sf[:, 0:cw]), r=[sf], w=[sbb])
                K.store(wbf[n], wbf[n].t[kc * 128:(kc + 1) * 128, c0:c0 + cw], sbb, sbb[:, 0:cw])
                ci += 1

    def load_w(name, rows, cols, c0=0, buf=None):
        kc = rows // 128
        wt = buf or K.sb([128, kc, cols], BF16, "W_" + name)
        for k in range(kc):
            K.load(wt, wt[:, k, :], wbf[name], wbf[name].t[k * 128:(k + 1) * 128, c0:c0 + cols], part=True)
        return wt

    def rmsnorm_T(xt, nt, ht, hT, sq, ss, col0=0):
        K.a(lambda e: e.activation(out=sq[:, :], in_=xt[:, :], func=AF.Square, accum_out=ss[:, 0:1]), r=[xt], w=[sq, ss])
        K.v(lambda e: e.tensor_scalar(out=ss[:, 1:2], in0=ss[:, 0:1], scalar1=1.0 / D, scalar2=EPS, op0=ALU.mult, op1=ALU.add),
            r=[ss], w=[ss])
        K.a(lambda e: e.sqrt(out=ss[:, 1:2], in_=ss[:, 1:2]), r=[ss], w=[ss])
        K.v(lambda e: e.reciprocal(out=ss[:, 1:2], in_=ss[:, 1:2]), r=[ss], w=[ss])
        K.v(lambda e: e.scalar_tensor_tensor(out=ht[:, :], in0=xt[:, :], scalar=ss[:, 1:2], in1=nt[:, :],
                                             op0=ALU.mult, op1=ALU.mult), r=[xt, ss, nt], w=[ht])
        transpose_to(ht, hT, 8, col0=col0)

    def transpose_to(src, dstT, nchunks, src_off=0, col0=0):
        for c0 in range(0, nchunks, 8):
            n = min(8, nchunks - c0)
            p = pst()
            for c in range(n):
                K.pe(lambda e, p=p, c=c, c0=c0: e.transpose(out=p[:, c * 128:(c + 1) * 128],
                                                          in_=src[:, src_off + (c0 + c) * 128: src_off + (c0 + c + 1) * 128],
                                                          identity=identb[:, :]), r=[src, identb], p=[p])
            evac(dstT[:, c0:c0 + n, col0:col0 + 128], p[:, 0:n * 128].rearrange("p (a b) -> p a b", a=n), [p], [dstT], part=True)

    K.new_phase()
    w_in = load_w("w_in", D, IN_COLS)
    n_mix = load_norm(0)
    dtb = K.sb([128, 16], F32, "dtb")
    K.load(dtb, dtb[:, :], d_dtb, d_dtb.t[:].partition_broadcast(128))
    nA = K.sb([128, 16], F32, "nA")
    K.load(nA, nA[:, :], d_alog, d_alog.t[:].partition_broadcast(128))
    K.a(lambda e: e.activation(out=nA[:, :], in_=nA[:, :], func=AF.Exp), r=[nA], w=[nA])
    K.v(lambda e: e.tensor_scalar(out=nA[:, :], in0=nA[:, :], scalar1=-1.0, scalar2=None, op0=ALU.mult), r=[nA], w=[nA])
    zt = K.sb([128, 2304], BF16, "zeros")
    K.v(lambda e: e.memset(zt[:, :], 0.0), w=[zt])
    x_t = [K.sb([128, D], F32, f"x{i}") for i in range(3)]
    h_t = [K.sb([128, D], BF16, f"h{i}") for i in range(2)]
    sq = K.sb([128, D], F32, "sq")
    sst = [K.sb([128, 2], F32, f"ss{i}") for i in range(2)]
    hTb = [K.sb([128, 8, 512], BF16, f"hT{i}") for i in range(2)]
    hT1 = [K.sb([128, 8, 128], BF16, f"hT1_{i}") for i in range(2)]
    pts = [K.sb([128, 6, 512], BF16, f"pts{i}") for i in range(2)]
    szs = [K.sb([128, D], BF16, f"szs{i}") for i in range(2)]
    qbs = [K.sb([128, 2304], BF16, f"qbs{i}") for i in range(2)]
    abt = [K.sb([128, 4, 32], F32, f"abt{i}") for i in range(2)]
    it = 0
    for nm, L in seqs:
        sc = S[nm]
        K.store(sc["PT"], sc["PT"].t[:, 0:2].rearrange("(c p) t -> p c t", p=128), zt, zt[:, 0:48].rearrange("p (c t) -> p c t", t=2))
        K.store(sc["PT"], sc["PT"].t[:, L + 2:L + 4].rearrange("(c p) t -> p c t", p=128), zt, zt[:, 0:48].rearrange("p (c t) -> p c t", t=2))
        for r0 in list(range(0, 1024, 128)) + list(range(L + 1024, L + 2048, 128)):
            K.store(sc["QB"], sc["QB"].t[r0:r0 + 128, :], zt, zt[:, :])
        for blk in range(L // 512):
            hTt = hTb[blk % 2]
            for tt in range(4):
                xt = x_t[it % 3]
                ht = h_t[it % 2]
                ss = sst[it % 2]
                h1 = hT1[it % 2]
                it += 1
                t0 = blk * 512 + tt * 128
                K.load(xt, xt[:, :], xin[nm], xin[nm].t[t0:t0 + 128, :])
                rmsnorm_T(xt, n_mix, ht, hTt, sq, ss, col0=tt * 128)
            for cg in range(4):
                pt_s = pts[cg % 2]
                for cc in range(6):
                    ct = cg * 6 + cc
                    p = psb()
                    for kc in range(8):
                        K.pe(lambda e, p=p, kc=kc, ct=ct, hTt=hTt: e.matmul(p[:, :], lhsT=w_in[:, kc, ct * 128:(ct + 1) * 128], rhs=hTt[:, kc, :],
                                                                       start=(kc == 0), stop=(kc == 7)), r=[w_in, hTt], p=[p])
                    evac(pt_s[:, cc, :], p[:, :], [p], [pt_s], part=True)
                K.store(sc["PT"], sc["PT"].t[cg * 768:(cg + 1) * 768, 2 + blk * 512: 2 + (blk + 1) * 512].rearrange("(c p) t -> p c t", p=128),
                        pt_s, pt_s[:, :, :])
            for tt in range(4):
                t0 = blk * 512 + tt * 128
                lhs = lambda kc, hTt=hTt, tt=tt: hTt[:, kc, tt * 128:(tt + 1) * 128]
                szt = szs[tt % 2]
                for n in range(2):
                    p = psb()
                    for kc in range(8):
                        K.pe(lambda e, p=p, kc=kc, n=n, lhs=lhs: e.matmul(p[:, :], lhsT=lhs(kc), rhs=w_in[:, kc, 3072 + n * 512:3072 + (n + 1) * 512],
                                                                     start=(kc == 0), stop=(kc == 7)), r=[w_in, hTt], p=[p])
                    K.a(lambda e, p=p, n=n, szt=szt: e.activation(out=szt[:, n * 512:(n + 1) * 512], in_=p[:, :], func=AF.Silu), r=[p], p=[szt])
                K.store(sc["SZ"], sc["SZ"].t[t0:t0 + 128, :], szt, szt[:, :])
                qbt = qbs[tt % 2]
                for n in range(5):
                    cw = 512 if n < 4 else 256
                    p = psb()
                    for kc in range(8):
                        K.pe(lambda e, p=p, kc=kc, n=n, cw=cw, lhs=lhs: e.matmul(p[:, 0:cw], lhsT=lhs(kc), rhs=w_in[:, kc, 4128 + n * 512:4128 + n * 512 + cw],
                                                                            start=(kc == 0), stop=(kc == 7)), r=[w_in, hTt], p=[p])
                    evac(qbt[:, n * 512:n * 512 + cw], p[:, 0:cw], [p], [qbt], part=True)
                K.store(sc["QB"], sc["QB"].t[1024 + t0:1024 + t0 + 128, :], qbt, qbt[:, :])
                ab = abt[tt % 2]
                p = psb()
                for kc in range(8):
                    K.pe(lambda e, p=p, kc=kc, lhs=lhs: e.matmul(p[:, 0:32], lhsT=lhs(kc), rhs=w_in[:, kc, 4096:4128],
                                                              start=(kc == 0), stop=(kc == 7)), r=[w_in, hTt], p=[p])
                K.v(lambda e, p=p, ab=ab: e.tensor_tensor(out=ab[:, 0, 0:16], in0=p[:, 0:16], in1=dtb[:, :], op=ALU.add), r=[p, dtb], w=[ab])
                K.a(lambda e, ab=ab: e.activation(out=ab[:, 0, 0:16], in_=ab[:, 0, 0:16], func=AF.Exp), r=[ab], w=[ab])
                K.a(lambda e, ab=ab: e.activation(out=ab[:, 0, 0:16], in_=ab[:, 0, 0:16], func=AF.Ln, bias=oneb[:, 0:1]), r=[ab, oneb], w=[ab])
                K.v(lambda e, ab=ab: e.tensor_tensor(out=ab[:, 1, 0:16], in0=ab[:, 0, 0:16], in1=nA[:, :], op=ALU.mult), r=[ab, nA], w=[ab])
                K.a(lambda e, p=p, ab=ab: e.activation(out=ab[:, 2, 0:16], in_=p[:, 16:32], func=AF.Exp, scale=-1.0), r=[p], w=[ab])
                K.v(lambda e, ab=ab: e.tensor_scalar(out=ab[:, 2, 0:16], in0=ab[:, 2, 0:16], scalar1=1.0, scalar2=None, op0=ALU.add), r=[ab], w=[ab])
                K.v(lambda e, ab=ab: e.reciprocal(out=ab[:, 3, 0:16], in_=ab[:, 2, 0:16]), r=[ab], w=[ab])
                K.store(sc["GG"], sc["GG"].t[t0:t0 + 128, :], ab, ab[:, 1, 0:16])
                K.store(sc["BB"], sc["BB"].t[t0:t0 + 128, :], ab, ab[:, 3, 0:16])

    if upto <= 1:
        return finish(K, S, yout, seqs)
    K.new_phase()
    cw = K.sb([128, 24, 5], F32, "cw")
    K.load(cw, cw[:, :, :], d_convw, d_convw.t[:, :, :])
    pin = [K.sb([128, 516], BF16, f"pin{i}") for i in range(4)]
    acc = [K.sb([128, 512], F32, f"acc{i}") for i in range(4)]
    sil = [K.sb([128, 512], F32, f"sil{i}") for i in range(4)]
    sqb = [K.sb([128, 512], BF16, f"sqb{i}") for i in range(4)]
    rn = [K.sb([128, 512], F32, f"rn{i}") for i in range(4)]
    qkn = [K.sb([128, 512], BF16, f"qkn{i}") for i in range(4)]
    tkb = [K.sb([128, 4, 128], BF16, f"tkb{i}") for i in range(4)]
    i2 = 0
    for nm, L in seqs:
        sc = S[nm]
        for blk in range(L // 512):
            for ct in range(24):
                pi, ac, si, sq_, rn_, qn, tk = pin[i2 % 4], acc[i2 % 4], sil[i2 % 4], sqb[i2 % 4], rn[i2 % 4], qkn[i2 % 4], tkb[i2 % 4]
                i2 += 1
                K.load(pi, pi[:, :], sc["PT"], sc["PT"].t[ct * 128:(ct + 1) * 128, blk * 512:blk * 512 + 516])
                K.v(lambda e, pi=pi, ac=ac, ct=ct: e.tensor_scalar(out=ac[:, :], in0=pi[:, 0:512], scalar1=cw[:, ct, 0:1], scalar2=None, op0=ALU.mult),
                    r=[pi, cw], w=[ac])
                for k in range(1, 5):
                    X("vector",
                      lambda e, pi=pi, ac=ac, ct=ct, k=k: e.scalar_tensor_tensor(out=ac[:, :], in0=pi[:, k:k + 512], scalar=cw[:, ct, k:k + 1], in1=ac[:, :],
                                                                                 op0=ALU.mult, op1=ALU.add), r=[pi, cw, ac], w=[ac])
                if ct < 16:
                    K.a(lambda e, ac=ac, si=si: e.activation(out=si[:, :], in_=ac[:, :], func=AF.Silu), r=[ac], w=[si])
                    K.a(lambda e, si=si, sq_=sq_: e.activation(out=sq_[:, :], in_=si[:, :], func=AF.Square), r=[si], w=[sq_])
                    p = psb()
                    K.pe(lambda e, p=p, sq_=sq_: e.matmul(p[:, :], lhsT=onesb[:, :], rhs=sq_[:, :], start=True, stop=True), r=[onesb, sq_], p=[p])
                    K.v(lambda e, p=p, rn_=rn_: e.tensor_scalar(out=rn_[:, :], in0=p[:, :], scalar1=EPS, scalar2=None, op0=ALU.add), r=[p], w=[rn_])
                    K.a(lambda e, rn_=rn_: e.sqrt(out=rn_[:, :], in_=rn_[:, :]), r=[rn_], w=[rn_])
                    K.v(lambda e, rn_=rn_: e.reciprocal(out=rn_[:, :], in_=rn_[:, :]), r=[rn_], w=[rn_])
                    scl = (128 ** -0.5) if ct < 8 else 1.0
                    K.v(lambda e, si=si, rn_=rn_, qn=qn, scl=scl: e.scalar_tensor_tensor(out=qn[:, :], in0=si[:, :], scalar=scl, in1=rn_[:, :], op0=ALU.mult, op1=ALU.mult),
                        r=[si, rn_], w=[qn])
                    hd = ct % 8
                    dst = sc["QT"] if ct < 8 else sc["KT"]
                    K.store(dst, dst.t[blk * 4:(blk + 1) * 4, :, hd, :].rearrange("t d k -> d t k"), qn, qn[:, :].rearrange("p (t k) -> p t k", t=4))
                else:
                    K.a(lambda e, ac=ac, qn=qn: e.activation(out=qn[:, :], in_=ac[:, :], func=AF.Silu), r=[ac], w=[qn])
                    hd = ct - 16
                if ct >= 8:
                    p = pst()
                    for tt in range(4):
                        K.pe(lambda e, p=p, qn=qn, tt=tt: e.transpose(out=p[:, tt * 128:(tt + 1) * 128], in_=qn[:, tt * 128:(tt + 1) * 128], identity=identb[:, :]),
                             r=[qn, identb], p=[p])
                    evac(tk[:, :, :], p[:, 0:512].rearrange("p (t d) -> p t d", t=4), [p], [tk])
                    dst = sc["KTOK"] if ct < 16 else sc["VTOK"]
                    K.store(dst, dst.t[blk * 512:(blk + 1) * 512, hd * 128:(hd + 1) * 128].rearrange("(t p) d -> p t d", p=128), tk, tk[:, :, :])
    if upto <= 2:
        return finish(K, S, yout, seqs)

    K.new_phase()
    PSQ = [Buf(K, PSB[i].t[:, j * 128:(j + 1) * 128], f"psq{i}_{j}") for i in range(6) for j in range(4)]
    PSTQ = [Buf(K, PST[i].t[:, j * 128:(j + 1) * 128], f"pstq{i}_{j}") for i in range(2) for j in range(8)]
    psq_i = [0]
    pstq_i = [0]

    def psq():
        psq_i[0] += 1
        return PSQ[psq_i[0] % len(PSQ)]

    def pstq():
        pstq_i[0] += 1
        return PSTQ[pstq_i[0] % len(PSTQ)]

    def ev(out_ap, in_ap, rb, wb):
        evac(out_ap, in_ap, rb, wb)

    qTd = [[K.sb([128, 8, 128], BF16, f"qTd{d}{i}") for i in range(2)] for d in range(2)]
    kTd = [[K.sb([128, 8, 128], BF16, f"kTd{d}{i}") for i in range(2)] for d in range(2)]
    ktk = [[K.sb([128, D], BF16, f"ktk{d}{i}") for i in range(2)] for d in range(2)]
    vtk = [[K.sb([128, D], BF16, f"vtk{d}{i}") for i in range(2)] for d in range(2)]
    ggd = [[K.sb([128, 8], F32, f"gg{d}{i}") for i in range(2)] for d in range(2)]
    bbd = [[K.sb([128, 8], F32, f"bb{d}{i}") for i in range(2)] for d in range(2)]
    gpd = [[K.sb([128, 8, 8], F32, f"gp{d}{i}") for i in range(2)] for d in range(2)]
    gsd = [[K.sb([128, 3, 8], BF16, f"gs{d}{i}") for i in range(2)] for d in range(2)]
    grd = [[K.sb([128, 2, 8], F32, f"gr{d}{i}") for i in range(2)] for d in range(2)]
    gfd = [[K.sb([128, 3, 8], F32, f"gf{d}{i}") for i in range(2)] for d in range(2)]
    o32 = [[K.sb([128, D], F32, f"o32{d}{i}") for i in range(2)] for d in range(2)]
    S32 = [K.sb([128, 128], F32, f"S32_{u}") for u in range(16)]
    Sb = [K.sb([128, 128], BF16, f"Sb_{u}") for u in range(16)]
    Ag3 = [K.sb([128, 3, 128], BF16, f"Ag3_{u}") for u in range(4)]
    kT2 = [K.sb([128, 8, 128], BF16, f"kT2_{d}") for d in range(2)]
    Eb = [K.sb([128, 128], F32, f"E{u}") for u in range(4)]
    EMS = [K.sb([128, 128], F32, f"EMS{u}") for u in range(4)]
    EMI = [K.sb([128, 128], F32, f"EMI{u}") for u in range(4)]
    Xb = [K.sb([128, 128], BF16, f"X{u}") for u in range(16)]
    XTb = [K.sb([128, 128], BF16, f"XT{u}") for u in range(16)]
    Pb = [[K.sb([128, 128], BF16, f"P{u}_{i}") for i in range(2)] for u in range(16)]
    PTb = [[K.sb([128, 128], BF16, f"PT{u}_{i}") for i in range(2)] for u in range(16)]
    Qb = [K.sb([128, 128], BF16, f"Q{u}") for u in range(16)]
    ATb = [K.sb([128, 128], BF16, f"AT{u}") for u in range(16)]
    rb_ = [K.sb([128, 128], BF16, f"r{u}") for u in range(16)]
    vnb = [K.sb([128, 128], BF16, f"vn{u}") for u in range(16)]
    vnsb = [K.sb([128, 128], BF16, f"vns{u}") for u in range(16)]
    tmpb = [K.sb([128, 128], F32, f"tmp{u}") for u in range(4)]
    for u in range(16):
        for t_ in (rb_[u], vnb[u], vnsb[u]):
            K.g(lambda e, t_=t_: e.memset(t_[:, :], 0.0), w=[t_])

    for nm, L in seqs:
        sc = S[nm]
        ntile = L // 128
        for u in range(16):
            K.g(lambda e, u=u: e.memset(S32[u][:, :], 0.0), w=[S32[u]])
            K.g(lambda e, u=u: e.memset(Sb[u][:, :], 0.0), w=[Sb[u]])
        for it_ in range(ntile):
            par = it_ % 2
            cur = {}
            for d in range(2):
                tile = it_ if d == 0 else ntile - 1 - it_
                t0 = tile * 128
                qT_, kT_, kt_, vt_, gg_, bb_, gp_, o_ = qTd[d][par], kTd[d][par], ktk[d][par], vtk[d][par], ggd[d][par], bbd[d][par], gpd[d][par], o32[d][par]
                cur[d] = (qT_, kT_, kt_, vt_, gg_, bb_, gp_, o_, t0)
                gs_, gr_, gf_ = gsd[d][par], grd[d][par], gfd[d][par]
                cur[(d, "gs")] = gf_
                K.load(qT_, qT_[:, :, :], sc["QT"], sc["QT"].t[tile])
                K.load(kT_, kT_[:, :, :], sc["KT"], sc["KT"].t[tile])
                K.load(kt_, kt_[:, :], sc["KTOK"], sc["KTOK"].t[t0:t0 + 128, :])
                K.load(vt_, vt_[:, :], sc["VTOK"], sc["VTOK"].t[t0:t0 + 128, :])
                K.load(gg_, gg_[:, :], sc["GG"], sc["GG"].t[t0:t0 + 128, d * 8:(d + 1) * 8])
                K.load(bb_, bb_[:, :], sc["BB"], sc["BB"].t[t0:t0 + 128, d * 8:(d + 1) * 8])
                pg = psb()
                tri = C_TRIF if d == 0 else C_TRIB
                K.v(lambda e, gs_=gs_, gg_=gg_: e.tensor_copy(out=gs_[:, 0, :], in_=gg_[:, :]), r=[gg_], w=[gs_])
                K.v(lambda e, gs_=gs_, gg_=gg_, gr_=gr_: e.tensor_tensor(out=gr_[:, 0, :], in0=gg_[:, :], in1=gs_[:, 0, :], op=ALU.subtract), r=[gg_, gs_], w=[gr_])
                K.v(lambda e, gs_=gs_, gr_=gr_: e.tensor_copy(out=gs_[:, 1, :], in_=gr_[:, 0, :]), r=[gr_], w=[gs_])
                K.v(lambda e, gs_=gs_, gr_=gr_: e.tensor_tensor(out=gr_[:, 1, :], in0=gr_[:, 0, :], in1=gs_[:, 1, :], op=ALU.subtract), r=[gr_, gs_], w=[gr_])
                K.v(lambda e, gs_=gs_, gr_=gr_: e.tensor_copy(out=gs_[:, 2, :], in_=gr_[:, 1, :]), r=[gr_], w=[gs_])
                K.v(lambda e, gs_=gs_, gf_=gf_: e.tensor_copy(out=gf_[:, :, :], in_=gs_[:, :, :]), r=[gs_], w=[gf_])
                for j, cm in enumerate((tri, C_BLK, C_CS0, C_CS1)):
                    for q3 in range(3):
                        K.pe(lambda e, pg=pg, j=j, cm=cm, gs_=gs_, q3=q3: e.matmul(pg[:, j * 8:(j + 1) * 8], lhsT=cstb[:, cm, :], rhs=gs_[:, q3, :],
                                                                             start=(q3 == 0), stop=(q3 == 2)), r=[cstb, gs_], p=[pg])
                K.v(lambda e, pg=pg, gp_=gp_: e.tensor_copy(out=gp_[:, 0, :], in_=pg[:, 0:8]), r=[pg], p=[gp_])
                K.a(lambda e, pg=pg, gp_=gp_: e.activation(out=gp_[:, 1, :], in_=pg[:, 0:8], func=AF.Exp), r=[pg], p=[gp_])
                K.v(lambda e, gp_=gp_: e.tensor_scalar(out=gp_[:, 2, :], in0=gp_[:, 1, :], scalar1=-1.0, scalar2=None, op0=ALU.mult), r=[gp_], w=[gp_])
                K.v(lambda e, pg=pg, gp_=gp_: e.tensor_tensor(out=gp_[:, 7, :], in0=pg[:, 8:16], in1=gp_[:, 0, :], op=ALU.subtract), r=[pg, gp_], w=[gp_])
                K.a(lambda e, gp_=gp_: e.activation(out=gp_[:, 7, :], in_=gp_[:, 7, :], func=AF.Exp), r=[gp_], w=[gp_])
                K.v(lambda e, gp_=gp_, bb_=bb_: e.tensor_tensor(out=gp_[:, 3, :], in0=gp_[:, 7, :], in1=bb_[:, :], op=ALU.mult), r=[gp_, bb_], w=[gp_])
                K.a(lambda e, pg=pg, gp_=gp_: e.activation(out=gp_[:, 4:6, :], in_=pg[:, 16:32].rearrange("p (a b) -> p a b", a=2), func=AF.Exp), r=[pg], w=[gp_])
                K.v(lambda e, gp_=gp_, bb_=bb_: e.tensor_scalar(out=gp_[:, 6, :], in0=bb_[:, :], scalar1=-1.0, scalar2=None, op0=ALU.mult), r=[bb_], w=[gp_])
            import os as _os
            PH3 = int(_os.environ.get('PH3', '9'))
            for ug in range(4 if PH3 >= 1 else 0):
                units = list(range(4 * ug, 4 * ug + 4))
                d = units[0] // 8
                qT_, kT_, kt_, vt_, gg_, bb_, gp_, o_, t0 = cur[d]
                cA = C_AF if d == 0 else C_AB
                cB = C_TRIF if d == 0 else C_TRIB
                cMS = C_MSF if d == 0 else C_MSB
                cMI = C_MIF if d == 0 else C_MIB
                sl = lambda j: slice(j * 128, (j + 1) * 128)
                gs_ = cur[(d, "gs")]
                kT2_ = kT_
                for j, u in enumerate(units):
                    h = u % 8
                    for q3 in range(3):
                        K.a(lambda e, j=j, h=h, q3=q3: e.activation(out=Ag3[j][:, q3, :], in_=cstb[:, cA, :], func=AF.Copy, scale=gs_[:, q3, h:h + 1]),
                            r=[cstb, gs_], p=[Ag3[j]])
                bD = psb()
                for j, u in enumerate(units):
                    for q3 in range(3):
                        K.pe(lambda e, j=j, q3=q3: e.matmul(bD[:, sl(j)], lhsT=Ag3[j][:, q3, :], rhs=cstb[:, cB, :], start=(q3 == 0), stop=(q3 == 2)),
                             r=[Ag3[j], cstb], p=[bD])
                for j, u in enumerate(units):
                    K.a(lambda e, j=j: e.activation(out=Eb[j][:, :], in_=bD[:, sl(j)], func=AF.Exp), r=[bD], w=[Eb[j]])
                if PH3 == 1 and int(_os.environ.get('PH3B', '9')) < 0:
                    continue
                bG = psb()
                for j, u in enumerate(units):
                    h = u % 8
                    K.pe(lambda e, j=j, h=h, kT2_=kT2_: e.matmul(bG[:, sl(j)], lhsT=kT_[:, h, :], rhs=kT2_[:, h, :], start=True, stop=True), r=[kT_], p=[bG])
                bQK = psb()
                for j, u in enumerate(units):
                    h = u % 8
                    K.pe(lambda e, j=j, h=h: e.matmul(bQK[:, sl(j)], lhsT=kT_[:, h, :], rhs=qT_[:, h, :], start=True, stop=True), r=[kT_, qT_], p=[bQK])
                for j, u in enumerate(units):
                    K.v(lambda e, j=j: e.tensor_tensor(out=EMS[j][:, :], in0=Eb[j][:, :], in1=cst[:, cMS, :], op=ALU.mult), r=[Eb[j], cst], w=[EMS[j]])
                    K.v(lambda e, j=j: e.tensor_tensor(out=EMI[j][:, :], in0=Eb[j][:, :], in1=cst[:, cMI, :], op=ALU.mult), r=[Eb[j], cst], w=[EMI[j]])
                for j, u in enumerate(units):
                    h = u % 8
                    K.v(lambda e, j=j, u=u, h=h: e.scalar_tensor_tensor(out=Xb[u][:, :], in0=bG[:, sl(j)], scalar=gp_[:, 6, h:h + 1], in1=EMS[j][:, :],
                                                                      op0=ALU.mult, op1=ALU.mult), r=[bG, gp_, EMS[j]], w=[Xb[u]])
                for j, u in enumerate(units):
                    K.v(lambda e, j=j, u=u: e.tensor_tensor(out=ATb[u][:, :], in0=bQK[:, sl(j)], in1=EMI[j][:, :], op=ALU.mult), r=[bQK, EMI[j]], w=[ATb[u]])
                if PH3 == 1 and int(_os.environ.get('PH3B', '9')) < 1:
                    continue
                tp = pst()
                for j, u in enumerate(units):
                    K.pe(lambda e, j=j, u=u: e.transpose(out=tp[:, sl(j)], in_=Xb[u][:, :], identity=identb[:, :]), r=[Xb[u], identb], p=[tp])
                for j, u in enumerate(units):
                    K.a(lambda e, j=j, u=u: e.copy(out=XTb[u][:, :], in_=tp[:, sl(j)]), r=[tp], w=[XTb[u]])
                    K.v(lambda e, u=u: e.tensor_tensor(out=Qb[u][:, :], in0=Xb[u][:, :], in1=identb[:, :], op=ALU.add), r=[Xb[u], identb], w=[Qb[u]])
                Pc = {u: (Xb[u], XTb[u]) for u in units}
                for lev in range(1, 6 if int(_os.environ.get('PH3B', '9')) >= 2 else 1):
                    if lev < 5:
                        bP = psb()
                        for j, u in enumerate(units):
                            P_, PT_ = Pc[u]
                            K.pe(lambda e, j=j, P_=P_, PT_=PT_, bP=bP: e.matmul(bP[:, sl(j)], lhsT=PT_[:, :], rhs=P_[:, :], start=True, stop=True), r=[P_, PT_], p=[bP])
                    bPT = psb()
                    for j, u in enumerate(units):
                        P_, PT_ = Pc[u]
                        K.pe(lambda e, j=j, P_=P_, PT_=PT_, bPT=bPT: e.matmul(bPT[:, sl(j)], lhsT=P_[:, :], rhs=PT_[:, :], start=True, stop=True), r=[P_, PT_], p=[bPT])
                    for j, u in enumerate(units):
                        Pn, PnT = Pb[u][lev % 2], PTb[u][lev % 2]
                        if lev < 5:
                            K.a(lambda e, j=j, Pn=Pn, bP=bP: e.copy(out=Pn[:, :], in_=bP[:, sl(j)]), r=[bP], w=[Pn])
                        K.a(lambda e, j=j, PnT=PnT, bPT=bPT: e.copy(out=PnT[:, :], in_=bPT[:, sl(j)]), r=[bPT], w=[PnT])
                    bQ = psb()
                    for j, u in enumerate(units):
                        PnT = PTb[u][lev % 2]
                        K.pe(lambda e, j=j, u=u, PnT=PnT, bQ=bQ: e.matmul(bQ[:, sl(j)], lhsT=PnT[:, :], rhs=Qb[u][:, :], start=True, stop=True), r=[PnT, Qb[u]], p=[bQ])
                    for j, u in enumerate(units):
                        K.v(lambda e, j=j, u=u, bQ=bQ: e.tensor_tensor(out=Qb[u][:, :], in0=bQ[:, sl(j)], in1=Qb[u][:, :], op=ALU.add), r=[bQ, Qb[u]], w=[Qb[u]])
                        Pc[u] = (Pb[u][lev % 2], PTb[u][lev % 2])
            for s_ in range(2 if PH3 >= 2 else 0):
                for stage in range(4):
                    for ug in range(4):
                        units = list(range(4 * ug, 4 * ug + 4))
                        d = units[0] // 8
                        qT_, kT_, kt_, vt_, gg_, bb_, gp_, o_, t0 = cur[d]
                        c = s_ if d == 0 else 1 - s_
                        rows = slice(64 * c, 64 * c + 64)
                        sl = lambda j: slice(j * 128, (j + 1) * 128)
                        hsl = lambda u: slice((u % 8) * 128, (u % 8 + 1) * 128)
                        if stage == 0:
                            b1 = psb()
                            for j, u in enumerate(units):
                                K.pe(lambda e, j=j, u=u, b1=b1: e.matmul(b1[:, sl(j)], lhsT=kT_[:, u % 8, :], rhs=Sb[u][:, :], start=True, stop=True), r=[kT_, Sb[u]], p=[b1])
                            for j, u in enumerate(units):
                                K.v(lambda e, j=j, u=u, b1=b1: e.scalar_tensor_tensor(
                                    out=rb_[u][rows, :], in0=b1[rows, sl(j)], scalar=gp_[rows, 2, u % 8:u % 8 + 1], in1=vt_[rows, hsl(u)], op0=ALU.mult, op1=ALU.add),
                                    r=[b1, gp_, vt_], w=[rb_[u]])
                        elif stage == 1:
                            b2 = psb()
                            for j, u in enumerate(units):
                                K.pe(lambda e, j=j, u=u, b2=b2: e.matmul(b2[:, sl(j)], lhsT=Qb[u][:, :], rhs=rb_[u][:, :], start=True, stop=True), r=[Qb[u], rb_[u]], p=[b2])
                            for j, u in enumerate(units):
                                K.v(lambda e, j=j, u=u, b2=b2: e.tensor_scalar(out=vnb[u][rows, :], in0=b2[rows, sl(j)], scalar1=bb_[rows, u % 8:u % 8 + 1], scalar2=None, op0=ALU.mult),
                                    r=[b2, bb_], w=[vnb[u]])
                                K.a(lambda e, u=u: e.activation(out=vnsb[u][rows, :], in_=vnb[u][rows, :], func=AF.Copy, scale=gp_[rows, 7, u % 8:u % 8 + 1]),
                                    r=[vnb[u], gp_], w=[vnsb[u]])
                        elif stage == 2:
                            bq = psb()
                            for j, u in enumerate(units):
                                K.pe(lambda e, j=j, u=u, bq=bq: e.matmul(bq[:, sl(j)], lhsT=qT_[:, u % 8, :], rhs=Sb[u][:, :], start=True, stop=True), r=[qT_, Sb[u]], p=[bq])
                            ba = psb()
                            for j, u in enumerate(units):
                                K.pe(lambda e, j=j, u=u, ba=ba: e.matmul(ba[:, sl(j)], lhsT=ATb[u][:, :], rhs=vnb[u][:, :], start=True, stop=True), r=[ATb[u], vnb[u]], p=[ba])
                            for j, u in enumerate(units):
                                K.a(lambda e, j=j, ba=ba: e.copy(out=tmpb[j][rows, :], in_=ba[rows, sl(j)]), r=[ba], w=[tmpb[j]])
                            for j, u in enumerate(units):
                                K.v(lambda e, j=j, u=u, bq=bq: e.scalar_tensor_tensor(
                                    out=o_[rows, hsl(u)], in0=bq[rows, sl(j)], scalar=gp_[rows, 1, u % 8:u % 8 + 1], in1=tmpb[j][rows, :], op0=ALU.mult, op1=ALU.add),
                                    r=[bq, tmpb[j], gp_], p=[o_])
                        else:
                            bs = psb()
                            for j, u in enumerate(units):
                                K.pe(lambda e, j=j, u=u, bs=bs: e.matmul(bs[:, sl(j)], lhsT=kt_[rows, hsl(u)], rhs=vnsb[u][rows, :], start=True, stop=True),
                                     r=[kt_, vnsb[u]], p=[bs])
                            for j, u in enumerate(units):
                                K.v(lambda e, j=j, u=u, bs=bs: e.scalar_tensor_tensor(out=S32[u][:, :], in0=S32[u][:, :], scalar=gp_[:, 4 + c, u % 8:u % 8 + 1], in1=bs[:, sl(j)],
                                                                                  op0=ALU.mult, op1=ALU.add), r=[bs, gp_, S32[u]], w=[S32[u]])
                                K.a(lambda e, u=u: e.copy(out=Sb[u][:, :], in_=S32[u][:, :]), r=[S32[u]], w=[Sb[u]])
            for d in range(2):
                qT_, kT_, kt_, vt_, gg_, bb_, gp_, o_, t0 = cur[d]
                dst = sc["OF"] if d == 0 else sc["OB"]
                K.store(dst, dst.t[t0:t0 + 128, :], o_, o_[:, :])
    if upto <= 3:
        return finish(K, S, yout, seqs)
    K.new_phase()
    bm = K.sb([128, 48, 256], F32, "bm")
    K.load(bm, bm[:, :, :], d_bm, d_bm.t[:, :, :, :, :].rearrange("p g h v k -> p (g h v) k"))
    dq = [K.sb([128, 256], BF16, f"dq{i}") for i in range(4)]
    dk = [K.sb([128, 2, 256], BF16, f"dk{i}") for i in range(4)]
    dv = [K.sb([128, 2, 256], BF16, f"dv{i}") for i in range(4)]
    dqT = [K.sb([128, 2, 128], BF16, f"dqT{i}") for i in range(4)]
    dkT = [K.sb([128, 2, 256], BF16, f"dkT{i}") for i in range(4)]
    ds_ = [K.sb([128, 256], F32, f"ds{i}") for i in range(4)]
    de_ = [K.sb([128, 256], BF16, f"de{i}") for i in range(4)]
    deT = [K.sb([128, 2, 128], BF16, f"deT{i}") for i in range(4)]
    dst_ = [K.sb([128, 16], F32, f"dst{i}") for i in range(4)]
    dus = [K.sb([128, 256], F32, f"dus{i}") for i in range(4)]
    i4 = 0
    i5 = 0
    for nm, L in seqs:
        sc = S[nm]
        QBd = sc["QB"]
        for g, dil in enumerate(DILS):
            Ls = L // dil
            ntile = Ls // 128
            cq = g * 768
            for r in range(dil):
                for j in range(ntile):
                    q_, k2, v2, qT2, kT2_, st, us = dq[i4 % 4], dk[i4 % 4], dv[i4 % 4], dqT[i4 % 4], dkT[i4 % 4], dst_[i4 % 4], dus[i4 % 4]
                    i4 += 1
                    m0 = j * 128
                    qrow0 = 1024 + m0 * dil + r
                    krow0 = 1024 + (m0 - 64) * dil + r
                    span = 127 * dil + 1
                    K.load(q_, q_[:, :], QBd, QBd.t[qrow0:qrow0 + span:dil, cq:cq + 256])
                    for c in range(2):
                        kr = krow0 + c * 128 * dil
                        K.load(k2, k2[:, c, :], QBd, QBd.t[kr:kr + span:dil, cq + 256:cq + 512], part=True)
                        K.load(v2, v2[:, c, :], QBd, QBd.t[kr:kr + span:dil, cq + 512:cq + 768], part=True)
                    tq = pst()
                    for hp in range(2):
                        K.pe(lambda e: e.transpose(out=tq[:, hp * 128:(hp + 1) * 128], in_=q_[:, hp * 128:(hp + 1) * 128], identity=identb[:, :]),
                             r=[q_, identb], p=[tq])
                    evac(qT2[:, :, :], tq[:, 0:256].rearrange("p (a b) -> p a b", a=2), [tq], [qT2])
                    tk = pst()
                    for hp in range(2):
                        for c in range(2):
                            K.pe(lambda e: e.transpose(out=tk[:, (hp * 2 + c) * 128:(hp * 2 + c + 1) * 128], in_=k2[:, c, hp * 128:(hp + 1) * 128], identity=identb[:, :]),
                                 r=[k2, identb], p=[tk])
                    evac(kT2_[:, :, :], tk[:, 0:512].rearrange("p (a b) -> p a b", a=2), [tk], [kT2_])
                    var = (1 if j == 0 else 0) | (2 if j == ntile - 1 else 0)
                    up = psb()
                    hb = []
                    for h in range(4):
                        s_, e_, eT_ = ds_[i5 % 4], de_[i5 % 4], deT[i5 % 4]
                        i5 += 1
                        hp = h // 2
                        po = (h % 2) * 64
                        sp = psb()
                        K.pe(lambda e: e.matmul(sp[:, 0:256], lhsT=qT2[po:po + 64, hp, :], rhs=kT2_[po:po + 64, hp, :], start=True, stop=True),
                             r=[qT2, kT2_], p=[sp])
                        hb.append((s_, e_, eT_, sp))
                    for h in range(4):
                        s_, e_, eT_, sp = hb[h]
                        bi = g * 16 + h * 4 + var
                        K.v(lambda e: e.scalar_tensor_tensor(out=s_[:, :], in0=sp[:, 0:256], scalar=0.125, in1=bm[:, bi, :], op0=ALU.mult, op1=ALU.add),
                            r=[sp, bm], w=[s_])
                        K.v(lambda e: e.reduce_max(out=st[:, h:h + 1], in_=s_[:, :], axis=AX.X), r=[s_], p=[st])
                        K.v(lambda e: e.tensor_scalar(out=st[:, 8 + h:9 + h], in0=st[:, h:h + 1], scalar1=-1.0, scalar2=None, op0=ALU.mult), r=[st], p=[st])
                    for h in range(4):
                        s_, e_, eT_, sp = hb[h]
                        K.a(lambda e: e.activation(out=e_[:, :], in_=s_[:, :], func=AF.Exp, bias=st[:, 8 + h:9 + h], accum_out=st[:, 4 + h:5 + h]),
                            r=[s_, st], w=[e_], p=[st])
                    for h in range(4):
                        s_, e_, eT_, sp = hb[h]
                        te = pst()
                        for c in range(2):
                            K.pe(lambda e: e.transpose(out=te[:, c * 128:(c + 1) * 128], in_=e_[:, c * 128:(c + 1) * 128], identity=identb[:, :]),
                                 r=[e_, identb], p=[te])
                        evac(eT_[:, :, :], te[:, 0:256].rearrange("p (a b) -> p a b", a=2), [te], [eT_])
                    for h in range(4):
                        s_, e_, eT_, sp = hb[h]
                        for c in range(2):
                            K.pe(lambda e: e.matmul(up[:, h * 64:(h + 1) * 64], lhsT=eT_[:, c, :], rhs=v2[:, c, h * 64:(h + 1) * 64], start=(c == 0), stop=(c == 1)),
                                 r=[eT_, v2], p=[up])
                    evac(us[:, :], up[:, 0:256], [up], [us])
                    trow = m0 * dil + r
                    K.store(sc["U"][g], sc["U"][g].t[trow:trow + span:dil, :], us, us[:, :])
                    K.store(sc["ST"][g], sc["ST"][g].t[trow:trow + span:dil, :], st, st[:, 0:8])
    if upto <= 4:
        return finish(K, S, yout, seqs)

    K.new_phase()
    w_ckv = load_w("w_ckv", D, 2 * D)
    n_mem = load_norm(2)
    mx_ = K.sb([128, D], F32, "mx")
    mh_ = K.sb([128, D], BF16, "mh")
    msq = K.sb([128, D], F32, "msq")
    mss = K.sb([128, 2], F32, "mss")
    mT1 = K.sb([128, 8, 128], BF16, "mT1")
    memT = K.sb([128, 8, 256], BF16, "memT")
    kmT_s = K.sb([128, 8, 256], BF16, "kmT_s")
    vm_s = K.sb([128, 2, D], BF16, "vm_s")
    for nm, L in seqs:
        sc = S[nm]
        for mt in range(2):
            K.load(mx_, mx_[:, :], memin[nm], memin[nm].t[mt * 128:(mt + 1) * 128, :])
            rmsnorm_T(mx_, n_mem, mh_, mT1, msq, mss)
            K.g(lambda e: e.tensor_copy(out=memT[:, :, mt * 128:(mt + 1) * 128], in_=mT1[:, :, :]), r=[mT1], p=[memT])
        for ft in range(8):
            p = psb()
            for kc in range(8):
                K.pe(lambda e: e.matmul(p[:, 0:256], lhsT=w_ckv[:, kc, ft * 128:(ft + 1) * 128], rhs=memT[:, kc, :], start=(kc == 0), stop=(kc == 7)),
                     r=[w_ckv, memT], p=[p])
            evac(kmT_s[:, ft, :], p[:, 0:256], [p], [kmT_s], part=True)
        for mt in range(2):
            for n in range(2):
                p = psb()
                for kc in range(8):
                    K.pe(lambda e: e.matmul(p[:, :], lhsT=memT[:, kc, mt * 128:(mt + 1) * 128], rhs=w_ckv[:, kc, D + n * 512:D + (n + 1) * 512],
                                            start=(kc == 0), stop=(kc == 7)), r=[w_ckv, memT], p=[p])
                evac(vm_s[:, mt, n * 512:(n + 1) * 512], p[:, :], [p], [vm_s], part=True)
        K.store(sc["KMT"], sc["KMT"].t[:, :, :], kmT_s, kmT_s[:, :, :])
        K.store(sc["VM"], sc["VM"].t[:, :, :], vm_s, vm_s[:, :, :])

    K.new_phase()
    w_gate = load_w("w_gate", D, 2 * D)
    w_pa = load_w("w_pa", D, D)
    w_pb = load_w("w_pb", 256, D)
    w_o = load_w("w_o", D, D)
    w_cq = load_w("w_cq", D, D)
    w_co = load_w("w_co", D, D)
    n_mix = load_norm(0)
    n_cross = load_norm(1)
    n_gdn = load_norm(5)
    kmT = K.sb([128, 8, 256], BF16, "kmT")
    vm = K.sb([128, 2, D], BF16, "vm")
    ex = [K.sb([128, D], F32, f"ex{i}") for i in range(2)]
    eh = K.sb([128, D], BF16, "eh")
    ehT = K.sb([128, 8, 128], BF16, "ehT")
    esq = K.sb([128, D], F32, "esq")
    ess = K.sb([128, 2], F32, "ess")
    gates = K.sb([128, 2 * D], BF16, "gates")
    eof = K.sb([128, D], F32, "eof")
    eob = K.sb([128, D], F32, "eob")
    esz = K.sb([128, D], BF16, "esz")
    egs = K.sb([128, 24], F32, "egs")
    eoa = K.sb([128, D], BF16, "eoa")
    eoaT = K.sb([128, 8, 128], BF16, "eoaT")
    eU = [K.sb([128, 256], F32, f"eU{g}") for g in range(3)]
    eST = [K.sb([128, 8], F32, f"eST{g}") for g in range(3)]
    emg = K.sb([128, 40], F32, "emg")
    eacc = K.sb([128, 256], F32, "eacc")
    eobm = K.sb([128, 256], BF16, "eobm")
    eobT = K.sb([128, 2, 128], BF16, "eobT")
    emix = K.sb([128, D], F32, "emix")
    emixb = K.sb([128, D], BF16, "emixb")
    emixT = K.sb([128, 8, 128], BF16, "emixT")
    ex1 = K.sb([128, D], F32, "ex1")
    eh2 = K.sb([128, D], BF16, "eh2")
    eh2T = K.sb([128, 8, 128], BF16, "eh2T")
    eqc = K.sb([128, 8, 128], BF16, "eqc")
    ecs = K.sb([128, 16], F32, "ecs")
    ece = [K.sb([128, 256], BF16, f"ece{i}") for i in range(2)]
    eceT = [K.sb([128, 2, 128], BF16, f"eceT{i}") for i in range(2)]
    eoc = K.sb([128, D], BF16, "eoc")
    eocT = K.sb([128, 8, 128], BF16, "eocT")
    ex2 = [K.sb([128, D], F32, f"ex2_{i}") for i in range(2)]
    i6 = 0
    for nm, L in seqs:
        sc = S[nm]
        K.load(kmT, kmT[:, :, :], sc["KMT"], sc["KMT"].t[:, :, :])
        K.load(vm, vm[:, :, :], sc["VM"], sc["VM"].t[:, :, :])
        for tile in range(L // 128):
            t0 = tile * 128
            xt = ex[i6 % 2]
            x2t = ex2[i6 % 2]
            i6 += 1
            K.load(xt, xt[:, :], xin[nm], xin[nm].t[t0:t0 + 128, :])
            K.load(eof, eof[:, :], sc["OF"], sc["OF"].t[t0:t0 + 128, :])
            K.load(eob, eob[:, :], sc["OB"], sc["OB"].t[t0:t0 + 128, :])
            K.load(esz, esz[:, :], sc["SZ"], sc["SZ"].t[t0:t0 + 128, :])
            for g in range(3):
                K.load(eU[g], eU[g][:, :], sc["U"][g], sc["U"][g].t[t0:t0 + 128, :])
                K.load(eST[g], eST[g][:, :], sc["ST"][g], sc["ST"][g].t[t0:t0 + 128, :])
            rmsnorm_T(xt, n_mix, eh, ehT, esq, ess)
            for n in range(4):
                p = psb()
                for kc in range(8):
                    K.pe(lambda e: e.matmul(p[:, :], lhsT=ehT[:, kc, :], rhs=w_gate[:, kc, n * 512:(n + 1) * 512], start=(kc == 0), stop=(kc == 7)),
                         r=[ehT, w_gate], p=[p])
                K.a(lambda e: e.activation(out=gates[:, n * 512:(n + 1) * 512], in_=p[:, :], func=AF.Sigmoid), r=[p], p=[gates])
            K.v(lambda e: e.tensor_tensor(out=eof[:, :], in0=eof[:, :], in1=eob[:, :], op=ALU.add), r=[eof, eob], w=[eof])
            K.g(lambda e: e.tensor_tensor(out=eob[:, :], in0=eof[:, :], in1=eof[:, :], op=ALU.mult), r=[eof], w=[eob])
            K.v(lambda e: e.reduce_sum(out=egs[:, 0:8], in_=eob[:, :].rearrange("p (h d) -> p h d", h=8), axis=AX.X), r=[eob], w=[egs])
            K.v(lambda e: e.tensor_scalar(out=egs[:, 8:16], in0=egs[:, 0:8], scalar1=1.0 / 128, scalar2=EPS, op0=ALU.mult, op1=ALU.add), r=[egs], w=[egs])
            K.a(lambda e: e.sqrt(out=egs[:, 8:16], in_=egs[:, 8:16]), r=[egs], w=[egs])
            K.v(lambda e: e.reciprocal(out=egs[:, 16:24], in_=egs[:, 8:16]), r=[egs], w=[egs])
            for h in range(8):
                hs = slice(h * 128, (h + 1) * 128)
                K.v(lambda e: e.scalar_tensor_tensor(out=eof[:, hs], in0=eof[:, hs], scalar=egs[:, 16 + h:17 + h], in1=n_gdn[:, hs], op0=ALU.mult, op1=ALU.mult),
                    r=[eof, egs, n_gdn], w=[eof])
            K.v(lambda e: e.tensor_tensor(out=eoa[:, :], in0=eof[:, :], in1=esz[:, :], op=ALU.mult), r=[eof, esz], w=[eoa])
            transpose_to(eoa, eoaT, 8)
            K.v(lambda e: e.tensor_tensor(out=emg[:, 0:4], in0=eST[0][:, 0:4], in1=eST[1][:, 0:4], op=ALU.max), r=[eST[0], eST[1]], w=[emg])
            K.v(lambda e: e.tensor_tensor(out=emg[:, 0:4], in0=emg[:, 0:4], in1=eST[2][:, 0:4], op=ALU.max), r=[emg, eST[2]], w=[emg])
            for g in range(3):
                K.v(lambda e: e.tensor_tensor(out=emg[:, 4 + 4 * g:8 + 4 * g], in0=eST[g][:, 0:4], in1=emg[:, 0:4], op=ALU.subtract), r=[eST[g], emg], w=[emg])
            K.a(lambda e: e.activation(out=emg[:, 4:16], in_=emg[:, 4:16], func=AF.Exp), r=[emg], w=[emg])
            K.v(lambda e: e.tensor_tensor(out=emg[:, 16:20], in0=emg[:, 4:8], in1=eST[0][:, 4:8], op=ALU.mult), r=[emg, eST[0]], w=[emg])
            for g in (1, 2):
                K.v(lambda e: e.tensor_tensor(out=emg[:, 36:40], in0=emg[:, 4 + 4 * g:8 + 4 * g], in1=eST[g][:, 4:8], op=ALU.mult), r=[emg, eST[g]], w=[emg])
                K.v(lambda e: e.tensor_tensor(out=emg[:, 16:20], in0=emg[:, 16:20], in1=emg[:, 36:40], op=ALU.add), r=[emg], w=[emg])
            K.v(lambda e: e.reciprocal(out=emg[:, 20:24], in_=emg[:, 16:20]), r=[emg], w=[emg])
            for g in range(3):
                K.v(lambda e: e.tensor_tensor(out=emg[:, 24 + 4 * g:28 + 4 * g], in0=emg[:, 4 + 4 * g:8 + 4 * g], in1=emg[:, 20:24], op=ALU.mult), r=[emg], w=[emg])
            for h in range(4):
                hs = slice(h * 64, (h + 1) * 64)
                K.v(lambda e: e.tensor_scalar(out=eacc[:, hs], in0=eU[0][:, hs], scalar1=emg[:, 24 + h:25 + h], scalar2=None, op0=ALU.mult), r=[eU[0], emg], w=[eacc])
                for g in (1, 2):
                    K.v(lambda e: e.scalar_tensor_tensor(out=eacc[:, hs], in0=eU[g][:, hs], scalar=emg[:, 24 + 4 * g + h:25 + 4 * g + h], in1=eacc[:, hs],
                                                         op0=ALU.mult, op1=ALU.add), r=[eU[g], emg, eacc], w=[eacc])
            K.v(lambda e: e.tensor_copy(out=eobm[:, :], in_=eacc[:, :]), r=[eacc], w=[eobm])
            transpose_to(eobm, eobT, 2)
            for n in range(2):
                ns = slice(n * 512, (n + 1) * 512)
                p = psb()
                for kc in range(8):
                    K.pe(lambda e: e.matmul(p[:, :], lhsT=eoaT[:, kc, :], rhs=w_pa[:, kc, ns], start=(kc == 0), stop=(kc == 7)), r=[eoaT, w_pa], p=[p])
                K.v(lambda e: e.tensor_tensor(out=emix[:, ns], in0=p[:, :], in1=gates[:, ns], op=ALU.mult), r=[p, gates], p=[emix])
                p2 = psb()
                for kc in range(2):
                    K.pe(lambda e: e.matmul(p2[:, :], lhsT=eobT[:, kc, :], rhs=w_pb[:, kc, ns], start=(kc == 0), stop=(kc == 1)), r=[eobT, w_pb], p=[p2])
                K.v(lambda e: e.tensor_tensor(out=esq[:, ns], in0=p2[:, :], in1=gates[:, D + n * 512:D + (n + 1) * 512], op=ALU.mult), r=[p2, gates], p=[esq])
            K.v(lambda e: e.tensor_tensor(out=emixb[:, :], in0=emix[:, :], in1=esq[:, :], op=ALU.add), r=[emix, esq], w=[emixb])
            transpose_to(emixb, emixT, 8)
            for n in range(2):
                ns = slice(n * 512, (n + 1) * 512)
                p = psb()
                for kc in range(8):
                    K.pe(lambda e: e.matmul(p[:, :], lhsT=emixT[:, kc, :], rhs=w_o[:, kc, ns], start=(kc == 0), stop=(kc == 7)), r=[emixT, w_o], p=[p])
                K.v(lambda e: e.tensor_tensor(out=ex1[:, ns], in0=p[:, :], in1=xt[:, ns], op=ALU.add), r=[p, xt], p=[ex1])
            rmsnorm_T(ex1, n_cross, eh2, eh2T, esq, ess)
            for half in range(2):
                p = psb()
                for f4 in range(4):
                    ft = half * 4 + f4
                    for kc in range(8):
                        K.pe(lambda e: e.matmul(p[:, f4 * 128:(f4 + 1) * 128], lhsT=w_cq[:, kc, ft * 128:(ft + 1) * 128], rhs=eh2T[:, kc, :], start=(kc == 0), stop=(kc == 7)),
                             r=[w_cq, eh2T], p=[p])
                evac(eqc[:, half * 4:(half + 1) * 4, :], p[:, :].rearrange("p (a b) -> p a b", a=4), [p], [eqc], part=True)
            po_ = psb()
            po2_ = psb()
            for hh in range(4):
                ce, ceT = ece[hh % 2], eceT[hh % 2]
                sp = psb()
                for c in range(2):
                    K.pe(lambda e: e.matmul(sp[:, 0:256], lhsT=eqc[:, 2 * hh + c, :], rhs=kmT[:, 2 * hh + c, :], start=(c == 0), stop=(c == 1)), r=[eqc, kmT], p=[sp])
                K.v(lambda e: e.reduce_max(out=ecs[:, hh:hh + 1], in_=sp[:, 0:256], axis=AX.X), r=[sp], p=[ecs])
                K.v(lambda e: e.tensor_scalar(out=ecs[:, 4 + hh:5 + hh], in0=ecs[:, hh:hh + 1], scalar1=-1.0 / 16, scalar2=None, op0=ALU.mult), r=[ecs], p=[ecs])
                K.a(lambda e: e.activation(out=ce[:, :], in_=sp[:, 0:256], func=AF.Exp, scale=1.0 / 16, bias=ecs[:, 4 + hh:5 + hh], accum_out=ecs[:, 8 + hh:9 + hh]),
                    r=[sp, ecs], w=[ce], p=[ecs])
                te = pst()
                for c in range(2):
                    K.pe(lambda e: e.transpose(out=te[:, c * 128:(c + 1) * 128], in_=ce[:, c * 128:(c + 1) * 128], identity=identb[:, :]), r=[ce, identb], p=[te])
                evac(ceT[:, :, :], te[:, 0:256].rearrange("p (a b) -> p a b", a=2), [te], [ceT])
                pv = po_ if hh < 2 else po2_
                for c in range(2):
                    K.pe(lambda e: e.matmul(pv[:, (hh % 2) * 256:(hh % 2 + 1) * 256], lhsT=ceT[:, c, :], rhs=vm[:, c, hh * 256:(hh + 1) * 256], start=(c == 0), stop=(c == 1)),
                         r=[ceT, vm], p=[pv])
            K.v(lambda e: e.reciprocal(out=ecs[:, 12:16], in_=ecs[:, 8:12]), r=[ecs], w=[ecs])
            for hh in range(4):
                pv = po_ if hh < 2 else po2_
                K.v(lambda e: e.tensor_scalar(out=eoc[:, hh * 256:(hh + 1) * 256], in0=pv[:, (hh % 2) * 256:(hh % 2 + 1) * 256], scalar1=ecs[:, 12 + hh:13 + hh], scalar2=None, op0=ALU.mult),
                    r=[pv, ecs], p=[eoc])
            transpose_to(eoc, eocT, 8)
            for n in range(2):
                ns = slice(n * 512, (n + 1) * 512)
                p = psb()
                for kc in range(8):
                    K.pe(lambda e: e.matmul(p[:, :], lhsT=eocT[:, kc, :], rhs=w_co[:, kc, ns], start=(kc == 0), stop=(kc == 7)), r=[eocT, w_co], p=[p])
                K.v(lambda e: e.tensor_tensor(out=x2t[:, ns], in0=p[:, :], in1=ex1[:, ns], op=ALU.add), r=[p, ex1], p=[x2t])
            K.store(sc["X2"], sc["X2"].t[t0:t0 + 128, :], x2t, x2t[:, :])
    if upto <= 5:
        return finish(K, S, yout, seqs)

    K.new_phase()
    w_ff1 = load_w("w_ff1", D, D_FF)
    w_ff3 = load_w("w_ff3", D, D_FF)
    w_ff2 = load_w("w_ff2", D_FF, D)
    n_ffn = load_norm(3)
    n_fin = load_norm(4)
    fx = [K.sb([128, D], F32, f"fx{i}") for i in range(2)]
    fh = K.sb([128, D], BF16, "fh")
    fhT = K.sb([128, 8, 128], BF16, "fhT")
    fsq = K.sb([128, D], F32, "fsq")
    fss = K.sb([128, 2], F32, "fss")
    fs1 = [K.sb([128, 512], F32, f"fs1_{i}") for i in range(2)]
    fhid = K.sb([128, D_FF], BF16, "fhid")
    fhidT = K.sb([128, 22, 128], BF16, "fhidT")
    fx3 = K.sb([128, D], F32, "fx3")
    fy = [K.sb([128, D], F32, f"fy{i}") for i in range(2)]
    i7 = 0
    for nm, L in seqs:
        sc = S[nm]
        for tile in range(L // 128):
            t0 = tile * 128
            xt = fx[i7 % 2]
            yt = fy[i7 % 2]
            i7 += 1
            K.load(xt, xt[:, :], sc["X2"], sc["X2"].t[t0:t0 + 128, :])
            rmsnorm_T(xt, n_ffn, fh, fhT, fsq, fss)
            for n in range(6):
                cw_ = 512 if n < 5 else 256
                ns = slice(n * 512, n * 512 + cw_)
                s1 = fs1[n % 2]
                p1 = psb()
                for kc in range(8):
                    K.pe(lambda e: e.matmul(p1[:, 0:cw_], lhsT=fhT[:, kc, :], rhs=w_ff1[:, kc, ns], start=(kc == 0), stop=(kc == 7)), r=[fhT, w_ff1], p=[p1])
                p3 = psb()
                for kc in range(8):
                    K.pe(lambda e: e.matmul(p3[:, 0:cw_], lhsT=fhT[:, kc, :], rhs=w_ff3[:, kc, ns], start=(kc == 0), stop=(kc == 7)), r=[fhT, w_ff3], p=[p3])
                K.a(lambda e: e.activation(out=s1[:, 0:cw_], in_=p1[:, 0:cw_], func=AF.Silu), r=[p1], w=[s1])
                K.v(lambda e: e.tensor_tensor(out=fhid[:, ns], in0=p3[:, 0:cw_], in1=s1[:, 0:cw_], op=ALU.mult), r=[p3, s1], p=[fhid])
            transpose_to(fhid, fhidT, 22)
            for n in range(2):
                ns = slice(n * 512, (n + 1) * 512)
                p = psb()
                for kc in range(22):
                    K.pe(lambda e: e.matmul(p[:, :], lhsT=fhidT[:, kc, :], rhs=w_ff2[:, kc, ns], start=(kc == 0), stop=(kc == 21)), r=[fhidT, w_ff2], p=[p])
                K.v(lambda e: e.tensor_tensor(out=fx3[:, ns], in0=p[:, :], in1=xt[:, ns], op=ALU.add), r=[p, xt], p=[fx3])
            K.a(lambda e: e.activation(out=fsq[:, :], in_=fx3[:, :], func=AF.Square, accum_out=fss[:, 0:1]), r=[fx3], w=[fsq, fss])
            K.v(lambda e: e.tensor_scalar(out=fss[:, 1:2], in0=fss[:, 0:1], scalar1=1.0 / D, scalar2=EPS, op0=ALU.mult, op1=ALU.add), r=[fss], w=[fss])
            K.a(lambda e: e.sqrt(out=fss[:, 1:2], in_=fss[:, 1:2]), r=[fss], w=[fss])
            K.v(lambda e: e.reciprocal(out=fss[:, 1:2], in_=fss[:, 1:2]), r=[fss], w=[fss])
            K.v(lambda e: e.scalar_tensor_tensor(out=yt[:, :], in0=fx3[:, :], scalar=fss[:, 1:2], in1=n_fin[:, :], op0=ALU.mult, op1=ALU.mult),
                r=[fx3, fss, n_fin], w=[yt])
            K.store(yout[nm], yout[nm].t[t0:t0 + 128, :], yt, yt[:, :])
    return finish(K, S, yout, seqs)


LA, LB = 16384, 2048


def kernel(**inputs):
    inp = {k: np.asarray(v) for k, v in inputs.items()}
    nc = build_program([("A", LA), ("B", LB)])
    cm = common_inputs(inp)
    zx = np.zeros((LA, D), np.float32)
    zm = np.zeros((NMEM, D), np.float32)
    maps = []
    for c in range(8):
        m = dict(cm)
        m["xA"] = np.ascontiguousarray(inp["x_prompt"][c]) if c < 2 else zx
        m["memA"] = np.ascontiguousarray(inp["mem_prompt"][c]) if c < 2 else zm
        m["xB"] = np.ascontiguousarray(inp["x_sample"][c])
        m["memB"] = np.ascontiguousarray(inp["mem_sample"][c])
        maps.append(m)
    res = run_bass_kernel_spmd(nc, maps, core_ids=list(range(8)))
    y_prompt = np.stack([np.asarray(res.results[c]["yA"], dtype=np.float32) for c in range(2)])
    y_sample = np.stack([np.asarray(res.results[c]["yB"], dtype=np.float32) for c in range(8)])
    return y_prompt, y_sample
```
